# Optimizing a Trainium2 kernel written in Bass

```python
import math
import jax
import jax.numpy as jnp
from jax import lax
import numpy as np

D_MODEL = 2048
BATCH = 32
SEQ = 256
DEPTH = 2
DEC_BATCH = 2
DEC_SEQ = 1024
PAST_LEN = 512

GRID_W = 64
RMS_EPS = 1e-6
N_MOD = 9
D_FF = 5632
GLA_HEADS = 4
GLA_DK = 128
GLA_DV = 256
GLA_RANK = 16
GLA_GATE_NORM = 16.0
GLA_CHUNK = 16
SSM_HEADS = 16
SSM_HEADDIM = 64
SSM_GROUPS = 4
SSM_HPG = SSM_HEADS // SSM_GROUPS
SSM_STATE = 128
SSM_CONV = 5
SSM_CHUNK = 64
SSM_INNER = SSM_HEADS * SSM_HEADDIM
SSM_BC = SSM_GROUPS * SSM_STATE
CONV_CH = SSM_INNER + 2 * SSM_BC
ATTN_HEADS = 8
KV_HEADS = 2
Q_PER_KV = ATTN_HEADS // KV_HEADS
HEAD_DIM = 128
WINDOW = 128
ATTN_BLOCK = 128
ROPE_THETA = 10000.0
GLA_QK = GLA_HEADS * GLA_DK
GLA_VAL = GLA_HEADS * GLA_DV
ATTN_Q = ATTN_HEADS * HEAD_DIM
ATTN_KV = KV_HEADS * HEAD_DIM
N_BRANCH = 3
IN_SPLITS = (GLA_QK, GLA_QK, GLA_VAL, GLA_VAL, 2 * GLA_RANK,
             SSM_INNER, CONV_CH, 2 * SSM_HEADS,
             ATTN_Q, ATTN_KV, ATTN_KV,
             N_BRANCH * D_MODEL)
D_IN = sum(IN_SPLITS)

kernel_name = 'hybrid_diffusion_trunk_step'


def _rmsnorm(x, w):
    xf = x.astype(jnp.float32)
    y = xf * lax.rsqrt(jnp.mean(xf * xf, axis=-1, keepdims=True) + RMS_EPS)
    return (y * w.astype(jnp.float32)).astype(x.dtype)


def _modulate(h, shift, scale):
    return h * (1.0 + scale[:, None]) + shift[:, None]


def _swiglu(h, w_gate, w_up, w_down):
    return (jax.nn.silu(h @ w_gate) * (h @ w_up)) @ w_down


def _flip(t):
    return jnp.flip(t, axis=1)


def _rope_1d(x, pos):
    half = x.shape[-1] // 2
    freqs = ROPE_THETA ** (-jnp.arange(half, dtype=jnp.float32) / half)
    ang = pos.astype(jnp.float32)[:, None] * freqs[None, :]
    cos = jnp.cos(ang)[:, None, :]
    sin = jnp.sin(ang)[:, None, :]
    x1 = x[..., :half].astype(jnp.float32)
    x2 = x[..., half:].astype(jnp.float32)
    return jnp.concatenate([x1 * cos - x2 * sin, x2 * cos + x1 * sin], axis=-1).astype(x.dtype)


def _rope_2d(x):
    t = jnp.arange(x.shape[1])
    half = x.shape[-1] // 2
    return jnp.concatenate([_rope_1d(x[..., :half], t // GRID_W), _rope_1d(x[..., half:], t % GRID_W)], axis=-1)


def _depthwise_conv(x, w, b):
    y = lax.conv_general_dilated(x, w[:, None, :], window_strides=(1,),
                                 padding=[(SSM_CONV // 2, SSM_CONV // 2)],
                                 dimension_numbers=('NWC', 'WIO', 'NWC'),
                                 feature_group_count=x.shape[-1])
    return y + b


def _gla_chunked(q, k, v, log_a, h0):
    B, L, H, K = q.shape
    V = v.shape[-1]
    n = L // GLA_CHUNK
    f32 = jnp.float32
    qc = q.astype(f32).reshape(B, n, GLA_CHUNK, H, K)
    kc = k.astype(f32).reshape(B, n, GLA_CHUNK, H, K)
    vc = v.astype(f32).reshape(B, n, GLA_CHUNK, H, V)
    cum = jnp.cumsum(log_a.astype(f32).reshape(B, n, GLA_CHUNK, H, K), axis=2)
    causal = jnp.tril(jnp.ones((GLA_CHUNK, GLA_CHUNK), bool))[:, :, None, None]
    decay = jnp.exp(jnp.where(causal, cum[:, :, :, None] - cum[:, :, None, :], -jnp.inf))
    scores = jnp.einsum('bcthk,bcshk,bctshk->bchts', qc, kc, decay)
    o_intra = jnp.einsum('bchts,bcshv->bcthv', scores, vc)
    cum_end = cum[:, :, -1]
    s_chunk = jnp.einsum('bcshk,bcshv->bchkv', kc * jnp.exp(cum_end[:, :, None] - cum), vc)

    def step(s, inp):
        sc, ae = inp
        return jnp.exp(ae)[..., None] * s + sc, s

    s_final, s_prev = lax.scan(step, h0.astype(f32), (jnp.moveaxis(s_chunk, 1, 0), jnp.moveaxis(cum_end, 1, 0)))
    s_prev = jnp.moveaxis(s_prev, 0, 1)
    o_inter = jnp.einsum('bcthk,bchkv->bcthv', qc * jnp.exp(cum), s_prev)
    return (o_intra + o_inter).reshape(B, L, H, V).astype(v.dtype), s_final


def _ssd_chunked(x, dt, A, bm, cm, h0):
    B, L, G, Hg, P = x.shape
    n = L // SSM_CHUNK
    f32 = jnp.float32
    xc = x.astype(f32).reshape(B, n, SSM_CHUNK, G, Hg, P)
    dtc = dt.astype(f32).reshape(B, n, SSM_CHUNK, G, Hg)
    bc = bm.astype(f32).reshape(B, n, SSM_CHUNK, G, SSM_STATE)
    cc = cm.astype(f32).reshape(B, n, SSM_CHUNK, G, SSM_STATE)
    cum = jnp.cumsum(dtc * A.astype(f32), axis=2)
    cum_t = jnp.moveaxis(cum, 2, -1)
    causal = jnp.tril(jnp.ones((SSM_CHUNK, SSM_CHUNK), bool))
    seg = jnp.exp(jnp.where(causal, cum_t[..., :, None] - cum_t[..., None, :], -jnp.inf))
    cb = jnp.einsum('bctgn,bcsgn->bcgts', cc, bc)
    w = cb[:, :, :, None] * seg * jnp.moveaxis(dtc, 2, -1)[..., None, :]
    y_intra = jnp.einsum('bcghts,bcsghp->bctghp', w, xc)
    cum_end = cum[:, :, -1]
    s_chunk = jnp.einsum('bcsgn,bcsgh,bcsghp->bcghpn', bc, dtc * jnp.exp(cum_end[:, :, None] - cum), xc)

    def step(s, inp):
        sc, ae = inp
        return jnp.exp(ae)[..., None, None] * s + sc, s

    s_final, s_prev = lax.scan(step, h0.astype(f32), (jnp.moveaxis(s_chunk, 1, 0), jnp.moveaxis(cum_end, 1, 0)))
    s_prev = jnp.moveaxis(s_prev, 0, 1)
    y_inter = jnp.einsum('bctgn,bctgh,bcghpn->bctghp', cc, jnp.exp(cum), s_prev)
    return (y_intra + y_inter).reshape(B, L, G, Hg, P).astype(x.dtype), s_final


def _sink_attention(q, k, v, mask, sink):
    s = jnp.einsum('bqkgd,bskd->bkgqs', q, k).astype(jnp.float32) * (HEAD_DIM ** -0.5)
    if mask is not None:
        s = jnp.where(mask, s, -jnp.inf)
    sink_col = jnp.broadcast_to(sink.astype(jnp.float32)[None, :, :, None, None], s.shape[:-1] + (1,))
    p = jax.nn.softmax(jnp.concatenate([sink_col, s], axis=-1), axis=-1)[..., 1:]
    return jnp.einsum('bkgqs,bskd->bqkgd', p.astype(v.dtype), v)


def _context_attention(q, k, v, sink):
    B, L = q.shape[:2]
    nb = L // ATTN_BLOCK
    q_blocks = jnp.moveaxis(q.reshape(B, nb, ATTN_BLOCK, KV_HEADS, Q_PER_KV, HEAD_DIM), 1, 0)
    out = lax.map(lambda qi: _sink_attention(qi, k, v, None, sink), q_blocks)
    return jnp.moveaxis(out, 0, 1).reshape(B, L, ATTN_Q)


def _latent_attention(q, k, v, k_ctx, v_ctx, sink):
    B, L = q.shape[:2]
    nb = L // ATTN_BLOCK
    span = ATTN_BLOCK + 2 * WINDOW
    pad = ((0, 0), (WINDOW, WINDOW), (0, 0), (0, 0))
    kp = jnp.pad(k, pad)
    vp = jnp.pad(v, pad)
    q_blocks = jnp.moveaxis(q.reshape(B, nb, ATTN_BLOCK, KV_HEADS, Q_PER_KV, HEAD_DIM), 1, 0)
    ctx_mask = jnp.ones((ATTN_BLOCK, k_ctx.shape[1]), bool)

    def block(args):
        i, qi = args
        start = i * ATTN_BLOCK
        kw = lax.dynamic_slice_in_dim(kp, start, span, axis=1)
        vw = lax.dynamic_slice_in_dim(vp, start, span, axis=1)
        qpos = start + jnp.arange(ATTN_BLOCK)
        kpos = start - WINDOW + jnp.arange(span)
        win = (jnp.abs(qpos[:, None] - kpos[None, :]) <= WINDOW) & (kpos[None, :] >= 0) & (kpos[None, :] < L)
        mask = jnp.concatenate([ctx_mask, win], axis=1)
        return _sink_attention(qi, jnp.concatenate([k_ctx, kw], axis=1),
                               jnp.concatenate([v_ctx, vw], axis=1), mask, sink)

    out = lax.map(block, (jnp.arange(nb), q_blocks))
    return jnp.moveaxis(out, 0, 1).reshape(B, L, ATTN_Q)


def _token_mix(h, lp, ctx):
    B, L, _ = h.shape
    f32 = jnp.float32
    split_at = [int(s) for s in np.cumsum(IN_SPLITS)[:-1]]
    (g_q, g_k, g_v, g_r, g_down, s_z, s_xbc, s_dt, a_q, a_k, a_v, br) = jnp.split(h @ lp['w_in'], split_at, axis=-1)
    if ctx is None:
        gla_h0 = jnp.zeros((2, B, GLA_HEADS, GLA_DK, GLA_DV), f32)
        ssm_h0 = jnp.zeros((2, B, SSM_GROUPS, SSM_HPG, SSM_HEADDIM, SSM_STATE), f32)
    else:
        gla_h0 = jnp.moveaxis(ctx['gla'], 1, 0)
        ssm_h0 = jnp.moveaxis(ctx['ssm'], 1, 0).reshape(2, B, SSM_GROUPS, SSM_HPG, SSM_HEADDIM, SSM_STATE)

    q = g_q.reshape(B, L, GLA_HEADS, GLA_DK) * (GLA_DK ** -0.5)
    k = g_k.reshape(B, L, GLA_HEADS, GLA_DK)
    v = g_v.reshape(B, L, GLA_HEADS, GLA_DV)
    gz = jnp.einsum('bldr,drk->bldk', g_down.reshape(B, L, 2, GLA_RANK), lp['gla_w_up']) + lp['gla_b_up']
    log_a = (jax.nn.log_sigmoid(gz.astype(f32)) / GLA_GATE_NORM).reshape(B, L, 2, GLA_HEADS, GLA_DK)
    o_f, sg_f = _gla_chunked(q, k, v, log_a[:, :, 0], gla_h0[0])
    o_b, sg_b = _gla_chunked(_flip(q), _flip(k), _flip(v), _flip(log_a[:, :, 1]), gla_h0[1])
    o_gla = _rmsnorm(o_f + _flip(o_b), lp['gla_norm']).reshape(B, L, GLA_VAL) * jax.nn.silu(g_r)

    xbc = jax.nn.silu(_depthwise_conv(s_xbc, lp['ssm_conv_w'], lp['ssm_conv_b']))
    xs, bm, cm = jnp.split(xbc, [SSM_INNER, SSM_INNER + SSM_BC], axis=-1)
    xs = xs.reshape(B, L, SSM_GROUPS, SSM_HPG, SSM_HEADDIM)
    bm = bm.reshape(B, L, SSM_GROUPS, SSM_STATE)
    cm = cm.reshape(B, L, SSM_GROUPS, SSM_STATE)
    dt = jax.nn.softplus(s_dt.astype(f32).reshape(B, L, 2, SSM_HEADS) + lp['ssm_dt_bias'].astype(f32))
    dt = dt.reshape(B, L, 2, SSM_GROUPS, SSM_HPG)
    A = -jnp.exp(lp['ssm_a_log'].astype(f32)).reshape(2, SSM_GROUPS, SSM_HPG)
    y_f, ss_f = _ssd_chunked(xs, dt[:, :, 0], A[0], bm, cm, ssm_h0[0])
    y_b, ss_b = _ssd_chunked(_flip(xs), _flip(dt[:, :, 1]), A[1], _flip(bm), _flip(cm), ssm_h0[1])
    y = y_f + _flip(y_b) + lp['ssm_d'].reshape(SSM_GROUPS, SSM_HPG)[:, :, None] * xs
    o_ssm = _rmsnorm(y.reshape(B, L, SSM_INNER) * jax.nn.silu(s_z), lp['ssm_norm'])

    qa = a_q.reshape(B, L, ATTN_HEADS, HEAD_DIM)
    ka = a_k.reshape(B, L, KV_HEADS, HEAD_DIM)
    va = a_v.reshape(B, L, KV_HEADS, HEAD_DIM)
    sink = lp['attn_sink'].reshape(KV_HEADS, Q_PER_KV)
    if ctx is None:
        o_att = _context_attention(qa.reshape(B, L, KV_HEADS, Q_PER_KV, HEAD_DIM), ka, va, sink)
    else:
        qa = _rope_2d(qa)
        ka = _rope_2d(ka)
        o_att = _latent_attention(qa.reshape(B, L, KV_HEADS, Q_PER_KV, HEAD_DIM), ka, va, ctx['k'], ctx['v'], sink)

    gates = jax.nn.sigmoid(br).reshape(B, L, N_BRANCH, D_MODEL)
    m = (gates[:, :, 0] * (o_gla @ lp['w_br_gla'])
         + gates[:, :, 1] * (o_ssm @ lp['w_br_ssm'])
         + gates[:, :, 2] * (o_att @ lp['w_br_attn']))
    out = m @ lp['w_out']
    gla_state = jnp.stack([sg_f, sg_b], axis=1)
    ssm_state = jnp.stack([ss_f, ss_b], axis=1).reshape(B, 2, SSM_HEADS, SSM_HEADDIM, SSM_STATE)
    return out, (ka, va, gla_state, ssm_state)


def _trunk_layer(x, mod, lp, ctx):
    sh1, sc1, g1, sh2, sc2, g2, sh3, sc3, g3 = jnp.split(mod, N_MOD, axis=-1)
    h = _modulate(_rmsnorm(x, lp['ffn1_norm']), sh1, sc1)
    x = x + 0.5 * g1[:, None] * _swiglu(h, lp['ffn1_w_gate'], lp['ffn1_w_up'], lp['ffn1_w_down'])
    h = _modulate(_rmsnorm(x, lp['mix_norm']), sh2, sc2)
    mix, ctx_tensors = _token_mix(h, lp, ctx)
    x = x + g2[:, None] * mix
    h = _modulate(_rmsnorm(x, lp['ffn2_norm']), sh3, sc3)
    x = x + 0.5 * g3[:, None] * _swiglu(h, lp['ffn2_w_gate'], lp['ffn2_w_up'], lp['ffn2_w_down'])
    return x, ctx_tensors


def setup_inputs(seed: int = 0) -> dict:
    key = jax.random.key(seed)
    ks = iter(jax.random.split(key, 48))
    D = D_MODEL

    def nrm(shape, scale=1.0):
        return jax.random.normal(next(ks), shape, jnp.float32) * scale

    def gain(shape):
        return 1.0 + nrm(shape, 0.05)

    dt0 = jnp.exp(jax.random.uniform(next(ks), (DEPTH, 2, SSM_HEADS), jnp.float32, math.log(1e-3), math.log(1e-1)))
    a0 = jax.random.uniform(next(ks), (DEPTH, 2, SSM_HEADS), jnp.float32, 1.0, 16.0)
    return {
        'x_prompt': nrm((BATCH, SEQ, D)),
        'x_sample': nrm((DEC_BATCH, DEC_SEQ, D)),
        'c': nrm((DEC_BATCH, D)),
        'cache_k': nrm((DEC_BATCH, DEPTH, PAST_LEN, KV_HEADS, HEAD_DIM)),
        'cache_v': nrm((DEC_BATCH, DEPTH, PAST_LEN, KV_HEADS, HEAD_DIM)),
        'state_gla': nrm((DEC_BATCH, DEPTH, 2, GLA_HEADS, GLA_DK, GLA_DV)),
        'state_ssm': nrm((DEC_BATCH, DEPTH, 2, SSM_HEADS, SSM_HEADDIM, SSM_STATE)),
        'c_ctx': nrm((D,)),
        'w_mod': nrm((DEPTH, D, N_MOD * D), D ** -0.5),
        'b_mod': nrm((DEPTH, N_MOD * D), 0.01),
        'ffn1_norm': gain((DEPTH, D)),
        'ffn1_w_gate': nrm((DEPTH, D, D_FF), D ** -0.5),
        'ffn1_w_up': nrm((DEPTH, D, D_FF), D ** -0.5),
        'ffn1_w_down': nrm((DEPTH, D_FF, D), D_FF ** -0.5),
        'mix_norm': gain((DEPTH, D)),
        'w_in': nrm((DEPTH, D, D_IN), D ** -0.5),
        'gla_w_up': nrm((DEPTH, 2, GLA_RANK, GLA_QK), GLA_RANK ** -0.5),
        'gla_b_up': nrm((DEPTH, 2, GLA_QK), 0.1),
        'gla_norm': gain((DEPTH, GLA_DV)),
        'ssm_conv_w': nrm((DEPTH, SSM_CONV, CONV_CH), SSM_CONV ** -0.5),
        'ssm_conv_b': nrm((DEPTH, CONV_CH), 0.01),
        'ssm_dt_bias': dt0 + jnp.log(-jnp.expm1(-dt0)),
        'ssm_a_log': jnp.log(a0),
        'ssm_d': 1.0 + nrm((DEPTH, SSM_HEADS), 0.1),
        'ssm_norm': gain((DEPTH, SSM_INNER)),
        'attn_sink': nrm((DEPTH, ATTN_HEADS), 0.5),
        'w_br_gla': nrm((DEPTH, GLA_VAL, D), GLA_VAL ** -0.5),
        'w_br_ssm': nrm((DEPTH, SSM_INNER, D), SSM_INNER ** -0.5),
        'w_br_attn': nrm((DEPTH, ATTN_Q, D), ATTN_Q ** -0.5),
        'w_out': nrm((DEPTH, D, D), D ** -0.5),
        'ffn2_norm': gain((DEPTH, D)),
        'ffn2_w_gate': nrm((DEPTH, D, D_FF), D ** -0.5),
        'ffn2_w_up': nrm((DEPTH, D, D_FF), D ** -0.5),
        'ffn2_w_down': nrm((DEPTH, D_FF, D), D_FF ** -0.5),
        'final_norm': gain((D,)),
    }


def reference(x_prompt, x_sample, c, cache_k, cache_v, state_gla, state_ssm, c_ctx,
              w_mod, b_mod, ffn1_norm, ffn1_w_gate, ffn1_w_up, ffn1_w_down,
              mix_norm, w_in, gla_w_up, gla_b_up, gla_norm,
              ssm_conv_w, ssm_conv_b, ssm_dt_bias, ssm_a_log, ssm_d, ssm_norm,
              attn_sink, w_br_gla, w_br_ssm, w_br_attn, w_out,
              ffn2_norm, ffn2_w_gate, ffn2_w_up, ffn2_w_down, final_norm):
    silu_ctx = jax.nn.silu(c_ctx)[None]
    silu_lat = jax.nn.silu(c)
    xp, xs = x_prompt, x_sample
    new_k, new_v, new_gla, new_ssm = [], [], [], []
    for l in range(DEPTH):
        lp = {
            'ffn1_norm': ffn1_norm[l], 'ffn1_w_gate': ffn1_w_gate[l], 'ffn1_w_up': ffn1_w_up[l],
            'ffn1_w_down': ffn1_w_down[l], 'mix_norm': mix_norm[l], 'w_in': w_in[l],
            'gla_w_up': gla_w_up[l], 'gla_b_up': gla_b_up[l], 'gla_norm': gla_norm[l],
            'ssm_conv_w': ssm_conv_w[l], 'ssm_conv_b': ssm_conv_b[l], 'ssm_dt_bias': ssm_dt_bias[l],
            'ssm_a_log': ssm_a_log[l], 'ssm_d': ssm_d[l], 'ssm_norm': ssm_norm[l],
            'attn_sink': attn_sink[l], 'w_br_gla': w_br_gla[l], 'w_br_ssm': w_br_ssm[l],
            'w_br_attn': w_br_attn[l], 'w_out': w_out[l], 'ffn2_norm': ffn2_norm[l],
            'ffn2_w_gate': ffn2_w_gate[l], 'ffn2_w_up': ffn2_w_up[l], 'ffn2_w_down': ffn2_w_down[l],
        }
        mod_ctx = silu_ctx @ w_mod[l] + b_mod[l]
        mod_lat = silu_lat @ w_mod[l] + b_mod[l]
        xp, (k_c, v_c, g_st, s_st) = _trunk_layer(xp, mod_ctx, lp, None)
        new_k.append(k_c)
        new_v.append(v_c)
        new_gla.append(g_st)
        new_ssm.append(s_st)
        ctx = {'k': cache_k[:, l], 'v': cache_v[:, l], 'gla': state_gla[:, l], 'ssm': state_ssm[:, l]}
        xs, _ = _trunk_layer(xs, mod_lat, lp, ctx)
    y_prompt = _rmsnorm(xp, final_norm)
    y_sample = _rmsnorm(xs, final_norm)
    return (y_prompt, y_sample, jnp.stack(new_k, axis=1), jnp.stack(new_v, axis=1),
            jnp.stack(new_gla, axis=1), jnp.stack(new_ssm, axis=1))
```

```python
import math
import contextlib
import numpy as np
import concourse.bass as bass
import concourse.mybir as mybir
from concourse.bass_utils import run_bass_kernel_spmd

F32 = mybir.dt.float32
BF16 = mybir.dt.bfloat16
AF = mybir.ActivationFunctionType
ALU = mybir.AluOpType

EPOCH = 8192
NCORES = 8
T = 1280
NT = 10
D = 2048
DFF = 5632
TB = [(0, 512), (512, 512), (1024, 256)]
NEG = -30000.0


class Sched:
    ENGS = ('pe', 'act', 'dve', 'pool', 'sp')

    def __init__(self, nc):
        self.nc = nc
        self.ops = {e: [] for e in self.ENGS}
        self.count = {e: 0 for e in self.ENGS}
        self.last_w = {}
        self.readers = {}
        self.waited = {e: {} for e in self.ENGS}
        self.dma_count = {}
        self.semnames = set()
        self.last_tok = {}

    def _need(self, eng, is_dma_consumer, tok, raw):
        semkey, val, peng, pdma = tok
        if not pdma and not is_dma_consumer and peng == eng:
            if eng == 'pe':
                return False
        return True

    def _add_wait(self, eng, waits, tok):
        semkey, val = tok[0], tok[1]
        w = self.waited[eng]
        if w.get(semkey, 0) >= val:
            return
        w[semkey] = val
        if isinstance(semkey, tuple):
            pe_, ep = semkey
            for e2 in range(ep):
                w[(pe_, e2)] = EPOCH
        for i, (k, v) in enumerate(waits):
            if k == semkey:
                waits[i] = (k, max(v, val))
                return
        waits.append((semkey, val))

    def op(self, eng, fn, reads=(), writes=(), dma=None):
        is_dma = dma is not None
        waits = []
        for k in reads:
            t = self.last_w.get(k)
            if t is not None and self._need(eng, is_dma, t, True):
                self._add_wait(eng, waits, t)
        for k in writes:
            t = self.last_w.get(k)
            if t is not None and self._need(eng, is_dma, t, False):
                self._add_wait(eng, waits, t)
            for t in self.readers.get(k, {}).values():
                if self._need(eng, is_dma, t, False):
                    self._add_wait(eng, waits, t)
        if is_dma:
            n = self.dma_count.get(dma, 0) + 1
            self.dma_count[dma] = n
            tok = (dma, 16 * n, eng, True)
            inc = (dma, 16)
            rkey = dma
        else:
            idx = self.count[eng]
            self.count[eng] = idx + 1
            semkey = (eng, idx // EPOCH)
            tok = (semkey, idx % EPOCH + 1, eng, False)
            inc = (semkey, 1)
            rkey = eng
        self.semnames.add(inc[0])
        self.last_tok[inc[0] if is_dma else eng] = tok
        for k in writes:
            self.last_w[k] = tok
            self.readers[k] = {}
        for k in reads:
            self.readers.setdefault(k, {})[rkey] = tok
        self.ops[eng].append((fn, waits, inc))
        return tok

    def barrier(self):
        toks = list(self.last_tok.values())
        for eng in self.ENGS:
            waits = []
            for t in toks:
                if (not t[3]) and t[2] == eng:
                    continue
                self._add_wait(eng, waits, t)
            if waits:
                self.ops[eng].append((None, waits, None))
        self.last_w = {}
        self.readers = {}

    def final_waits(self, eng, toks):
        waits = []
        for t in toks:
            self._add_wait(eng, waits, t)
        self.ops[eng].append((None, waits, None))

    def emit(self):
        nc = self.nc
        with contextlib.ExitStack() as st:
            sems = {}
            for i, k in enumerate(sorted(self.semnames, key=str)):
                sems[k] = st.enter_context(nc.semaphore("sm%d" % i))
            block = st.enter_context(nc.Block())

            def run(engname):
                def body(e):
                    for fn, waits, inc in self.ops[engname]:
                        for (k, v) in waits:
                            e.wait_ge(sems[k], v)
                        if fn is not None:
                            fn(e).then_inc(sems[inc[0]], inc[1])
                return body
            block.tensor(run('pe'))
            block.scalar(run('act'))
            block.vector(run('dve'))
            block.gpsimd(run('pool'))
            block.sync(run('sp'))


def _layout(items):
    off = {}
    o = 0
    for name, n in items:
        off[name] = (o, n)
        o += n
    return off, o


def cf_layout(L):
    return _layout([
        ('ident', 128), ('triU', 128), ('triL', 128), ('ntriU', 128), ('ntriL', 128), ('mnegF', 128), ('mnegB', 128),
        ('cummask', 1280), ('fs', 1), ('ctxbias', 1), ('cvec', 32),
        ('normw', L * 48), ('fnorm', 16), ('bmod', L * 144), ('nbup', L * 8), ('gnorm', L * 2),
        ('convw', L * 80), ('convb', L * 16), ('ssmD', L * 8), ('ssmnorm', L * 8),
        ('dtbias', L * 32), ('alog', L * 32), ('sink', L * 8),
    ])


def cb_layout(L):
    return _layout([
        ('ones', 128), ('identb', 128), ('pm', 128), ('maskFB', 256), ('amask', 18 * 128),
        ('wup', L * 1024),
    ])


def ws_chunks():
    ch = [('gdown', 3072, 32)]
    for h in range(4):
        ch += [('gq%d' % h, h * 128, 128), ('gk%d' % h, 512 + h * 128, 128),
               ('gr%d_0' % h, 2048 + h * 256, 128), ('gr%d_1' % h, 2048 + h * 256 + 128, 128)]
    for g in range(4):
        ch += [('sx%d_0' % g, 4128 + g * 256, 128), ('sx%d_1' % g, 4128 + g * 256 + 128, 128),
               ('sB%d' % g, 5152 + g * 128, 128), ('sC%d' % g, 5664 + g * 128, 128),
               ('sz%d_0' % g, 3104 + g * 256, 128), ('sz%d_1' % g, 3104 + g * 256 + 128, 128)]
    for kv in range(2):
        for j in range(4):
            ch.append(('aq%d' % (kv * 4 + j), 6208 + (kv * 4 + j) * 128, 128))
        ch.append(('ak%d' % kv, 7232 + kv * 128, 128))
    for b in range(3):
        for c in range(16):
            ch.append(('br%d_%d' % (b, c), 7744 + b * 2048 + c * 128, 128))
    return ch


WS = ws_chunks()
WSI = {n: i for i, (n, _, _) in enumerate(WS)}

ARENA_B = 210944
O_CF = 0
O_CB = 12288
O_RSTD = 22528
O_MODV = 27648
O_SCR = 30208
O_X = 36352
O_H = 118272
O_W = 159232


def build_program(L):
    nc = bass.Bass("TRN2", target_bir_lowering=False)
    cfo, NCF = cf_layout(L)
    cbo, NCB = cb_layout(L)
    assert NCF * 4 <= O_CB and NCB * 2 <= O_RSTD - O_CB, (NCF, NCB)

    def din(name, shape):
        return nc.dram_tensor(name, shape, F32, kind="ExternalInput").ap()

    def dout(name, shape):
        return nc.dram_tensor(name, shape, F32, kind="ExternalOutput").ap()

    xin = din("xin", [128, 16 * T])
    cf_d = din("cf", [128, NCF])
    cb_d = din("cb", [128, NCB])
    rope_d = din("rope", [128, 2 * T])
    h0g_d = din("h0g", [L, 2, 4, 128, 256])
    h0s_d = din("h0s", [L, 2, 128, 1024])
    ctxk_d = din("ctxk", [L, 2, 128, 512])
    ctxv_d = din("ctxv", [L, 2, 128, 512])
    wmod_d = din("wmod", [L, 144, 128, 2048])
    wgu_d = din("wgu", [L, 2, 88, 128, 2048])
    wd_d = din("wd", [L, 2, 22, 128, 4096])
    wws_d = din("wws", [L, len(WS), 128, 2048])
    wv_d = din("wv", [L, 5, 128, 4096])
    wdt_d = din("wdt", [L, 128, 512])
    wbr_d = din("wbr", [L, 48, 128, 1024])
    wout_d = din("wout", [L, 16, 128, 2048])
    y_d = dout("yT", [128, 16 * T])
    ok_d = dout("ok", [L, 2, 128, T])
    ov_d = dout("ov", [L, NT, 128, 256])
    og_d = dout("og", [L, 5, 2, 4, 128, 256])
    os_d = dout("os", [L, 5, 2, 128, 1024])
    xsp_d = dout("xspill", [128, 16 * T])

    st = contextlib.ExitStack()
    arena = st.enter_context(nc.sbuf_tensor("arena", [128, ARENA_B // 4], F32))
    ps = [st.enter_context(nc.psum_tensor("ps%d" % i, [128, 512], F32)) for i in range(8)]
    S = Sched(nc)
    out_toks = []

    def f32v(off, n):
        assert off % 4 == 0
        return arena[:, off // 4: off // 4 + n]

    def bfv(off, n):
        assert off % 4 == 0 and n % 2 == 0
        return arena[:, off // 4: off // 4 + n // 2].bitcast(BF16)

    def r3(ap, a):
        return ap.rearrange("p (a b) -> p a b", a=a)

    def act(out, in_, func, r, w, **kw):
        S.op('act', lambda e: e.activation(out=out, in_=in_, func=func, **kw), reads=r, writes=w)

    def tt(out, a, b, op, r, w, eng='dve'):
        S.op(eng, lambda e: e.tensor_tensor(out=out, in0=a, in1=b, op=op), reads=r, writes=w)

    def ts(out, a, s1, s2, op0, op1, r, w):
        if s2 is None:
            S.op('dve', lambda e: e.tensor_scalar(out=out, in0=a, scalar1=s1, scalar2=None, op0=op0), reads=r, writes=w)
        else:
            S.op('dve', lambda e: e.tensor_scalar(out=out, in0=a, scalar1=s1, scalar2=s2, op0=op0, op1=op1), reads=r, writes=w)

    def stt(out, a, sc, b, op0, op1, r, w):
        S.op('dve', lambda e: e.scalar_tensor_tensor(out=out, in0=a, scalar=sc, in1=b, op0=op0, op1=op1), reads=r, writes=w)

    def cp(out, in_, r, w, eng='dve'):
        if eng == 'act':
            S.op(eng, lambda e: e.activation(out=out, in_=in_, func=AF.Copy), reads=r, writes=w)
        else:
            S.op(eng, lambda e: e.tensor_copy(out=out, in_=in_), reads=r, writes=w)

    def mm(out, lhsT, rhs, start, stop, r, w):
        S.op('pe', lambda e: e.matmul(out, lhsT=lhsT, rhs=rhs, start=start, stop=stop), reads=r, writes=w)

    def dma(q, out, in_, r, w, sem):
        return S.op(q, lambda e: e.dma_start(out=out, in_=in_), reads=r, writes=w, dma=sem)

    def memset(ap, val, w, eng='pool'):
        S.op(eng, lambda e: e.memset(ap, val), writes=w)

    cf = f32v(O_CF, NCF)
    cb = bfv(O_CB, NCB)

    def CF(name, i=0, n=None):
        o, m = cfo[name]
        n = m if n is None else n
        return cf[:, o + i: o + i + n]

    def CB(name, i=0, n=None):
        o, m = cbo[name]
        n = m if n is None else n
        return cb[:, o + i: o + i + n]

    rstd = f32v(O_RSTD, T)
    modv = r3(f32v(O_MODV, 288), 144)
    Avec = f32v(O_MODV + 1152, 96)
    Gvec = f32v(O_MODV + 1152 + 384, 96)
    siluc = f32v(O_MODV + 1152 + 768, 32)
    sqb = [bfv(O_SCR + i * 1024, 512) for i in range(2)]
    tmpf = [f32v(O_SCR + 2048 + i * 2048, 512) for i in range(2)]
    xT = r3(f32v(O_X, 16 * T), 16)
    hT = r3(bfv(O_H, 16 * T), 16)
    ones_b = CB('ones')
    ident_b = CB('identb')
    ident_f = CF('ident')
    fs = CF('fs')
    ctxbias = CF('ctxbias')

    dma('sp', cf, cf_d, [], ['cf'], 'ld_cf')
    dma('pool', cb, cb_d, [], ['cb'], 'ld_cb')
    def xkeys(q):
        return [('x', kc, tb_) for kc in range(q * 4, q * 4 + 4) for tb_ in range(3)]
    xflat = f32v(O_X, 16 * T)
    for q in range(4):
        dma('sp', xflat[:, q * 4 * T:(q + 1) * 4 * T], xin[:, q * 4 * T:(q + 1) * 4 * T], [], xkeys(q), 'ld_x%d' % q)
    act(siluc, CF('cvec'), AF.Silu, ['cf'], ['siluc'])
    ts(CF('nbup'), CF('nbup'), -1.0, None, ALU.mult, None, ['cf'], ['cf'])

    psrr = [0]
    held = set()

    def nextps(hold=False):
        while True:
            b = psrr[0] % 8
            psrr[0] += 1
            if b not in held:
                break
        if hold:
            held.add(b)
        return b

    def release(*bs):
        for b in bs:
            held.discard(b)

    def mod_phase(l):
        NS = 6
        slots = [bfv(O_W + i * 4096, 2048) for i in range(NS)]
        scb = bfv(O_W + NS * 4096, 32)
        cp(scb, siluc, ['siluc'], ['scb'])
        scb3 = r3(scb, 16)
        b = nextps(hold=True)
        for m in range(144):
            s = m % NS
            dma('pool', slots[s], wmod_d[l, m], [], [('wmm', s)], 'ld_mod%d' % s)
            w3 = r3(slots[s], 16)
            for kc in range(16):
                mm(ps[b][:, 2 * m:2 * m + 2], w3[:, kc, :], scb3[:, kc, :], kc == 0, kc == 15,
                   [('wmm', s), 'scb'], ['ps%d' % b])
        bm = CF('bmod', l * 144, 144)
        tt(modv, r3(ps[b][:, 0:288], 144), bm.unsqueeze(2).to_broadcast([128, 144, 2]), ALU.add,
           ['ps%d' % b, 'cf'], ['modv'])
        release(b)
        for i in range(3):
            nw = CF('normw', l * 48 + i * 16, 16)
            stt(r3(Avec[:, i * 32:(i + 1) * 32], 16), modv[:, (3 * i + 1) * 16:(3 * i + 2) * 16, :], 1.0,
                nw.unsqueeze(2).to_broadcast([128, 16, 2]), ALU.add, ALU.mult, ['modv', 'cf'], ['Avec'])
            ts(r3(Gvec[:, i * 32:(i + 1) * 32], 16), modv[:, (3 * i + 2) * 16:(3 * i + 3) * 16, :],
               1.0 if i == 1 else 0.5, None, ALU.mult, None, ['modv'], ['Gvec'])

    def Acol(i, kc, grp):
        return Avec[:, i * 32 + kc * 2 + grp: i * 32 + kc * 2 + grp + 1]

    def Bcol(i, kc, grp):
        return modv[:, 3 * i * 16 + kc, grp:grp + 1]

    def Gcol(i, kc, grp):
        return Gvec[:, i * 32 + kc * 2 + grp: i * 32 + kc * 2 + grp + 1]

    def rms_rstd(src_fn, nchunks, t0, tn, rkeys_fn, out_ap, wkey, dim):
        b = nextps()
        for kc in range(nchunks):
            sq = sqb[kc % 2]
            act(sq[:, :tn], src_fn(kc), AF.Square, rkeys_fn(kc), [('sq', kc % 2)])
            mm(ps[b][:, :tn], ones_b, sq[:, :tn], kc == 0, kc == nchunks - 1, [('sq', kc % 2), 'cb'], ['ps%d' % b])
        act(out_ap, ps[b][:, :tn], AF.Sqrt, ['ps%d' % b], [wkey], scale=1.0 / dim, bias=1e-6)
        S.op('dve', lambda e: e.reciprocal(out=out_ap, in_=out_ap), reads=[wkey], writes=[wkey])

    def norm_to_h(i):
        for tbi, (t0, tn) in enumerate(TB):
            grp = 0 if tbi < 2 else 1
            rms_rstd(lambda kc: xT[:, kc, t0:t0 + tn], 16, t0, tn, lambda kc: [('x', kc, tbi)],
                     rstd[:, t0:t0 + tn], ('rstd', tbi), D)
            for kc in range(16):
                tf = tmpf[kc % 2]
                stt(tf[:, :tn], xT[:, kc, t0:t0 + tn], Acol(i, kc, grp), rstd[:, t0:t0 + tn], ALU.mult, ALU.mult,
                    [('x', kc, tbi), ('rstd', tbi), 'Avec'], [('tmpf', kc % 2)])
                act(hT[:, kc, t0:t0 + tn], tf[:, :tn], AF.Identity, [('tmpf', kc % 2), 'modv'], [('h', kc, tbi)],
                    bias=Bcol(i, kc, grp), scale=1.0)

    def proj_ws(wslots, wkey, dram_tiles, KCn, rhs_fn, rhs_keys_fn, epi, M=128, tbs=None):
        cnt = proj_ws.cnt
        for i, dt_ in enumerate(dram_tiles):
            s = cnt[wkey] % len(wslots)
            cnt[wkey] += 1
            wt = wslots[s]
            dma('pool', wt[:, :KCn * 128], dt_, [], [(wkey, s)], 'ld_%s%d' % (wkey, s))
            w3 = r3(wt[:, :KCn * 128], KCn)
            for tbi, (t0, tn) in enumerate(TB if tbs is None else tbs):
                b = nextps()
                for kc in range(KCn):
                    mm(ps[b][:M, :tn], w3[:, kc, 0:M], rhs_fn(kc, t0, tn), kc == 0, kc == KCn - 1,
                       [(wkey, s)] + rhs_keys_fn(kc, tbi), ['ps%d' % b])
                epi(i, tbi, t0, tn, ps[b], 'ps%d' % b)
    import collections
    proj_ws.cnt = collections.defaultdict(int)

    def h_rhs(kc, t0, tn):
        return hT[:, kc, t0:t0 + tn]

    def h_keys(kc, tbi):
        return [('h', kc, tbi)]

    def ffn_phase(l, which):
        i_sub = 0 if which == 0 else 2
        norm_to_h(i_sub)
        gT = r3(bfv(O_W, 4 * T), 4)
        sg = [bfv(O_W + 10240 + i * 2560, T) for i in range(2)]
        wsl = [bfv(O_W + 15360 + i * 4096, 2048) for i in range(4)]
        wdsl = [bfv(O_W + 31744 + i * 8192, 4096) for i in range(2)]
        for g in range(11):
            for j in range(4):
                hc = g * 4 + j

                def epi(i, tbi, t0, tn, p, pk, j=j):
                    if i == 0:
                        act(sg[j % 2][:, t0:t0 + tn], p[:, :tn], AF.Silu, [pk], [('sg', j % 2, tbi)])
                    else:
                        tt(gT[:, j, t0:t0 + tn], p[:, :tn], sg[j % 2][:, t0:t0 + tn], ALU.mult,
                           [pk, ('sg', j % 2, tbi)], [('g', j, tbi)])
                proj_ws(wsl, 'wf', [wgu_d[l, which, 2 * hc], wgu_d[l, which, 2 * hc + 1]], 16, h_rhs, h_keys, epi)
            for half in range(2):
                s = proj_ws.cnt['wd'] % 2
                proj_ws.cnt['wd'] += 1
                dma('pool', wdsl[s], wd_d[l, which, g * 2 + half], [], [('wd', s)], 'ld_wd%d' % s)
                w4 = wdsl[s].rearrange("p (m k j) -> p m k j", m=8, k=4)
                for mi in range(8):
                    mc = half * 8 + mi
                    for tbi, (t0, tn) in enumerate(TB):
                        grp = 0 if tbi < 2 else 1
                        b = nextps()
                        for k in range(4):
                            mm(ps[b][:, :tn], w4[:, mi, k, :], gT[:, k, t0:t0 + tn], k == 0, k == 3,
                               [('wd', s), ('g', k, tbi)], ['ps%d' % b])
                        stt(xT[:, mc, t0:t0 + tn], ps[b][:, :tn], Gcol(i_sub, mc, grp), xT[:, mc, t0:t0 + tn],
                            ALU.mult, ALU.add, ['ps%d' % b, 'Gvec', ('x', mc, tbi)], [('x', mc, tbi)])

    import os
    KMIX = int(os.environ.get('KMIX', '9'))
    KATT = int(os.environ.get('KATT', '99'))

    def mix_phase(l):
        norm_to_h(1)
        for q in range(4):
            dma('sp', xsp_d[:, q * 4 * T:(q + 1) * 4 * T], xflat[:, q * 4 * T:(q + 1) * 4 * T], xkeys(q), ['xsp%d' % q], 'st_x%d' % q)
        S.barrier()
        if KMIX == 0:
            for q in range(4):
                dma('sp', xflat[:, q * 4 * T:(q + 1) * 4 * T], xsp_d[:, q * 4 * T:(q + 1) * 4 * T], ['xsp%d' % q], xkeys(q), 'ld_x%d' % q)
            S.barrier()
            return
        oG = r3(bfv(O_X, 8 * T), 8)
        oS = r3(bfv(O_X + 20480, 8 * T), 8)
        oA = r3(bfv(O_X + 40960, 8 * T), 8)
        wsl = [bfv(O_X + 61440 + i * 4096, 2048) for i in range(3)]
        wasl = [bfv(O_X + 61440 + 12288, 4096)]

        def wtile(name):
            return wws_d[l, WSI[name]]

        W0 = O_W
        gdT = f32v(W0, T)
        gdb = bfv(W0 + 5120, T)
        qT = f32v(W0 + 7680, T)
        kT = f32v(W0 + 12800, T)
        rg = r3(bfv(W0 + 17920, 2 * T), 2)
        vtok = r3(bfv(W0 + 23040, NT * 256), NT)
        la = f32v(W0 + 28160, T)
        ex = f32v(W0 + 33280, T)
        qk = [bfv(W0 + 38400 + i * 2560, T) for i in range(4)]
        sc1 = O_X + 20480
        SIn = [r3(bfv(sc1 + d_ * 5120, NT * 256), NT) for d_ in range(2)]
        S32 = [f32v(sc1 + 10240 + d_ * 1024, 256) for d_ in range(2)]
        Stmps = [f32v(sc1 + 12288, 256), f32v(sc1 + 31744, 256)]
        ktok = [bfv(sc1 + 13312 + d_ * 256, 128) for d_ in range(2)]
        ABt = [bfv(sc1 + 13824 + i * 512, 256) for i in range(2)]
        o32 = r3(f32v(sc1 + 14848, 2 * T), 2)
        rs2 = f32v(sc1 + 25088, T)
        Ecol = f32v(sc1 + 30208, 32)
        cendb = f32v(sc1 + 30336, 16)

        def epi_gd(i, tbi, t0, tn, p, pk):
            cp(gdb[0:32, t0:t0 + tn], p[0:32, :tn], [pk], [('gdb', tbi)])
        proj_ws(wsl, 'wm', [wtile('gdown')], 16, h_rhs, h_keys, epi_gd, M=32)
        lnscale = math.log(128 ** -0.5)
        for h in range(4):
            def epi_q(i, tbi, t0, tn, p, pk):
                act(qT[:, t0:t0 + tn], p[:, :tn], AF.Copy, [pk], [('qT', tbi)])

            def epi_k(i, tbi, t0, tn, p, pk):
                act(kT[:, t0:t0 + tn], p[:, :tn], AF.Copy, [pk], [('kT', tbi)])

            def epi_r(i, tbi, t0, tn, p, pk):
                act(rg[:, i, t0:t0 + tn], p[:, :tn], AF.Silu, [pk], [('rg', i, tbi)])
            proj_ws(wsl, 'wm', [wtile('gq%d' % h)], 16, h_rhs, h_keys, epi_q)
            proj_ws(wsl, 'wm', [wtile('gk%d' % h)], 16, h_rhs, h_keys, epi_k)
            proj_ws(wsl, 'wm', [wtile('gr%d_0' % h), wtile('gr%d_1' % h)], 16, h_rhs, h_keys, epi_r)
            dma('pool', wasl[0], wv_d[l, h], [], [('wa', 0)], 'ld_wa0')
            wv3 = r3(wasl[0], 16)
            for tt_ in range(NT):
                b = nextps()
                for kc in range(16):
                    mm(ps[b][:, 0:256], hT[:, kc, tt_ * 128:(tt_ + 1) * 128], wv3[:, kc, :], kc == 0, kc == 15,
                       [('wa', 0), ('h', kc, tt_ // 4)], ['ps%d' % b])
                cp(vtok[:, tt_, :], ps[b][:, 0:256], ['ps%d' % b], [('vtok', tt_)])
            for d_ in range(2):
                wup = CB('wup', l * 1024 + d_ * 512 + h * 128, 128)
                nb = CF('nbup', l * 8 + d_ * 4 + h, 1)
                for tbi, (t0, tn) in enumerate(TB):
                    b = nextps()
                    mm(ps[b][:, :tn], wup[0:32, :], gdb[0:32, t0:t0 + tn], True, True, ['cb', ('gdb', tbi)], ['ps%d' % b])
                    act(ex[:, t0:t0 + tn], ps[b][:, :tn], AF.Exp, ['ps%d' % b, 'cf'], ['ex'], scale=-1.0, bias=nb)
                    act(ex[:, t0:t0 + tn], ex[:, t0:t0 + tn], AF.Ln, ['ex'], ['ex'], bias=1.0, scale=1.0)
                ts(la, ex, -1.0 / 16.0, None, ALU.mult, None, ['ex'], ['la'])
                S.op('dve', lambda e: e.tensor_tensor_scan(out=ex, data0=CF('cummask'), data1=la, initial=0.0,
                                                           op0=ALU.mult, op1=ALU.add),
                     reads=['la', 'cf'], writes=['ex'])
                la3 = r3(la, NT)
                ex3 = r3(ex, NT)
                if d_ == 1:
                    tt(la3, la3, ex3, ALU.subtract, ['la', 'ex'], ['la'])
                    cp(cendb[:, 0:NT], ex3[:, :, 127], ['ex'], ['cendb'])
                    tt(ex3, la3, cendb[:, 0:NT].unsqueeze(2).to_broadcast([128, NT, 128]), ALU.add, ['la', 'ex', 'cendb'], ['ex'])
                    ckey = 'ex'
                    ecol = ex3[:, :, 0]
                else:
                    ckey = 'ex'
                    ecol = ex3[:, :, 127]
                act(Ecol[:, d_ * 16:d_ * 16 + NT], ecol, AF.Exp, [ckey], [('Ecol', d_)])
                act(la, ex, AF.Exp, [ckey], ['la'], bias=lnscale, scale=1.0)
                tt(qk[2 * d_], qT, la, ALU.mult, ['la', ('qT', 0), ('qT', 1), ('qT', 2)], [('qk', 2 * d_)])
                act(la, ex, AF.Exp, [ckey, ('qk', 2 * d_)], ['la'], scale=-1.0)
                tt(qk[2 * d_ + 1], kT, la, ALU.mult, ['la', ('kT', 0), ('kT', 1), ('kT', 2)], [('qk', 2 * d_ + 1)])
            for step in range(NT):
              for d_ in range(2):
                    ti = step if d_ == 0 else NT - 1 - step
                    kt_ = qk[2 * d_ + 1]
                    Stmp = Stmps[d_]
                    slot = ti // 2
                    first = (ti % 2 == 0) if d_ == 0 else (ti % 2 == 1)
                    if first:
                        chain = (slot in (1, 2, 3)) if d_ == 0 else (slot in (0, 1, 2))
                        has_h0 = (slot == 0) if d_ == 0 else (slot == 3)
                        if has_h0:
                            dma('sp', S32[d_], h0g_d[l, d_, h], [], [('S32', d_)], 'ld_h0%d' % d_)
                        elif chain:
                            ts(S32[d_], S32[d_], fs, None, ALU.mult, None, [('S32', d_), 'cf'], [('S32', d_)])
                        else:
                            memset(S32[d_], 0.0, [('S32', d_)], eng='dve')
                    cp(SIn[d_][:, ti, :], S32[d_], [('S32', d_)], [('SIn', d_, ti)], eng='act')
                    b = nextps()
                    mm(ps[b][:, 0:128], kt_[:, ti * 128:(ti + 1) * 128], ident_b, True, True, [('qk', 2 * d_ + 1), 'cb'], ['ps%d' % b])
                    cp(ktok[d_], ps[b][:, 0:128], ['ps%d' % b], [('ktok', d_)])
                    b2 = nextps()
                    mm(ps[b2][:, 0:256], ktok[d_], vtok[:, ti, :], True, True, [('ktok', d_), ('vtok', ti)], ['ps%d' % b2])
                    tt(Stmp, ps[b2][:, 0:256], S32[d_], ALU.add, ['ps%d' % b2, ('S32', d_)], [('Stmp', d_)])
                    ts(S32[d_], Stmp, Ecol[:, d_ * 16 + ti:d_ * 16 + ti + 1], None, ALU.mult, None,
                       [('Stmp', d_), ('Ecol', d_)], [('S32', d_)])
                    last = (ti % 2 == 1) if d_ == 0 else (ti % 2 == 0)
                    if last:
                        out_toks.append(dma('sp', og_d[l, slot, d_, h], S32[d_], [('S32', d_)], [], 'st_og%d' % d_))
            for tbi, (t0, tn) in enumerate(TB):
                ntl = tn // 128
                bo = [nextps(hold=True), nextps(hold=True)]
                for tq in range(ntl):
                    ti = t0 // 128 + tq
                    sl = slice(ti * 128, (ti + 1) * 128)
                    b = nextps()
                    mm(ps[b][:, 0:128], qk[1][:, sl], qk[0][:, sl], True, True, [('qk', 0), ('qk', 1)], ['ps%d' % b])
                    mm(ps[b][:, 128:256], qk[3][:, sl], qk[2][:, sl], True, True, [('qk', 2), ('qk', 3)], ['ps%d' % b])
                    AB = ABt[ti % 2]
                    tt(AB, ps[b][:, 0:256], CB('maskFB'), ALU.mult, ['ps%d' % b, 'cb'], [('AB', ti % 2)])
                    for vc in range(2):
                        o = ps[bo[vc]][:, tq * 128:(tq + 1) * 128]
                        vs = vtok[:, ti, vc * 128:(vc + 1) * 128]
                        mm(o, vs, AB[:, 0:128], True, False, [('vtok', ti), ('AB', ti % 2)], ['ps%d' % bo[vc]])
                        mm(o, vs, AB[:, 128:256], False, False, [('vtok', ti), ('AB', ti % 2)], ['ps%d' % bo[vc]])
                        mm(o, SIn[0][:, ti, vc * 128:(vc + 1) * 128], qk[0][:, sl], False, False,
                           [('SIn', 0, ti), ('qk', 0)], ['ps%d' % bo[vc]])
                        mm(o, SIn[1][:, ti, vc * 128:(vc + 1) * 128], qk[2][:, sl], False, True,
                           [('SIn', 1, ti), ('qk', 2)], ['ps%d' % bo[vc]])
                for vc in range(2):
                    act(o32[:, vc, t0:t0 + tn], ps[bo[vc]][:, :tn], AF.Copy, ['ps%d' % bo[vc]], [('o32', vc, tbi)])
                release(*bo)
                rms_rstd(lambda vc: o32[:, vc, t0:t0 + tn], 2, t0, tn, lambda vc: [('o32', vc, tbi)],
                         rs2[:, t0:t0 + tn], ('rs2', tbi), 256)
                for vc in range(2):
                    stt(o32[:, vc, t0:t0 + tn], o32[:, vc, t0:t0 + tn], CF('gnorm', l * 2 + vc, 1), rs2[:, t0:t0 + tn],
                        ALU.mult, ALU.mult, [('o32', vc, tbi), ('rs2', tbi), 'cf'], [('o32', vc, tbi)])
                    tt(oG[:, h * 2 + vc, t0:t0 + tn], o32[:, vc, t0:t0 + tn], rg[:, vc, t0:t0 + tn], ALU.mult,
                       [('o32', vc, tbi), ('rg', vc, tbi)], [('oG', h * 2 + vc, tbi)])
        S.barrier()
        if KMIX == 1:
            return

        dtt = r3(f32v(W0, NT * 32), NT)
        att = r3(f32v(W0 + 1280, NT * 32), NT)
        Abc = f32v(W0 + 2560, 32)
        abc = r3(f32v(W0 + 2688, 1024), 8)
        BtA = r3(bfv(W0 + 6784, NT * 128), NT)
        XtA = r3(bfv(O_RSTD, NT * 256), NT)
        xpad = r3(f32v(W0 + 10880, 5 * 260), 5)
        cacc = r3(f32v(W0 + 16080, T), 5)
        xc = r3(f32v(W0 + 21200, 2 * T), 2)
        BT = bfv(W0 + 31440, T)
        CT = bfv(W0 + 34000, T)
        zg = r3(bfv(W0 + 36560, 2 * T), 2)
        ssacc = f32v(W0 + 41680, T)
        cumt = f32v(W0 + 46800, 32)
        wdec = f32v(W0 + 46928, 8)
        cend = f32v(W0 + 46960, 8)
        Eend = f32v(W0 + 46992, 8)
        sc2 = O_X + 40960
        xtok = f32v(sc2, 256)
        Btok = bfv(sc2 + 1024, 128)
        xdt = [bfv(sc2 + 1280 + i * 512, 256) for i in range(4)]
        segb = [bfv(sc2 + 3328 + i * 1024, 512) for i in range(2)]
        CBm = [bfv(sc2 + 5376 + i * 256, 128) for i in range(2)]
        Cdec = [bfv(sc2 + 5888 + i * 1024, 512) for i in range(2)]
        SsIn = [r3(bfv(sc2 + 7936 + d_ * 5120, NT * 256), NT) for d_ in range(2)]
        Ss32 = [f32v(sc2 + 18176 + d_ * 1024, 256) for d_ in range(2)]
        wdtb = wasl[0]
        dma('pool', wdtb[:, 0:512], wdt_d[l], [], [('wa', 0)], 'ld_wa0')
        wdt3 = r3(wdtb[:, 0:512], 16)
        act(Abc, CF('alog', l * 32, 32), AF.Exp, ['cf'], ['Abc'])
        memset(ssacc, 0.0, ['ssacc'], eng='dve')
        for ti in range(NT):
            b = nextps()
            for kc in range(16):
                mm(ps[b][:, 0:32], hT[:, kc, ti * 128:(ti + 1) * 128], wdt3[:, kc, :], kc == 0, kc == 15,
                   [('wa', 0), ('h', kc, ti // 4)], ['ps%d' % b])
            tt(dtt[:, ti, :], ps[b][:, 0:32], CF('dtbias', l * 32, 32), ALU.add, ['ps%d' % b, 'cf'], [('dtt', ti)])
            act(dtt[:, ti, :], dtt[:, ti, :], AF.Exp, [('dtt', ti)], [('dtt', ti)])
            act(dtt[:, ti, :], dtt[:, ti, :], AF.Ln, [('dtt', ti)], [('dtt', ti)], bias=1.0, scale=1.0)
            stt(att[:, ti, :], dtt[:, ti, :], -1.0, Abc, ALU.mult, ALU.mult, [('dtt', ti), 'Abc'], [('att', ti)])
        memset(r3(f32v(W0 + 10880, 5 * 260), 5), 0.0, ['xpad'], eng='dve')
        for g in range(4):
            names = ['sx%d_0' % g, 'sx%d_1' % g, 'sB%d' % g, 'sC%d' % g]
            for ci, nm in enumerate(names):
                chan = [g * 2, g * 2 + 1, 8 + g, 12 + g][ci]

                def epi_c(i, tbi, t0, tn, p, pk):
                    for s_ in range(tn // 256):
                        slot = t0 // 256 + s_
                        act(xpad[:, slot, 2:258], p[:, s_ * 256:(s_ + 1) * 256], AF.Copy, [pk, 'xpad'], [('xpad', slot)])
                proj_ws(wsl, 'wm', [wtile(nm)], 16, h_rhs, h_keys, epi_c)
                allx = [('xpad', s_) for s_ in range(5)]
                ts(xpad[:, 1:4, 0:2], xpad[:, 0:3, 256:258], fs, None, ALU.mult, None, allx + ['cf'], ['xhalo'])
                ts(xpad[:, 0:3, 258:260], xpad[:, 1:4, 2:4], fs, None, ALU.mult, None, allx + ['cf', 'xhalo'], ['xhalo2'])
                cw = lambda j: CF('convw', l * 80 + chan * 5 + j, 1)
                ts(cacc, xpad[:, :, 0:256], cw(0), CF('convb', l * 16 + chan, 1), ALU.mult, ALU.add,
                   allx + ['xhalo', 'xhalo2', 'cf'], ['cacc'])
                for j in range(1, 5):
                    stt(cacc, xpad[:, :, j:j + 256], cw(j), cacc, ALU.mult, ALU.add, allx + ['xhalo', 'xhalo2', 'cacc'],
                        ['cacc'] + (allx if j == 4 else []))
                dst = [xc[:, 0, :], xc[:, 1, :], BT, CT][ci]
                act(dst, cacc.rearrange("p a b -> p (a b)"), AF.Silu, ['cacc'], [('cv', ci)])
            def epi_z(i, tbi, t0, tn, p, pk):
                act(zg[:, i, t0:t0 + tn], p[:, :tn], AF.Silu, [pk], [('zg', i, tbi)])
            proj_ws(wsl, 'wm', [wtile('sz%d_0' % g), wtile('sz%d_1' % g)], 16, h_rhs, h_keys, epi_z)
            for ti in range(NT):
                sl = slice(ti * 128, (ti + 1) * 128)
                b2 = nextps()
                for c2 in range(2):
                    mm(ps[b2][:, c2 * 128:(c2 + 1) * 128], xc[:, c2, sl], ident_f, True, True, [('cv', c2), 'cf'], ['ps%d' % b2])
                mm(ps[b2][:, 256:384], BT[:, sl], ident_b, True, True, [('cv', 2), 'cb'], ['ps%d' % b2])
                cp(XtA[:, ti, :], ps[b2][:, 0:256], ['ps%d' % b2], [('XtA', ti)], eng='act')
                cp(BtA[:, ti, :], ps[b2][:, 256:384], ['ps%d' % b2], [('BtA', ti)], eng='act')
            for step in range(NT):
                for d_ in range(2):
                    ti = step if d_ == 0 else NT - 1 - step
                    tri = CF('triU') if d_ == 0 else CF('triL')
                    slot = ti // 2
                    first = (ti % 2 == 0) if d_ == 0 else (ti % 2 == 1)
                    cu = cumt[:, d_ * 8:d_ * 8 + 8]
                    wd_ = wdec[:, d_ * 4:d_ * 4 + 4]
                    Ee = Eend[:, d_ * 4:d_ * 4 + 4]
                    if first:
                        chain = (slot in (1, 2, 3)) if d_ == 0 else (slot in (0, 1, 2))
                        has_h0 = (slot == 0) if d_ == 0 else (slot == 3)
                        if has_h0:
                            dma('sp', Ss32[d_], h0s_d[l, d_, :, g * 256:(g + 1) * 256], [], [('Ss32', d_)], 'ld_h0%d' % d_)
                        elif chain:
                            ts(Ss32[d_], Ss32[d_], fs, None, ALU.mult, None, [('Ss32', d_), 'cf'], [('Ss32', d_)])
                        else:
                            memset(Ss32[d_], 0.0, [('Ss32', d_)], eng='dve')
                    cp(SsIn[d_][:, ti, :], Ss32[d_], [('Ss32', d_)], [('SsIn', d_, ti)], eng='act')
                    a4 = att[:, ti, d_ * 16 + g * 4: d_ * 16 + g * 4 + 4]
                    b = nextps()
                    mm(ps[b][:, 0:4], tri, a4, True, True, ['cf', ('att', ti)], ['ps%d' % b])
                    mm(ps[b][:, 4:8], ones_f, a4, True, True, ['onesf', ('att', ti)], ['ps%d' % b])
                    cp(cu, ps[b][:, 0:8], ['ps%d' % b], [('cumt', d_)])
                    tt(wd_, cu[:, 4:8], cu[:, 0:4], ALU.subtract, [('cumt', d_)], [('wdec', d_)])
                    act(wd_, wd_, AF.Exp, [('wdec', d_)], [('wdec', d_)])
                    tt(wd_, wd_, dtt[:, ti, d_ * 16 + g * 4: d_ * 16 + g * 4 + 4], ALU.mult,
                       [('wdec', d_), ('dtt', ti)], [('wdec', d_)])
                    act(Ee, cu[:, 4:8], AF.Exp, [('cumt', d_)], [('Eend', d_)])
                    xd = xdt[2 + d_]
                    tt(r3(xd, 4), r3(XtA[:, ti, :], 4), wd_.unsqueeze(2).to_broadcast([128, 4, 64]), ALU.mult,
                       [('XtA', ti), ('wdec', d_)], [('xd', d_)])
                    b3 = nextps()
                    mm(ps[b3][:, 0:256], BtA[:, ti, :], xd, True, True, [('BtA', ti), ('xd', d_)], ['ps%d' % b3])
                    tt(r3(Ss32[d_], 4), r3(Ss32[d_], 4), Ee.unsqueeze(2).to_broadcast([128, 4, 64]), ALU.mult,
                       [('Ss32', d_), ('Eend', d_)], [('Ss32', d_)])
                    tt(Ss32[d_], Ss32[d_], ps[b3][:, 0:256], ALU.add, [('Ss32', d_), 'ps%d' % b3], [('Ss32', d_)])
                    last = (ti % 2 == 1) if d_ == 0 else (ti % 2 == 0)
                    if last:
                        out_toks.append(dma('sp', os_d[l, slot, d_, :, g * 256:(g + 1) * 256], Ss32[d_], [('Ss32', d_)], [], 'st_os%d' % d_))
            for tbi, (t0, tn) in enumerate(TB):
                ntl = tn // 128
                by = [nextps(hold=True), nextps(hold=True)]
                for tq in range(ntl):
                    ti = t0 // 128 + tq
                    sl = slice(ti * 128, (ti + 1) * 128)
                    for d_ in range(2):
                        tt(r3(xdt[d_], 4), r3(XtA[:, ti, :], 4),
                           dtt[:, ti, d_ * 16 + g * 4: d_ * 16 + g * 4 + 4].unsqueeze(2).to_broadcast([128, 4, 64]), ALU.mult,
                           [('XtA', ti), ('dtt', ti)], [('xdt', d_)])
                    bc = nextps()
                    mm(ps[bc][:, 0:128], BT[:, sl], CT[:, sl], True, True, [('cv', 2), ('cv', 3)], ['ps%d' % bc])
                    cp(CBm[ti % 2], ps[bc][:, 0:128], ['ps%d' % bc], [('CBm', ti % 2)], eng='act')
                    for d_ in range(2):
                        a4 = att[:, ti, d_ * 16 + g * 4: d_ * 16 + g * 4 + 4]
                        tri = CF('triU') if d_ == 0 else CF('triL')
                        ntri = CF('ntriU') if d_ == 0 else CF('ntriL')
                        mneg = CF('mnegF') if d_ == 0 else CF('mnegB')
                        cp(abc[:, d_ * 4:d_ * 4 + 4, :], a4.unsqueeze(2).to_broadcast([128, 4, 128]), [('att', ti)], [('abc', d_)])
                        bd = nextps()
                        be = nextps()
                        for hh in range(4):
                            o = ps[bd][:, hh * 128:(hh + 1) * 128]
                            mm(o, abc[:, d_ * 4 + hh, :], tri, True, False, [('abc', d_), 'cf'], ['ps%d' % bd])
                            mm(o, ntri, abc[:, d_ * 4 + hh, :], False, False, [('abc', d_), 'cf'], ['ps%d' % bd])
                            mm(o, ident_f, mneg, False, True, ['cf'], ['ps%d' % bd])
                            mm(ps[be][:, hh * 128:(hh + 1) * 128], abc[:, d_ * 4 + hh, :], tri, True, True,
                               [('abc', d_), 'cf'], ['ps%d' % be])
                        sgb = segb[d_]
                        act(sgb, ps[bd][:, 0:512], AF.Exp, ['ps%d' % bd], [('seg', d_)])
                        tt(r3(sgb, 4), r3(sgb, 4), CBm[ti % 2].unsqueeze(1).to_broadcast([128, 4, 128]), ALU.mult,
                           [('seg', d_), ('CBm', ti % 2)], [('seg', d_)])
                        cd = Cdec[d_]
                        act(cd, ps[be][:, 0:512], AF.Exp, ['ps%d' % be], [('Cdec', d_)])
                        tt(r3(cd, 4), r3(cd, 4), CT[:, sl].unsqueeze(1).to_broadcast([128, 4, 128]), ALU.mult,
                           [('Cdec', d_), ('cv', 3)], [('Cdec', d_)])
                    for hh in range(4):
                        o = ps[by[hh // 2]][(hh % 2) * 64:(hh % 2) * 64 + 64, tq * 128:(tq + 1) * 128]
                        for d_ in range(2):
                            mm(o, xdt[d_][:, hh * 64:(hh + 1) * 64], segb[d_][:, hh * 128:(hh + 1) * 128], d_ == 0, False,
                               [('xdt', d_), ('seg', d_)], ['ps%d' % by[hh // 2]])
                        for d_ in range(2):
                            mm(o, SsIn[d_][:, ti, hh * 64:(hh + 1) * 64], Cdec[d_][:, hh * 128:(hh + 1) * 128], False, d_ == 1,
                               [('SsIn', d_, ti), ('Cdec', d_)], ['ps%d' % by[hh // 2]])
                for c2 in range(2):
                    ch = g * 2 + c2
                    stt(tmpf[c2][:, :tn], xc[:, c2, t0:t0 + tn], CF('ssmD', l * 8 + ch, 1), ps[by[c2]][:, :tn], ALU.mult, ALU.add,
                        [('cv', c2), 'cf', 'ps%d' % by[c2]], [('tmpf', c2)])
                    tt(oS[:, ch, t0:t0 + tn], tmpf[c2][:, :tn], zg[:, c2, t0:t0 + tn], ALU.mult,
                       [('tmpf', c2), ('zg', c2, tbi)], [('oS', ch, tbi)])
                release(*by)
                bq = nextps()
                for c2 in range(2):
                    ch = g * 2 + c2
                    act(sqb[c2][:, :tn], oS[:, ch, t0:t0 + tn], AF.Square, [('oS', ch, tbi)], [('sq', c2)])
                    mm(ps[bq][:, :tn], ones_b, sqb[c2][:, :tn], c2 == 0, c2 == 1, [('sq', c2), 'cb'], ['ps%d' % bq])
                tt(ssacc[:, t0:t0 + tn], ssacc[:, t0:t0 + tn], ps[bq][:, :tn], ALU.add, ['ps%d' % bq, 'ssacc'], ['ssacc'])
        act(ssacc, ssacc, AF.Sqrt, ['ssacc'], ['ssacc'], scale=1.0 / 1024.0, bias=1e-6)
        S.op('dve', lambda e: e.reciprocal(out=ssacc, in_=ssacc), reads=['ssacc'], writes=['ssacc'])
        for ch in range(8):
            for tbi, (t0, tn) in enumerate(TB):
                stt(oS[:, ch, t0:t0 + tn], oS[:, ch, t0:t0 + tn], CF('ssmnorm', l * 8 + ch, 1), ssacc[:, t0:t0 + tn],
                    ALU.mult, ALU.mult, [('oS', ch, tbi), 'ssacc', 'cf'], [('oS', ch, tbi)])
        S.barrier()
        if KMIX == 2:
            return

        cosT = f32v(W0, T)
        sinT = f32v(W0 + 5120, T)
        qr = r3(bfv(W0 + 10240, 4 * T), 4)
        krT = bfv(W0 + 20480, T)
        q32 = f32v(W0 + 23040, T)
        qb16 = bfv(W0 + 28160, T)
        vtk = r3(bfv(W0 + 30720, NT * 256), NT)
        v32 = [f32v(W0 + 35840 + i * 1024, 256) for i in range(2)]
        cK = bfv(W0 + 37888, 512)
        cV = r3(bfv(W0 + 38912, 512), 4)
        PT = [bfv(W0 + 39936 + i * 1024, 512) for i in range(3)]
        k32 = f32v(W0 + 43008, T)
        rden = f32v(W0 + 48128, 512)
        sinkE = f32v(W0 + 50176, 8)
        dma('sp', f32v(W0, 2 * T), rope_d, [], ['rope'], 'ld_rope')
        act(sinkE, CF('sink', l * 8, 8), AF.Exp, ['cf'], ['sinkE'])
        if KATT == 0:
            S.barrier()
            return
        dma('pool', wasl[0], wv_d[l, 4], [], [('wa', 0)], 'ld_wa0')
        wv3 = r3(wasl[0], 16)
        for ti in range(NT):
            b = nextps()
            for kc in range(16):
                mm(ps[b][:, 0:256], hT[:, kc, ti * 128:(ti + 1) * 128], wv3[:, kc, :], kc == 0, kc == 15,
                   [('wa', 0), ('h', kc, ti // 4)], ['ps%d' % b])
            act(v32[ti % 2], ps[b][:, 0:256], AF.Copy, ['ps%d' % b], [('v32', ti % 2)])
            cp(vtk[:, ti, :], v32[ti % 2], [('v32', ti % 2)], [('vtk', ti)])
            if os.environ.get('KNOV') != '1':
                out_toks.append(dma('sp', ov_d[l, ti], v32[ti % 2], [('v32', ti % 2)], [], 'st_ov%d' % (ti % 2)))
        scale = 128 ** -0.5
        if KATT == 1:
            S.barrier()
            return

        def rope(dst, srckeys_w):
            for tbi, (t0, tn) in enumerate(TB):
                b = nextps()
                mm(ps[b][:, :tn], CB('pm'), qb16[:, t0:t0 + tn], True, True, ['cb', ('qb16', tbi)], ['ps%d' % b])
                tt(tmpf[0][:, :tn], q32[:, t0:t0 + tn], cosT[:, t0:t0 + tn], ALU.mult, [('q32', tbi), 'rope'], [('tmpf', 0)])
                tt(tmpf[1][:, :tn], ps[b][:, :tn], sinT[:, t0:t0 + tn], ALU.mult, ['ps%d' % b, 'rope'], [('tmpf', 1)])
                tt(dst[:, t0:t0 + tn], tmpf[0][:, :tn], tmpf[1][:, :tn], ALU.add, [('tmpf', 0), ('tmpf', 1)], [(srckeys_w, tbi)])

        for kv in range(2):
            def epi_qk(i, tbi, t0, tn, p, pk):
                act(q32[:, t0:t0 + tn], p[:, :tn], AF.Copy, [pk], [('q32', tbi)])
                cp(qb16[:, t0:t0 + tn], q32[:, t0:t0 + tn], [('q32', tbi)], [('qb16', tbi)])
            for j in range(4):
                proj_ws(wsl, 'wm', [wtile('aq%d' % (kv * 4 + j))], 16, h_rhs, h_keys, epi_qk)
                rope(qr[:, j, :], ('qr', j))

            def epi_kk(i, tbi, t0, tn, p, pk):
                act(q32[:, t0:t0 + tn], p[:, :tn], AF.Copy, [pk], [('q32', tbi)])
                act(k32[:, t0:t0 + tn], p[:, :tn], AF.Copy, [pk], [('k32', tbi)])
                cp(qb16[:, t0:t0 + tn], q32[:, t0:t0 + tn], [('q32', tbi)], [('qb16', tbi)])
            proj_ws(wsl, 'wm', [wtile('ak%d' % kv)], 16, h_rhs, h_keys, epi_kk)
            out_toks.append(dma('sp', ok_d[l, kv], k32, [('k32', 0), ('k32', 1), ('k32', 2)], [], 'st_ok'))
            rope(krT, 'krT')
            if KATT == 2:
                S.barrier()
                return
            dma('pool', cK, ctxk_d[l, kv], [], ['cK'], 'ld_cK')
            dma('pool', cV.rearrange("p a b -> p (a b)"), ctxv_d[l, kv], [], ['cV'], 'ld_cV')
            krkeys = [('krT', 0), ('krT', 1), ('krT', 2)]
            if KATT == 3:
                S.barrier()
                return
            for ti in range(NT):
                if KATT == 4 and ti == 1:
                    S.barrier()
                    return
                sl = slice(ti * 128, (ti + 1) * 128)
                qrhs = qr[:, :, sl]
                qkeys = [(('qr', j), ti // 4) for j in range(4)]
                bo = nextps(hold=True)
                bden = nextps(hold=True)
                chunks = [('c', c) for c in range(4)] if ti < 8 else []
                if ti > 0:
                    chunks.append(('l', ti - 1))
                chunks.append(('l', ti))
                if ti < NT - 1:
                    chunks.append(('l', ti + 1))
                for ci, (kind, c) in enumerate(chunks):
                    bs = nextps()
                    P = PT[ci % 3]
                    if kind == 'c':
                        mm(ps[bs][:, 0:512], cK[:, c * 128:(c + 1) * 128], qrhs, True, True, ['cK'] + qkeys, ['ps%d' % bs])
                        act(P, ps[bs][:, 0:512], AF.Exp, ['ps%d' % bs, 'cf'], [('PT', ci % 3)], scale=scale, bias=ctxbias)
                        vl = cV[:, c, :]
                        vkeys = ['cV']
                    else:
                        mm(ps[bs][:, 0:512], krT[:, c * 128:(c + 1) * 128], qrhs, True, True, krkeys + qkeys, ['ps%d' % bs])
                        act(P, ps[bs][:, 0:512], AF.Exp, ['ps%d' % bs], [('PT', ci % 3)], scale=scale)
                        if c != ti:
                            mi = (ti - 1) if c < ti else (9 + ti)
                            mk = CB('amask', mi * 128, 128)
                            tt(r3(P, 4), r3(P, 4), mk.unsqueeze(1).to_broadcast([128, 4, 128]), ALU.mult,
                               [('PT', ci % 3), 'cb'], [('PT', ci % 3)])
                        vl = vtk[:, c, kv * 128:(kv + 1) * 128]
                        vkeys = [('vtk', c)]
                    first = ci == 0
                    lastc = ci == len(chunks) - 1
                    mm(ps[bo][:, 0:512], vl, P, first, lastc, vkeys + [('PT', ci % 3)], ['ps%d' % bo])
                    mm(ps[bden][:, 0:512], ones_b, P, first, lastc, ['cb', ('PT', ci % 3)], ['ps%d' % bden])
                tt(r3(rden, 4), r3(ps[bden][:, 0:512], 4),
                   sinkE[:, kv * 4:kv * 4 + 4].unsqueeze(2).to_broadcast([128, 4, 128]), ALU.add, ['ps%d' % bden, 'sinkE'], ['rden'])
                S.op('dve', lambda e: e.reciprocal(out=rden, in_=rden), reads=['rden'], writes=['rden'])
                tt(oA[:, kv * 4:kv * 4 + 4, sl], r3(ps[bo][:, 0:512], 4), r3(rden, 4), ALU.mult, ['ps%d' % bo, 'rden'],
                   [('oAt', kv, ti)])
                release(bo, bden)
        S.barrier()
        if KMIX == 3:
            return

        mT = r3(bfv(O_W, 16 * T), 16)
        gsl = [bfv(O_W + 40960 + i * 4096, 2048) for i in range(2)]
        bsl = [bfv(O_X + 61440 + i * 2048, 1024) for i in range(4)]
        gat = [f32v(O_X + 61440 + 8192 + i * 2048, 512) for i in range(3)]
        osrc = [oG, oS, oA]
        mt2 = f32v(O_SCR, 512)
        cnt = proj_ws.cnt
        for c in range(16):
            for b_ in range(3):
                s = cnt['wgl'] % 2
                cnt['wgl'] += 1
                dma('pool', gsl[s], wws_d[l, WSI['br%d_%d' % (b_, c)]], [], [('wg', s)], 'ld_wg%d' % s)
                g3 = r3(gsl[s], 16)
                s2 = cnt['wbl'] % 4
                cnt['wbl'] += 1
                dma('pool', bsl[s2], wbr_d[l, b_ * 16 + c], [], [('wb', s2)], 'ld_wb%d' % s2)
                b3 = r3(bsl[s2], 8)
                for tbi, (t0, tn) in enumerate(TB):
                    bg = nextps()
                    for kc in range(16):
                        mm(ps[bg][:, :tn], g3[:, kc, :], hT[:, kc, t0:t0 + tn], kc == 0, kc == 15,
                           [('wg', s), ('h', kc, tbi)], ['ps%d' % bg])
                    act(gat[tbi][:, :tn], ps[bg][:, :tn], AF.Sigmoid, ['ps%d' % bg], [('gat', tbi)])
                    bp = nextps()
                    for kc in range(8):
                        mm(ps[bp][:, :tn], b3[:, kc, :], osrc[b_][:, kc, t0:t0 + tn], kc == 0, kc == 7,
                           [('wb', s2)], ['ps%d' % bp])
                    if b_ == 0:
                        tt(macc[tbi][:, :tn], ps[bp][:, :tn], gat[tbi][:, :tn], ALU.mult,
                           ['ps%d' % bp, ('gat', tbi)], [('macc', tbi)])
                    else:
                        tt(mt2[:, :tn], ps[bp][:, :tn], gat[tbi][:, :tn], ALU.mult,
                           ['ps%d' % bp, ('gat', tbi)], ['mt2'])
                        if b_ == 1:
                            tt(macc[tbi][:, :tn], macc[tbi][:, :tn], mt2[:, :tn], ALU.add,
                               [('macc', tbi), 'mt2'], [('macc', tbi)])
                        else:
                            tt(mT[:, c, t0:t0 + tn], macc[tbi][:, :tn], mt2[:, :tn], ALU.add,
                               [('macc', tbi), 'mt2'], [('m', c, tbi)])
        S.barrier()
        for q in range(4):
            dma('sp', xflat[:, q * 4 * T:(q + 1) * 4 * T], xsp_d[:, q * 4 * T:(q + 1) * 4 * T], ['xsp%d' % q], xkeys(q), 'ld_x%d' % q)
        wosl = [bfv(O_W + 40960 + i * 4096, 2048) for i in range(2)]

        def epi_o(i, tbi, t0, tn, p, pk):
            grp = 0 if tbi < 2 else 1
            stt(xT[:, i, t0:t0 + tn], p[:, :tn], Gcol(1, i, grp), xT[:, i, t0:t0 + tn], ALU.mult, ALU.add,
                [pk, 'Gvec', ('x', i, tbi)], [('x', i, tbi)])
        proj_ws(wosl, 'wo', [wout_d[l, i] for i in range(16)], 16, lambda kc, t0, tn: mT[:, kc, t0:t0 + tn],
                lambda kc, tbi: [('m', kc, tbi)], epi_o)
        S.barrier()

    ones_f = f32v(O_MODV + 2048, 128)
    macc = [tmpf[0], tmpf[1], f32v(O_RSTD, 512)]
    memset(ones_f, 1.0, ['onesf'], eng='dve')

    import os
    STOP = int(os.environ.get('KSTOP', '9'))
    for l in range(L):
        if STOP >= 1:
            mod_phase(l)
            S.barrier()
        if STOP >= 2:
            ffn_phase(l, 0)
            S.barrier()
        if STOP >= 3:
            mix_phase(l)
        if STOP >= 4:
            ffn_phase(l, 1)
            S.barrier()

    for tbi, (t0, tn) in enumerate(TB):
        rms_rstd(lambda kc: xT[:, kc, t0:t0 + tn], 16, t0, tn, lambda kc: [('x', kc, tbi)],
                 rstd[:, t0:t0 + tn], ('rstd', tbi), D)
        for kc in range(16):
            stt(xT[:, kc, t0:t0 + tn], xT[:, kc, t0:t0 + tn], CF('fnorm', kc, 1), rstd[:, t0:t0 + tn], ALU.mult, ALU.mult,
                [('x', kc, tbi), ('rstd', tbi), 'cf'], [('x', kc, tbi)])
    for q in range(4):
        out_toks.append(dma('sp', y_d[:, q * 4 * T:(q + 1) * 4 * T], xflat[:, q * 4 * T:(q + 1) * 4 * T], xkeys(q), [], 'st_y%d' % q))
    S.final_waits('sp', out_toks)
    S.emit()
    st.close()
    return nc


def _tiles_ws(W, KC):
    K, M = W.shape
    nm = M // 128
    return np.ascontiguousarray(W.reshape(KC, 128, nm, 128).transpose(2, 1, 0, 3)).reshape(nm, 128, KC * 128)


def _tile_cols(W, c0, n, pad_to):
    blk = W[:, c0:c0 + n]
    if n < pad_to:
        blk = np.concatenate([blk, np.zeros((W.shape[0], pad_to - n), W.dtype)], axis=1)
    KC = W.shape[0] // 128
    return np.ascontiguousarray(blk.reshape(KC, 128, pad_to).transpose(1, 0, 2)).reshape(128, KC * pad_to)


def _fm(v):
    v = np.asarray(v)
    C = v.shape[-1] // 128
    lead = v.shape[:-1]
    return np.ascontiguousarray(np.moveaxis(v.reshape(lead + (C, 128)), -1, 0))


def _core_slots(c):
    if c < 2:
        return [('s', c, i) for i in range(4)] + [('p', 30 + c, 0)]
    return [('p', 5 * (c - 2) + i, 0) for i in range(5)]


def _rope_tables(slots):
    half = 32
    freqs = (10000.0 ** (-np.arange(half, dtype=np.float32) / half)).astype(np.float32)
    cosT = np.ones((128, T), np.float32)
    sinT = np.zeros((128, T), np.float32)
    for si, (kind, idx, part) in enumerate(slots):
        if kind != 's':
            continue
        tpos = part * 256 + np.arange(256)
        row = (tpos // 64).astype(np.float32)
        col = (tpos % 64).astype(np.float32)
        for d in range(128):
            pos = row if d < 64 else col
            f = freqs[d % 32]
            ang = (pos * f).astype(np.float32)
            cosT[d, si * 256:(si + 1) * 256] = np.cos(ang)
            sgn = -1.0 if (d % 64) < 32 else 1.0
            sinT[d, si * 256:(si + 1) * 256] = sgn * np.sin(ang)
    return np.concatenate([cosT, sinT], axis=1)


_PROG = {}


NLAYERS = 2
CORES = list(range(NCORES))


def kernel(**inp):
    L = NLAYERS
    f32 = np.float32
    g = {k: np.asarray(v) for k, v in inp.items()}
    cfo, NCF = cf_layout(L)
    cbo, NCB = cb_layout(L)
    ar = np.arange(128)
    ident = np.eye(128, dtype=f32)
    triU = (ar[:, None] <= ar[None, :]).astype(f32)
    triL = (ar[:, None] >= ar[None, :]).astype(f32)

    shared = {}
    shared['wmod'] = np.stack([_tiles_ws(g['w_mod'][l], 16) for l in range(L)])
    wgu = np.empty((L, 2, 88, 128, 2048), f32)
    wd = np.empty((L, 2, 22, 128, 4096), f32)
    for l in range(L):
        for w, pre in enumerate(('ffn1', 'ffn2')):
            tg = _tiles_ws(g[pre + '_w_gate'][l], 16)
            tu = _tiles_ws(g[pre + '_w_up'][l], 16)
            wgu[l, w] = np.stack([tg, tu], axis=1).reshape(88, 128, 2048)
            Wd = g[pre + '_w_down'][l]
            wd[l, w] = np.ascontiguousarray(
                Wd.reshape(11, 4, 128, 2, 8, 128).transpose(0, 3, 2, 4, 1, 5)).reshape(22, 128, 4096)
    shared['wgu'] = wgu
    shared['wd'] = wd
    wws = np.empty((L, len(WS), 128, 2048), f32)
    wv = np.empty((L, 5, 128, 4096), f32)
    wdt = np.empty((L, 128, 512), f32)
    wbr = np.empty((L, 48, 128, 1024), f32)
    wout = np.empty((L, 16, 128, 2048), f32)
    for l in range(L):
        Win = g['w_in'][l]
        for i, (nm, c0, n) in enumerate(WS):
            wws[l, i] = _tile_cols(Win, c0, n, 128)
        for h in range(4):
            wv[l, h] = _tile_cols(Win, 1024 + h * 256, 256, 256)
        wv[l, 4] = _tile_cols(Win, 7488, 256, 256)
        wdt[l] = _tile_cols(Win, 6176, 32, 32)
        for b_, nm in enumerate(('w_br_gla', 'w_br_ssm', 'w_br_attn')):
            wbr[l, b_ * 16:(b_ + 1) * 16] = _tiles_ws(g[nm][l], 8)
        wout[l] = _tiles_ws(g['w_out'][l], 16)
    shared.update(wws=wws, wv=wv, wdt=wdt, wbr=wbr, wout=wout)

    in_maps = []
    for c in CORES:
        slots = _core_slots(c)
        is_s = c < 2
        toks = []
        for (kind, idx, part) in slots:
            if kind == 's':
                toks.append(g['x_sample'][idx, part * 256:(part + 1) * 256])
            else:
                toks.append(g['x_prompt'][idx])
        x = np.concatenate(toks, axis=0)
        xin = np.ascontiguousarray(x.T.reshape(16, 128, T).transpose(1, 0, 2)).reshape(128, 16 * T)
        cf = np.zeros((128, NCF), f32)

        def put(name, arr):
            o, n = cfo[name]
            cf[:, o:o + n] = np.asarray(arr, f32).reshape(128, n)
        put('ident', ident)
        put('triU', triU)
        put('triL', triL)
        put('ntriU', -triU)
        put('ntriL', -triL)
        put('mnegF', np.where(ar[:, None] <= ar[None, :], 0.0, NEG))
        put('mnegB', np.where(ar[:, None] >= ar[None, :], 0.0, NEG))
        put('cummask', np.broadcast_to((np.arange(T) % 128 != 0).astype(f32)[None, :], (128, T)))
        put('fs', np.full((128, 1), 1.0 if is_s else 0.0))
        put('ctxbias', np.full((128, 1), 0.0 if is_s else NEG))
        cA = g['c'][c] if is_s else g['c_ctx']
        put('cvec', np.stack([_fm(cA), _fm(g['c_ctx'])], axis=2))
        nw = np.stack([np.stack([_fm(g[k][l]) for k in ('ffn1_norm', 'mix_norm', 'ffn2_norm')], axis=1) for l in range(L)], axis=1)
        put('normw', nw)
        put('fnorm', _fm(g['final_norm']))
        put('bmod', np.stack([_fm(g['b_mod'][l]) for l in range(L)], axis=1))
        put('nbup', np.stack([_fm(g['gla_b_up'][l]) for l in range(L)], axis=1))
        put('gnorm', np.stack([_fm(g['gla_norm'][l]) for l in range(L)], axis=1))
        put('convw', np.stack([_fm(g['ssm_conv_w'][l]).transpose(0, 2, 1) for l in range(L)], axis=1))
        put('convb', np.stack([_fm(g['ssm_conv_b'][l]) for l in range(L)], axis=1))
        put('ssmD', np.stack([_fm(np.repeat(g['ssm_d'][l], 64)) for l in range(L)], axis=1))
        put('ssmnorm', np.stack([_fm(g['ssm_norm'][l]) for l in range(L)], axis=1))
        put('dtbias', np.broadcast_to(g['ssm_dt_bias'][:L].reshape(1, L * 32), (128, L * 32)))
        put('alog', np.broadcast_to(g['ssm_a_log'][:L].reshape(1, L * 32), (128, L * 32)))
        put('sink', np.broadcast_to(g['attn_sink'][:L].reshape(1, L * 8), (128, L * 8)))

        cbm = np.zeros((128, NCB), f32)

        def putb(name, arr):
            o, n = cbo[name]
            cbm[:, o:o + n] = np.asarray(arr, f32).reshape(128, n)
        putb('ones', np.ones((128, 128)))
        putb('identb', ident)
        perm = np.array([d + 32 if (d % 64) < 32 else d - 32 for d in range(128)])
        pm = np.zeros((128, 128), f32)
        pm[perm, np.arange(128)] = 1.0
        putb('pm', pm)
        putb('maskFB', np.concatenate([triU, triL], axis=1))
        am = np.zeros((128, 18, 128), f32)
        ones = np.ones((128, 128), f32)
        for j in range(NT):
            kind = slots[j // 2][0]
            if j >= 1:
                if kind == 's' and slots[(j - 1) // 2][0] == 's':
                    am[:, j - 1, :] = triL
                elif kind == 'p' and (j % 2 == 1):
                    am[:, j - 1, :] = ones
            if j <= NT - 2:
                if kind == 's' and slots[(j + 1) // 2][0] == 's':
                    am[:, 9 + j, :] = triU
                elif kind == 'p' and (j % 2 == 0):
                    am[:, 9 + j, :] = ones
        putb('amask', am)
        wup = np.zeros((128, L, 2, 512), f32)
        for l in range(L):
            for d_ in range(2):
                wup[d_ * 16:(d_ + 1) * 16, l, d_, :] = g['gla_w_up'][l, d_]
        putb('wup', wup)

        h0g = np.zeros((L, 2, 4, 128, 256), f32)
        h0s = np.zeros((L, 2, 128, 1024), f32)
        ctxk = np.zeros((L, 2, 128, 512), f32)
        ctxv = np.zeros((L, 2, 128, 512), f32)
        if is_s:
            for l in range(L):
                h0g[l] = g['state_gla'][c, l]
                h0s[l] = g['state_ssm'][c, l].transpose(0, 3, 1, 2).reshape(2, 128, 1024)
                ctxk[l] = g['cache_k'][c, l].transpose(1, 2, 0)
                ctxv[l] = g['cache_v'][c, l].reshape(4, 128, 2, 128).transpose(2, 1, 0, 3).reshape(2, 128, 512)
        m = dict(xin=xin, cf=cf, cb=cbm, rope=_rope_tables(slots), h0g=h0g, h0s=h0s, ctxk=ctxk, ctxv=ctxv)
        m.update(shared)
        in_maps.append(m)

    if L not in _PROG:
        _PROG[L] = build_program(L)
    import os
    if os.environ.get('KTRACE') == '1':
        res = run_bass_kernel_spmd(_PROG[L], in_maps, core_ids=list(range(len(CORES))), trace=True)
        print('EXEC_TIME_NS', res.exec_time_ns)
    else:
        res = run_bass_kernel_spmd(_PROG[L], in_maps, core_ids=list(range(len(CORES))))
    R = res.results

    y_prompt = np.zeros((32, 256, D), f32)
    y_sample = np.zeros((2, 1024, D), f32)
    nk = np.zeros((32, L, 256, 2, 128), f32)
    nv = np.zeros((32, L, 256, 2, 128), f32)
    ng = np.zeros((32, L, 2, 4, 128, 256), f32)
    ns = np.zeros((32, L, 2, 16, 64, 128), f32)
    for ci, c in enumerate(CORES):
        r = R[ci]
        y = r['yT'].reshape(128, 16, T).transpose(2, 1, 0).reshape(T, D)
        ok = r['ok'].reshape(L, 2, 128, T)
        ov = r['ov'].reshape(L, T, 2, 128)
        og = r['og'].reshape(L, 5, 2, 4, 128, 256)
        os_ = r['os'].reshape(L, 5, 2, 128, 16, 64)
        for si, (kind, idx, part) in enumerate(_core_slots(c)):
            sl = slice(si * 256, (si + 1) * 256)
            if kind == 's':
                y_sample[idx, part * 256:(part + 1) * 256] = y[sl]
            else:
                y_prompt[idx] = y[sl]
                nk[idx] = ok[:, :, :, sl].transpose(0, 3, 1, 2)
                nv[idx] = ov[:, sl]
                ng[idx] = og[:, si]
                ns[idx] = os_[:, si].transpose(0, 1, 3, 4, 2)
    return (y_prompt, y_sample, nk, nv, ng, ns)
```

```python
import math
import contextlib
import numpy as np
import concourse.bass as bass
import concourse.mybir as mybir
from concourse.bass_utils import run_bass_kernel_spmd

F32 = mybir.dt.float32
BF16 = mybir.dt.bfloat16
AF = mybir.ActivationFunctionType
ALU = mybir.AluOpType

EPOCH = 8192
NCORES = 8
T = 1280
NT = 10
D = 2048
DFF = 5632
TB = [(0, 512), (512, 512), (1024, 256)]
NEG = -30000.0


class Sched:
    ENGS = ('pe', 'act', 'dve', 'pool', 'sp')

    def __init__(self, nc):
        self.nc = nc
        self.ops = {e: [] for e in self.ENGS}
        self.count = {e: 0 for e in self.ENGS}
        self.last_w = {}
        self.readers = {}
        self.waited = {e: {} for e in self.ENGS}
        self.dma_count = {}
        self.semnames = set()
        self.last_tok = {}

    def _need(self, eng, is_dma_consumer, tok, raw):
        semkey, val, peng, pdma = tok
        if not pdma and not is_dma_consumer and peng == eng:
            if eng == 'pe':
                return False
        return True

    def _add_wait(self, eng, waits, tok):
        semkey, val = tok[0], tok[1]
        w = self.waited[eng]
        if w.get(semkey, 0) >= val:
            return
        w[semkey] = val
        if isinstance(semkey, tuple):
            pe_, ep = semkey
            for e2 in range(ep):
                w[(pe_, e2)] = EPOCH
        for i, (k, v) in enumerate(waits):
            if k == semkey:
                waits[i] = (k, max(v, val))
                return
        waits.append((semkey, val))

    def op(self, eng, fn, reads=(), writes=(), dma=None):
        is_dma = dma is not None
        waits = []
        for k in reads:
            t = self.last_w.get(k)
            if t is not None and self._need(eng, is_dma, t, True):
                self._add_wait(eng, waits, t)
        for k in writes:
            t = self.last_w.get(k)
            if t is not None and self._need(eng, is_dma, t, False):
                self._add_wait(eng, waits, t)
            for t in self.readers.get(k, {}).values():
                if self._need(eng, is_dma, t, False):
                    self._add_wait(eng, waits, t)
        if is_dma:
            n = self.dma_count.get(dma, 0) + 1
            self.dma_count[dma] = n
            tok = (dma, 16 * n, eng, True)
            inc = (dma, 16)
            rkey = dma
        else:
            idx = self.count[eng]
            self.count[eng] = idx + 1
            semkey = (eng, idx // EPOCH)
            tok = (semkey, idx % EPOCH + 1, eng, False)
            inc = (semkey, 1)
            rkey = eng
        self.semnames.add(inc[0])
        self.last_tok[inc[0] if is_dma else eng] = tok
        for k in writes:
            self.last_w[k] = tok
            self.readers[k] = {}
        for k in reads:
            self.readers.setdefault(k, {})[rkey] = tok
        self.ops[eng].append((fn, waits, inc))
        return tok

    def barrier(self):
        toks = list(self.last_tok.values())
        for eng in self.ENGS:
            waits = []
            for t in toks:
                if (not t[3]) and t[2] == eng:
                    continue
                self._add_wait(eng, waits, t)
            if waits:
                self.ops[eng].append((None, waits, None))
        self.last_w = {}
        self.readers = {}

    def final_waits(self, eng, toks):
        waits = []
        for t in toks:
            self._add_wait(eng, waits, t)
        self.ops[eng].append((None, waits, None))

    def emit(self):
        nc = self.nc
        with contextlib.ExitStack() as st:
            sems = {}
            for i, k in enumerate(sorted(self.semnames, key=str)):
                sems[k] = st.enter_context(nc.semaphore("sm%d" % i))
            block = st.enter_context(nc.Block())

            def run(engname):
                def body(e):
                    for fn, waits, inc in self.ops[engname]:
                        for (k, v) in waits:
                            e.wait_ge(sems[k], v)
                        if fn is not None:
                            fn(e).then_inc(sems[inc[0]], inc[1])
                return body
            block.tensor(run('pe'))
            block.scalar(run('act'))
            block.vector(run('dve'))
            block.gpsimd(run('pool'))
            block.sync(run('sp'))


def _layout(items):
    off = {}
    o = 0
    for name, n in items:
        off[name] = (o, n)
        o += n
    return off, o


def cf_layout(L):
    return _layout([
        ('ident', 128), ('triU', 128), ('triL', 128), ('ntriU', 128), ('ntriL', 128), ('mnegF', 128), ('mnegB', 128),
        ('cummask', 1280), ('fs', 1), ('ctxbias', 1), ('cvec', 32),
        ('normw', L * 48), ('fnorm', 16), ('bmod', L * 144), ('nbup', L * 8), ('gnorm', L * 2),
        ('convw', L * 80), ('convb', L * 16), ('ssmD', L * 8), ('ssmnorm', L * 8),
        ('dtbias', L * 32), ('alog', L * 32), ('sink', L * 8),
    ])


def cb_layout(L):
    return _layout([
        ('ones', 128), ('identb', 128), ('pm', 128), ('maskFB', 256), ('amask', 18 * 128),
        ('wup', L * 1024),
    ])


def ws_chunks():
    ch = [('gdown', 3072, 32)]
    for h in range(4):
        ch += [('gq%d' % h, h * 128, 128), ('gk%d' % h, 512 + h * 128, 128),
               ('gr%d_0' % h, 2048 + h * 256, 128), ('gr%d_1' % h, 2048 + h * 256 + 128, 128)]
    for g in range(4):
        ch += [('sx%d_0' % g, 4128 + g * 256, 128), ('sx%d_1' % g, 4128 + g * 256 + 128, 128),
               ('sB%d' % g, 5152 + g * 128, 128), ('sC%d' % g, 5664 + g * 128, 128),
               ('sz%d_0' % g, 3104 + g * 256, 128), ('sz%d_1' % g, 3104 + g * 256 + 128, 128)]
    for kv in range(2):
        for j in range(4):
            ch.append(('aq%d' % (kv * 4 + j), 6208 + (kv * 4 + j) * 128, 128))
        ch.append(('ak%d' % kv, 7232 + kv * 128, 128))
    for b in range(3):
        for c in range(16):
            ch.append(('br%d_%d' % (b, c), 7744 + b * 2048 + c * 128, 128))
    return ch


WS = ws_chunks()
WSI = {n: i for i, (n, _, _) in enumerate(WS)}

ARENA_B = 210944
O_CF = 0
O_CB = 12288
O_RSTD = 22528
O_MODV = 27648
O_SCR = 30208
O_X = 36352
O_H = 118272
O_W = 159232


def build_program(L):
    nc = bass.Bass("TRN2", target_bir_lowering=False)
    cfo, NCF = cf_layout(L)
    cbo, NCB = cb_layout(L)
    assert NCF * 4 <= O_CB and NCB * 2 <= O_RSTD - O_CB, (NCF, NCB)

    def din(name, shape):
        return nc.dram_tensor(name, shape, F32, kind="ExternalInput").ap()

    def dout(name, shape):
        return nc.dram_tensor(name, shape, F32, kind="ExternalOutput").ap()

    xin = din("xin", [128, 16 * T])
    cf_d = din("cf", [128, NCF])
    cb_d = din("cb", [128, NCB])
    rope_d = din("rope", [128, 2 * T])
    h0g_d = din("h0g", [L, 2, 4, 128, 256])
    h0s_d = din("h0s", [L, 2, 128, 1024])
    ctxk_d = din("ctxk", [L, 2, 128, 512])
    ctxv_d = din("ctxv", [L, 2, 128, 512])
    wmod_d = din("wmod", [L, 144, 128, 2048])
    wgu_d = din("wgu", [L, 2, 88, 128, 2048])
    wd_d = din("wd", [L, 2, 22, 128, 4096])
    wws_d = din("wws", [L, len(WS), 128, 2048])
    wv_d = din("wv", [L, 5, 128, 4096])
    wdt_d = din("wdt", [L, 128, 512])
    wbr_d = din("wbr", [L, 48, 128, 1024])
    wout_d = din("wout", [L, 16, 128, 2048])
    y_d = dout("yT", [128, 16 * T])
    ok_d = dout("ok", [L, 2, 128, T])
    ov_d = dout("ov", [L, NT, 128, 256])
    og_d = dout("og", [L, 5, 2, 4, 128, 256])
    os_d = dout("os", [L, 5, 2, 128, 1024])
    xsp_d = dout("xspill", [128, 16 * T])

    st = contextlib.ExitStack()
    arena = st.enter_context(nc.sbuf_tensor("arena", [128, ARENA_B // 4], F32))
    ps = [st.enter_context(nc.psum_tensor("ps%d" % i, [128, 512], F32)) for i in range(8)]
    S = Sched(nc)
    out_toks = []

    def f32v(off, n):
        assert off % 4 == 0
        return arena[:, off // 4: off // 4 + n]

    def bfv(off, n):
        assert off % 4 == 0 and n % 2 == 0
        return arena[:, off // 4: off // 4 + n // 2].bitcast(BF16)

    def r3(ap, a):
        return ap.rearrange("p (a b) -> p a b", a=a)

    def act(out, in_, func, r, w, **kw):
        S.op('act', lambda e: e.activation(out=out, in_=in_, func=func, **kw), reads=r, writes=w)

    def tt(out, a, b, op, r, w, eng='dve'):
        S.op(eng, lambda e: e.tensor_tensor(out=out, in0=a, in1=b, op=op), reads=r, writes=w)

    def ts(out, a, s1, s2, op0, op1, r, w):
        if s2 is None:
            S.op('dve', lambda e: e.tensor_scalar(out=out, in0=a, scalar1=s1, scalar2=None, op0=op0), reads=r, writes=w)
        else:
            S.op('dve', lambda e: e.tensor_scalar(out=out, in0=a, scalar1=s1, scalar2=s2, op0=op0, op1=op1), reads=r, writes=w)

    def stt(out, a, sc, b, op0, op1, r, w):
        S.op('dve', lambda e: e.scalar_tensor_tensor(out=out, in0=a, scalar=sc, in1=b, op0=op0, op1=op1), reads=r, writes=w)

    def cp(out, in_, r, w, eng='dve'):
        if eng == 'act':
            S.op(eng, lambda e: e.activation(out=out, in_=in_, func=AF.Copy), reads=r, writes=w)
        else:
            S.op(eng, lambda e: e.tensor_copy(out=out, in_=in_), reads=r, writes=w)

    def mm(out, lhsT, rhs, start, stop, r, w):
        S.op('pe', lambda e: e.matmul(out, lhsT=lhsT, rhs=rhs, start=start, stop=stop), reads=r, writes=w)

    def dma(q, out, in_, r, w, sem):
        return S.op(q, lambda e: e.dma_start(out=out, in_=in_), reads=r, writes=w, dma=sem)

    def memset(ap, val, w, eng='pool'):
        S.op(eng, lambda e: e.memset(ap, val), writes=w)

    cf = f32v(O_CF, NCF)
    cb = bfv(O_CB, NCB)

    def CF(name, i=0, n=None):
        o, m = cfo[name]
        n = m if n is None else n
        return cf[:, o + i: o + i + n]

    def CB(name, i=0, n=None):
        o, m = cbo[name]
        n = m if n is None else n
        return cb[:, o + i: o + i + n]

    rstd = f32v(O_RSTD, T)
    modv = r3(f32v(O_MODV, 288), 144)
    Avec = f32v(O_MODV + 1152, 96)
    Gvec = f32v(O_MODV + 1152 + 384, 96)
    siluc = f32v(O_MODV + 1152 + 768, 32)
    sqb = [bfv(O_SCR + i * 1024, 512) for i in range(2)]
    tmpf = [f32v(O_SCR + 2048 + i * 2048, 512) for i in range(2)]
    xT = r3(f32v(O_X, 16 * T), 16)
    hT = r3(bfv(O_H, 16 * T), 16)
    ones_b = CB('ones')
    ident_b = CB('identb')
    ident_f = CF('ident')
    fs = CF('fs')
    ctxbias = CF('ctxbias')

    dma('sp', cf, cf_d, [], ['cf'], 'ld_cf')
    dma('pool', cb, cb_d, [], ['cb'], 'ld_cb')
    def xkeys(q):
        return [('x', kc, tb_) for kc in range(q * 4, q * 4 + 4) for tb_ in range(3)]
    xflat = f32v(O_X, 16 * T)
    for q in range(4):
        dma('sp', xflat[:, q * 4 * T:(q + 1) * 4 * T], xin[:, q * 4 * T:(q + 1) * 4 * T], [], xkeys(q), 'ld_x%d' % q)
    act(siluc, CF('cvec'), AF.Silu, ['cf'], ['siluc'])
    ts(CF('nbup'), CF('nbup'), -1.0, None, ALU.mult, None, ['cf'], ['cf'])

    psrr = [0]
    held = set()

    def nextps(hold=False):
        while True:
            b = psrr[0] % 8
            psrr[0] += 1
            if b not in held:
                break
        if hold:
            held.add(b)
        return b

    def release(*bs):
        for b in bs:
            held.discard(b)

    scb = bfv(O_CF + 12032, 32)
    scb3 = r3(scb, 16)
    cp(scb, siluc, ['siluc'], ['scb'])

    def mod_finish(l, p, b):
        bm = CF('bmod', l * 144 + p * 48, 48)
        tt(modv[:, p * 48:(p + 1) * 48, :], r3(ps[b][:, 0:96], 48), bm.unsqueeze(2).to_broadcast([128, 48, 2]), ALU.add,
           ['ps%d' % b, 'cf'], ['modv'])
        release(b)
        i = p
        nw = CF('normw', l * 48 + i * 16, 16)
        stt(r3(Avec[:, i * 32:(i + 1) * 32], 16), modv[:, (3 * i + 1) * 16:(3 * i + 2) * 16, :], 1.0,
            nw.unsqueeze(2).to_broadcast([128, 16, 2]), ALU.add, ALU.mult, ['modv', 'cf'], ['Avec'])
        ts(r3(Gvec[:, i * 32:(i + 1) * 32], 16), modv[:, (3 * i + 2) * 16:(3 * i + 3) * 16, :],
           1.0 if i == 1 else 0.5, None, ALU.mult, None, ['modv'], ['Gvec'])

    def mod_part_fast(l, p):
        NS = 6
        slots = [bfv(O_W + i * 4096, 2048) for i in range(NS)]
        b = nextps(hold=True)
        for j in range(48):
            m = p * 48 + j
            s = j % NS
            dma('pool', slots[s], wmod_d[l, m], [], [('wmm', s)], 'ld_mod%d' % s)
            w3 = r3(slots[s], 16)
            for kc in range(16):
                mm(ps[b][:, 2 * j:2 * j + 2], w3[:, kc, :], scb3[:, kc, :], kc == 0, kc == 15,
                   [('wmm', s), 'scb'], ['ps%d' % b])
        mod_finish(l, p, b)

    bg = {'pending': [], 'bank': None}

    def bg_start(l, p, slot_ap):
        bg.update(l=l, p=p, slot=slot_ap, bank=nextps(hold=True), pending=list(range(48)))

    def bg_step(n=1):
        for _ in range(n):
            if bg['bank'] is None or not bg['pending']:
                return
            j = bg['pending'].pop(0)
            m = bg['p'] * 48 + j
            b = bg['bank']
            dma('pool', bg['slot'], wmod_d[bg['l'], m], [], ['bgw'], 'ld_bgw')
            w3 = r3(bg['slot'], 16)
            for kc in range(16):
                mm(ps[b][:, 2 * j:2 * j + 2], w3[:, kc, :], scb3[:, kc, :], kc == 0, kc == 15,
                   ['bgw', 'scb'], ['ps%d' % b])

    def bg_flush():
        if bg['bank'] is None:
            return
        bg_step(48)
        b = bg['bank']
        bg['bank'] = None
        mod_finish(bg['l'], bg['p'], b)

    def Acol(i, kc, grp):
        return Avec[:, i * 32 + kc * 2 + grp: i * 32 + kc * 2 + grp + 1]

    def Bcol(i, kc, grp):
        return modv[:, 3 * i * 16 + kc, grp:grp + 1]

    def Gcol(i, kc, grp):
        return Gvec[:, i * 32 + kc * 2 + grp: i * 32 + kc * 2 + grp + 1]

    def rms_rstd(src_fn, nchunks, t0, tn, rkeys_fn, out_ap, wkey, dim):
        b = nextps()
        for kc in range(nchunks):
            sq = sqb[kc % 2]
            act(sq[:, :tn], src_fn(kc), AF.Square, rkeys_fn(kc), [('sq', kc % 2)])
            mm(ps[b][:, :tn], ones_b, sq[:, :tn], kc == 0, kc == nchunks - 1, [('sq', kc % 2), 'cb'], ['ps%d' % b])
        act(out_ap, ps[b][:, :tn], AF.Sqrt, ['ps%d' % b], [wkey], scale=1.0 / dim, bias=1e-6)
        S.op('dve', lambda e: e.reciprocal(out=out_ap, in_=out_ap), reads=[wkey], writes=[wkey])

    def norm_to_h(i):
        for tbi, (t0, tn) in enumerate(TB):
            grp = 0 if tbi < 2 else 1
            rms_rstd(lambda kc: xT[:, kc, t0:t0 + tn], 16, t0, tn, lambda kc: [('x', kc, tbi)],
                     rstd[:, t0:t0 + tn], ('rstd', tbi), D)
            for kc in range(16):
                tf = tmpf[kc % 2]
                stt(tf[:, :tn], xT[:, kc, t0:t0 + tn], Acol(i, kc, grp), rstd[:, t0:t0 + tn], ALU.mult, ALU.mult,
                    [('x', kc, tbi), ('rstd', tbi), 'Avec'], [('tmpf', kc % 2)])
                act(hT[:, kc, t0:t0 + tn], tf[:, :tn], AF.Identity, [('tmpf', kc % 2), 'modv'], [('h', kc, tbi)],
                    bias=Bcol(i, kc, grp), scale=1.0)

    def proj_ws(wslots, wkey, dram_tiles, KCn, rhs_fn, rhs_keys_fn, epi, M=128, tbs=None):
        cnt = proj_ws.cnt
        for i, dt_ in enumerate(dram_tiles):
            s = cnt[wkey] % len(wslots)
            cnt[wkey] += 1
            wt = wslots[s]
            dma('pool', wt[:, :KCn * 128], dt_, [], [(wkey, s)], 'ld_%s%d' % (wkey, s))
            w3 = r3(wt[:, :KCn * 128], KCn)
            for tbi, (t0, tn) in enumerate(TB if tbs is None else tbs):
                b = nextps()
                for kc in range(KCn):
                    mm(ps[b][:M, :tn], w3[:, kc, 0:M], rhs_fn(kc, t0, tn), kc == 0, kc == KCn - 1,
                       [(wkey, s)] + rhs_keys_fn(kc, tbi), ['ps%d' % b])
                epi(i, tbi, t0, tn, ps[b], 'ps%d' % b)
    import collections
    proj_ws.cnt = collections.defaultdict(int)

    def h_rhs(kc, t0, tn):
        return hT[:, kc, t0:t0 + tn]

    def h_keys(kc, tbi):
        return [('h', kc, tbi)]

    def ffn_phase(l, which):
        i_sub = 0 if which == 0 else 2
        norm_to_h(i_sub)
        gT = r3(bfv(O_W, 4 * T), 4)
        sg = [bfv(O_W + 10240 + i * 2560, T) for i in range(2)]
        wsl = [bfv(O_W + 15360 + i * 4096, 2048) for i in range(3)]
        wdsl = [bfv(O_W + 31744 + i * 8192, 4096) for i in range(2)]
        if which == 0:
            bg_start(l, 1, bfv(O_W + 27648, 2048))
        elif l + 1 < L:
            bg_start(l + 1, 0, bfv(O_W + 27648, 2048))
        for g in range(11):
            for j in range(4):
                hc = g * 4 + j

                def epi(i, tbi, t0, tn, p, pk, j=j):
                    if i == 0:
                        act(sg[j % 2][:, t0:t0 + tn], p[:, :tn], AF.Silu, [pk], [('sg', j % 2, tbi)])
                    else:
                        tt(gT[:, j, t0:t0 + tn], p[:, :tn], sg[j % 2][:, t0:t0 + tn], ALU.mult,
                           [pk, ('sg', j % 2, tbi)], [('g', j, tbi)])
                proj_ws(wsl, 'wf', [wgu_d[l, which, 2 * hc], wgu_d[l, which, 2 * hc + 1]], 16, h_rhs, h_keys, epi)
                bg_step(1)
            for half in range(2):
                s = proj_ws.cnt['wd'] % 2
                proj_ws.cnt['wd'] += 1
                dma('pool', wdsl[s], wd_d[l, which, g * 2 + half], [], [('wd', s)], 'ld_wd%d' % s)
                w4 = wdsl[s].rearrange("p (m k j) -> p m k j", m=8, k=4)
                for mi in range(8):
                    mc = half * 8 + mi
                    for tbi, (t0, tn) in enumerate(TB):
                        grp = 0 if tbi < 2 else 1
                        b = nextps()
                        for k in range(4):
                            mm(ps[b][:, :tn], w4[:, mi, k, :], gT[:, k, t0:t0 + tn], k == 0, k == 3,
                               [('wd', s), ('g', k, tbi)], ['ps%d' % b])
                        stt(xT[:, mc, t0:t0 + tn], ps[b][:, :tn], Gcol(i_sub, mc, grp), xT[:, mc, t0:t0 + tn],
                            ALU.mult, ALU.add, ['ps%d' % b, 'Gvec', ('x', mc, tbi)], [('x', mc, tbi)])
                bg_step(1)
        bg_flush()

    import os
    KMIX = int(os.environ.get('KMIX', '9'))
    KATT = int(os.environ.get('KATT', '99'))

    def mix_phase(l):
        norm_to_h(1)
        for q in range(4):
            dma('sp', xsp_d[:, q * 4 * T:(q + 1) * 4 * T], xflat[:, q * 4 * T:(q + 1) * 4 * T], xkeys(q), ['xsp%d' % q], 'st_x%d' % q)
        S.barrier()
        if KMIX == 0:
            for q in range(4):
                dma('sp', xflat[:, q * 4 * T:(q + 1) * 4 * T], xsp_d[:, q * 4 * T:(q + 1) * 4 * T], ['xsp%d' % q], xkeys(q), 'ld_x%d' % q)
            S.barrier()
            return
        oG = r3(bfv(O_X, 8 * T), 8)
        oS = r3(bfv(O_X + 20480, 8 * T), 8)
        oA = r3(bfv(O_X + 40960, 8 * T), 8)
        wsl = [bfv(O_X + 61440 + i * 4096, 2048) for i in range(3)]
        wasl = [bfv(O_X + 61440 + 12288, 4096)]

        def wtile(name):
            return wws_d[l, WSI[name]]

        W0 = O_W
        gdT = f32v(W0, T)
        gdb = bfv(W0 + 5120, T)
        qT = f32v(W0 + 7680, T)
        kT = f32v(W0 + 12800, T)
        rg = r3(bfv(W0 + 17920, 2 * T), 2)
        vtok = r3(bfv(W0 + 23040, NT * 256), NT)
        la = f32v(W0 + 28160, T)
        ex = f32v(W0 + 33280, T)
        qk = [bfv(W0 + 38400 + i * 2560, T) for i in range(4)]
        sc1 = O_X + 20480
        SIn = [r3(bfv(sc1 + d_ * 5120, NT * 256), NT) for d_ in range(2)]
        S32 = [f32v(sc1 + 10240 + d_ * 1024, 256) for d_ in range(2)]
        Stmps = [f32v(sc1 + 12288, 256), f32v(sc1 + 31744, 256)]
        ktok = [bfv(sc1 + 13312 + d_ * 256, 128) for d_ in range(2)]
        ABt = [bfv(sc1 + 13824 + i * 512, 256) for i in range(2)]
        o32 = r3(f32v(sc1 + 14848, 2 * T), 2)
        rs2 = f32v(sc1 + 25088, T)
        Ecol = f32v(sc1 + 30208, 32)
        cendb = f32v(sc1 + 30336, 16)

        def epi_gd(i, tbi, t0, tn, p, pk):
            cp(gdb[0:32, t0:t0 + tn], p[0:32, :tn], [pk], [('gdb', tbi)])
        proj_ws(wsl, 'wm', [wtile('gdown')], 16, h_rhs, h_keys, epi_gd, M=32)
        lnscale = math.log(128 ** -0.5)
        for h in range(4):
            def epi_q(i, tbi, t0, tn, p, pk):
                act(qT[:, t0:t0 + tn], p[:, :tn], AF.Copy, [pk], [('qT', tbi)])

            def epi_k(i, tbi, t0, tn, p, pk):
                act(kT[:, t0:t0 + tn], p[:, :tn], AF.Copy, [pk], [('kT', tbi)])

            def epi_r(i, tbi, t0, tn, p, pk):
                act(rg[:, i, t0:t0 + tn], p[:, :tn], AF.Silu, [pk], [('rg', i, tbi)])
            proj_ws(wsl, 'wm', [wtile('gq%d' % h)], 16, h_rhs, h_keys, epi_q)
            proj_ws(wsl, 'wm', [wtile('gk%d' % h)], 16, h_rhs, h_keys, epi_k)
            proj_ws(wsl, 'wm', [wtile('gr%d_0' % h), wtile('gr%d_1' % h)], 16, h_rhs, h_keys, epi_r)
            dma('pool', wasl[0], wv_d[l, h], [], [('wa', 0)], 'ld_wa0')
            wv3 = r3(wasl[0], 16)
            for tt_ in range(NT):
                b = nextps()
                for kc in range(16):
                    mm(ps[b][:, 0:256], hT[:, kc, tt_ * 128:(tt_ + 1) * 128], wv3[:, kc, :], kc == 0, kc == 15,
                       [('wa', 0), ('h', kc, tt_ // 4)], ['ps%d' % b])
                cp(vtok[:, tt_, :], ps[b][:, 0:256], ['ps%d' % b], [('vtok', tt_)])
            for d_ in range(2):
                wup = CB('wup', l * 1024 + d_ * 512 + h * 128, 128)
                nb = CF('nbup', l * 8 + d_ * 4 + h, 1)
                for tbi, (t0, tn) in enumerate(TB):
                    b = nextps()
                    mm(ps[b][:, :tn], wup[0:32, :], gdb[0:32, t0:t0 + tn], True, True, ['cb', ('gdb', tbi)], ['ps%d' % b])
                    act(ex[:, t0:t0 + tn], ps[b][:, :tn], AF.Exp, ['ps%d' % b, 'cf'], ['ex'], scale=-1.0, bias=nb)
                    act(ex[:, t0:t0 + tn], ex[:, t0:t0 + tn], AF.Ln, ['ex'], ['ex'], bias=1.0, scale=1.0)
                ts(la, ex, -1.0 / 16.0, None, ALU.mult, None, ['ex'], ['la'])
                S.op('dve', lambda e: e.tensor_tensor_scan(out=ex, data0=CF('cummask'), data1=la, initial=0.0,
                                                           op0=ALU.mult, op1=ALU.add),
                     reads=['la', 'cf'], writes=['ex'])
                la3 = r3(la, NT)
                ex3 = r3(ex, NT)
                if d_ == 1:
                    tt(la3, la3, ex3, ALU.subtract, ['la', 'ex'], ['la'])
                    cp(cendb[:, 0:NT], ex3[:, :, 127], ['ex'], ['cendb'])
                    tt(ex3, la3, cendb[:, 0:NT].unsqueeze(2).to_broadcast([128, NT, 128]), ALU.add, ['la', 'ex', 'cendb'], ['ex'])
                    ckey = 'ex'
                    ecol = ex3[:, :, 0]
                else:
                    ckey = 'ex'
                    ecol = ex3[:, :, 127]
                act(Ecol[:, d_ * 16:d_ * 16 + NT], ecol, AF.Exp, [ckey], [('Ecol', d_)])
                act(la, ex, AF.Exp, [ckey], ['la'], bias=lnscale, scale=1.0)
                tt(qk[2 * d_], qT, la, ALU.mult, ['la', ('qT', 0), ('qT', 1), ('qT', 2)], [('qk', 2 * d_)])
                act(la, ex, AF.Exp, [ckey, ('qk', 2 * d_)], ['la'], scale=-1.0)
                tt(qk[2 * d_ + 1], kT, la, ALU.mult, ['la', ('kT', 0), ('kT', 1), ('kT', 2)], [('qk', 2 * d_ + 1)])
            for step in range(NT):
              for d_ in range(2):
                    ti = step if d_ == 0 else NT - 1 - step
                    kt_ = qk[2 * d_ + 1]
                    Stmp = Stmps[d_]
                    slot = ti // 2
                    first = (ti % 2 == 0) if d_ == 0 else (ti % 2 == 1)
                    if first:
                        chain = (slot in (1, 2, 3)) if d_ == 0 else (slot in (0, 1, 2))
                        has_h0 = (slot == 0) if d_ == 0 else (slot == 3)
                        if has_h0:
                            dma('sp', S32[d_], h0g_d[l, d_, h], [], [('S32', d_)], 'ld_h0%d' % d_)
                        elif chain:
                            ts(S32[d_], S32[d_], fs, None, ALU.mult, None, [('S32', d_), 'cf'], [('S32', d_)])
                        else:
                            memset(S32[d_], 0.0, [('S32', d_)], eng='dve')
                    cp(SIn[d_][:, ti, :], S32[d_], [('S32', d_)], [('SIn', d_, ti)], eng='act')
                    b = nextps()
                    mm(ps[b][:, 0:128], kt_[:, ti * 128:(ti + 1) * 128], ident_b, True, True, [('qk', 2 * d_ + 1), 'cb'], ['ps%d' % b])
                    cp(ktok[d_], ps[b][:, 0:128], ['ps%d' % b], [('ktok', d_)])
                    b2 = nextps()
                    mm(ps[b2][:, 0:256], ktok[d_], vtok[:, ti, :], True, True, [('ktok', d_), ('vtok', ti)], ['ps%d' % b2])
                    tt(Stmp, ps[b2][:, 0:256], S32[d_], ALU.add, ['ps%d' % b2, ('S32', d_)], [('Stmp', d_)])
                    ts(S32[d_], Stmp, Ecol[:, d_ * 16 + ti:d_ * 16 + ti + 1], None, ALU.mult, None,
                       [('Stmp', d_), ('Ecol', d_)], [('S32', d_)])
                    last = (ti % 2 == 1) if d_ == 0 else (ti % 2 == 0)
                    if last:
                        out_toks.append(dma('sp', og_d[l, slot, d_, h], S32[d_], [('S32', d_)], [], 'st_og%d' % d_))
            for tbi, (t0, tn) in enumerate(TB):
                ntl = tn // 128
                bo = [nextps(hold=True), nextps(hold=True)]
                for tq in range(ntl):
                    ti = t0 // 128 + tq
                    sl = slice(ti * 128, (ti + 1) * 128)
                    b = nextps()
                    mm(ps[b][:, 0:128], qk[1][:, sl], qk[0][:, sl], True, True, [('qk', 0), ('qk', 1)], ['ps%d' % b])
                    mm(ps[b][:, 128:256], qk[3][:, sl], qk[2][:, sl], True, True, [('qk', 2), ('qk', 3)], ['ps%d' % b])
                    AB = ABt[ti % 2]
                    tt(AB, ps[b][:, 0:256], CB('maskFB'), ALU.mult, ['ps%d' % b, 'cb'], [('AB', ti % 2)])
                    for vc in range(2):
                        o = ps[bo[vc]][:, tq * 128:(tq + 1) * 128]
                        vs = vtok[:, ti, vc * 128:(vc + 1) * 128]
                        mm(o, vs, AB[:, 0:128], True, False, [('vtok', ti), ('AB', ti % 2)], ['ps%d' % bo[vc]])
                        mm(o, vs, AB[:, 128:256], False, False, [('vtok', ti), ('AB', ti % 2)], ['ps%d' % bo[vc]])
                        mm(o, SIn[0][:, ti, vc * 128:(vc + 1) * 128], qk[0][:, sl], False, False,
                           [('SIn', 0, ti), ('qk', 0)], ['ps%d' % bo[vc]])
                        mm(o, SIn[1][:, ti, vc * 128:(vc + 1) * 128], qk[2][:, sl], False, True,
                           [('SIn', 1, ti), ('qk', 2)], ['ps%d' % bo[vc]])
                for vc in range(2):
                    act(o32[:, vc, t0:t0 + tn], ps[bo[vc]][:, :tn], AF.Copy, ['ps%d' % bo[vc]], [('o32', vc, tbi)])
                release(*bo)
                rms_rstd(lambda vc: o32[:, vc, t0:t0 + tn], 2, t0, tn, lambda vc: [('o32', vc, tbi)],
                         rs2[:, t0:t0 + tn], ('rs2', tbi), 256)
                for vc in range(2):
                    stt(o32[:, vc, t0:t0 + tn], o32[:, vc, t0:t0 + tn], CF('gnorm', l * 2 + vc, 1), rs2[:, t0:t0 + tn],
                        ALU.mult, ALU.mult, [('o32', vc, tbi), ('rs2', tbi), 'cf'], [('o32', vc, tbi)])
                    tt(oG[:, h * 2 + vc, t0:t0 + tn], o32[:, vc, t0:t0 + tn], rg[:, vc, t0:t0 + tn], ALU.mult,
                       [('o32', vc, tbi), ('rg', vc, tbi)], [('oG', h * 2 + vc, tbi)])
        S.barrier()
        if KMIX == 1:
            return

        dtt = r3(f32v(W0, NT * 32), NT)
        att = r3(f32v(W0 + 1280, NT * 32), NT)
        Abc = f32v(W0 + 2560, 32)
        abc = r3(f32v(W0 + 2688, 1024), 8)
        BtA = r3(bfv(W0 + 6784, NT * 128), NT)
        XtA = r3(bfv(O_RSTD, NT * 256), NT)
        xpad = r3(f32v(W0 + 10880, 5 * 260), 5)
        cacc = r3(f32v(W0 + 16080, T), 5)
        xc = r3(f32v(W0 + 21200, 2 * T), 2)
        BT = bfv(W0 + 31440, T)
        CT = bfv(W0 + 34000, T)
        zg = r3(bfv(W0 + 36560, 2 * T), 2)
        ssacc = f32v(W0 + 41680, T)
        cumt = f32v(W0 + 46800, 32)
        wdec = f32v(W0 + 46928, 8)
        cend = f32v(W0 + 46960, 8)
        Eend = f32v(W0 + 46992, 8)
        sc2 = O_X + 40960
        xtok = f32v(sc2, 256)
        Btok = bfv(sc2 + 1024, 128)
        xdt = [bfv(sc2 + 1280 + i * 512, 256) for i in range(4)]
        segb = [bfv(sc2 + 3328 + i * 1024, 512) for i in range(2)]
        CBm = [bfv(sc2 + 5376 + i * 256, 128) for i in range(2)]
        Cdec = [bfv(sc2 + 5888 + i * 1024, 512) for i in range(2)]
        SsIn = [r3(bfv(sc2 + 7936 + d_ * 5120, NT * 256), NT) for d_ in range(2)]
        Ss32 = [f32v(sc2 + 18176 + d_ * 1024, 256) for d_ in range(2)]
        wdtb = wasl[0]
        dma('pool', wdtb[:, 0:512], wdt_d[l], [], [('wa', 0)], 'ld_wa0')
        wdt3 = r3(wdtb[:, 0:512], 16)
        act(Abc, CF('alog', l * 32, 32), AF.Exp, ['cf'], ['Abc'])
        memset(ssacc, 0.0, ['ssacc'], eng='dve')
        for ti in range(NT):
            b = nextps()
            for kc in range(16):
                mm(ps[b][:, 0:32], hT[:, kc, ti * 128:(ti + 1) * 128], wdt3[:, kc, :], kc == 0, kc == 15,
                   [('wa', 0), ('h', kc, ti // 4)], ['ps%d' % b])
            tt(dtt[:, ti, :], ps[b][:, 0:32], CF('dtbias', l * 32, 32), ALU.add, ['ps%d' % b, 'cf'], [('dtt', ti)])
            act(dtt[:, ti, :], dtt[:, ti, :], AF.Exp, [('dtt', ti)], [('dtt', ti)])
            act(dtt[:, ti, :], dtt[:, ti, :], AF.Ln, [('dtt', ti)], [('dtt', ti)], bias=1.0, scale=1.0)
            stt(att[:, ti, :], dtt[:, ti, :], -1.0, Abc, ALU.mult, ALU.mult, [('dtt', ti), 'Abc'], [('att', ti)])
        memset(r3(f32v(W0 + 10880, 5 * 260), 5), 0.0, ['xpad'], eng='dve')
        for g in range(4):
            names = ['sx%d_0' % g, 'sx%d_1' % g, 'sB%d' % g, 'sC%d' % g]
            for ci, nm in enumerate(names):
                chan = [g * 2, g * 2 + 1, 8 + g, 12 + g][ci]

                def epi_c(i, tbi, t0, tn, p, pk):
                    for s_ in range(tn // 256):
                        slot = t0 // 256 + s_
                        act(xpad[:, slot, 2:258], p[:, s_ * 256:(s_ + 1) * 256], AF.Copy, [pk, 'xpad'], [('xpad', slot)])
                proj_ws(wsl, 'wm', [wtile(nm)], 16, h_rhs, h_keys, epi_c)
                allx = [('xpad', s_) for s_ in range(5)]
                ts(xpad[:, 1:4, 0:2], xpad[:, 0:3, 256:258], fs, None, ALU.mult, None, allx + ['cf'], ['xhalo'])
                ts(xpad[:, 0:3, 258:260], xpad[:, 1:4, 2:4], fs, None, ALU.mult, None, allx + ['cf', 'xhalo'], ['xhalo2'])
                cw = lambda j: CF('convw', l * 80 + chan * 5 + j, 1)
                ts(cacc, xpad[:, :, 0:256], cw(0), CF('convb', l * 16 + chan, 1), ALU.mult, ALU.add,
                   allx + ['xhalo', 'xhalo2', 'cf'], ['cacc'])
                for j in range(1, 5):
                    stt(cacc, xpad[:, :, j:j + 256], cw(j), cacc, ALU.mult, ALU.add, allx + ['xhalo', 'xhalo2', 'cacc'],
                        ['cacc'] + (allx if j == 4 else []))
                dst = [xc[:, 0, :], xc[:, 1, :], BT, CT][ci]
                act(dst, cacc.rearrange("p a b -> p (a b)"), AF.Silu, ['cacc'], [('cv', ci)])
            def epi_z(i, tbi, t0, tn, p, pk):
                act(zg[:, i, t0:t0 + tn], p[:, :tn], AF.Silu, [pk], [('zg', i, tbi)])
            proj_ws(wsl, 'wm', [wtile('sz%d_0' % g), wtile('sz%d_1' % g)], 16, h_rhs, h_keys, epi_z)
            for ti in range(NT):
                sl = slice(ti * 128, (ti + 1) * 128)
                b2 = nextps()
                for c2 in range(2):
                    mm(ps[b2][:, c2 * 128:(c2 + 1) * 128], xc[:, c2, sl], ident_f, True, True, [('cv', c2), 'cf'], ['ps%d' % b2])
                mm(ps[b2][:, 256:384], BT[:, sl], ident_b, True, True, [('cv', 2), 'cb'], ['ps%d' % b2])
                cp(XtA[:, ti, :], ps[b2][:, 0:256], ['ps%d' % b2], [('XtA', ti)], eng='act')
                cp(BtA[:, ti, :], ps[b2][:, 256:384], ['ps%d' % b2], [('BtA', ti)], eng='act')
            for step in range(NT):
                for d_ in range(2):
                    ti = step if d_ == 0 else NT - 1 - step
                    tri = CF('triU') if d_ == 0 else CF('triL')
                    slot = ti // 2
                    first = (ti % 2 == 0) if d_ == 0 else (ti % 2 == 1)
                    cu = cumt[:, d_ * 8:d_ * 8 + 8]
                    wd_ = wdec[:, d_ * 4:d_ * 4 + 4]
                    Ee = Eend[:, d_ * 4:d_ * 4 + 4]
                    if first:
                        chain = (slot in (1, 2, 3)) if d_ == 0 else (slot in (0, 1, 2))
                        has_h0 = (slot == 0) if d_ == 0 else (slot == 3)
                        if has_h0:
                            dma('sp', Ss32[d_], h0s_d[l, d_, :, g * 256:(g + 1) * 256], [], [('Ss32', d_)], 'ld_h0%d' % d_)
                        elif chain:
                            ts(Ss32[d_], Ss32[d_], fs, None, ALU.mult, None, [('Ss32', d_), 'cf'], [('Ss32', d_)])
                        else:
                            memset(Ss32[d_], 0.0, [('Ss32', d_)], eng='dve')
                    cp(SsIn[d_][:, ti, :], Ss32[d_], [('Ss32', d_)], [('SsIn', d_, ti)], eng='act')
                    a4 = att[:, ti, d_ * 16 + g * 4: d_ * 16 + g * 4 + 4]
                    b = nextps()
                    mm(ps[b][:, 0:4], tri, a4, True, True, ['cf', ('att', ti)], ['ps%d' % b])
                    mm(ps[b][:, 4:8], ones_f, a4, True, True, ['onesf', ('att', ti)], ['ps%d' % b])
                    cp(cu, ps[b][:, 0:8], ['ps%d' % b], [('cumt', d_)])
                    tt(wd_, cu[:, 4:8], cu[:, 0:4], ALU.subtract, [('cumt', d_)], [('wdec', d_)])
                    act(wd_, wd_, AF.Exp, [('wdec', d_)], [('wdec', d_)])
                    tt(wd_, wd_, dtt[:, ti, d_ * 16 + g * 4: d_ * 16 + g * 4 + 4], ALU.mult,
                       [('wdec', d_), ('dtt', ti)], [('wdec', d_)])
                    act(Ee, cu[:, 4:8], AF.Exp, [('cumt', d_)], [('Eend', d_)])
                    xd = xdt[2 + d_]
                    tt(r3(xd, 4), r3(XtA[:, ti, :], 4), wd_.unsqueeze(2).to_broadcast([128, 4, 64]), ALU.mult,
                       [('XtA', ti), ('wdec', d_)], [('xd', d_)])
                    b3 = nextps()
                    mm(ps[b3][:, 0:256], BtA[:, ti, :], xd, True, True, [('BtA', ti), ('xd', d_)], ['ps%d' % b3])
                    tt(r3(Ss32[d_], 4), r3(Ss32[d_], 4), Ee.unsqueeze(2).to_broadcast([128, 4, 64]), ALU.mult,
                       [('Ss32', d_), ('Eend', d_)], [('Ss32', d_)])
                    tt(Ss32[d_], Ss32[d_], ps[b3][:, 0:256], ALU.add, [('Ss32', d_), 'ps%d' % b3], [('Ss32', d_)])
                    last = (ti % 2 == 1) if d_ == 0 else (ti % 2 == 0)
                    if last:
                        out_toks.append(dma('sp', os_d[l, slot, d_, :, g * 256:(g + 1) * 256], Ss32[d_], [('Ss32', d_)], [], 'st_os%d' % d_))
            for tbi, (t0, tn) in enumerate(TB):
                ntl = tn // 128
                by = [nextps(hold=True), nextps(hold=True)]
                for tq in range(ntl):
                    ti = t0 // 128 + tq
                    sl = slice(ti * 128, (ti + 1) * 128)
                    for d_ in range(2):
                        tt(r3(xdt[d_], 4), r3(XtA[:, ti, :], 4),
                           dtt[:, ti, d_ * 16 + g * 4: d_ * 16 + g * 4 + 4].unsqueeze(2).to_broadcast([128, 4, 64]), ALU.mult,
                           [('XtA', ti), ('dtt', ti)], [('xdt', d_)])
                    bc = nextps()
                    mm(ps[bc][:, 0:128], BT[:, sl], CT[:, sl], True, True, [('cv', 2), ('cv', 3)], ['ps%d' % bc])
                    cp(CBm[ti % 2], ps[bc][:, 0:128], ['ps%d' % bc], [('CBm', ti % 2)], eng='act')
                    for d_ in range(2):
                        a4 = att[:, ti, d_ * 16 + g * 4: d_ * 16 + g * 4 + 4]
                        tri = CF('triU') if d_ == 0 else CF('triL')
                        ntri = CF('ntriU') if d_ == 0 else CF('ntriL')
                        mneg = CF('mnegF') if d_ == 0 else CF('mnegB')
                        cp(abc[:, d_ * 4:d_ * 4 + 4, :], a4.unsqueeze(2).to_broadcast([128, 4, 128]), [('att', ti)], [('abc', d_)])
                        bd = nextps()
                        be = nextps()
                        for hh in range(4):
                            o = ps[bd][:, hh * 128:(hh + 1) * 128]
                            mm(o, abc[:, d_ * 4 + hh, :], tri, True, False, [('abc', d_), 'cf'], ['ps%d' % bd])
                            mm(o, ntri, abc[:, d_ * 4 + hh, :], False, False, [('abc', d_), 'cf'], ['ps%d' % bd])
                            mm(o, ident_f, mneg, False, True, ['cf'], ['ps%d' % bd])
                            mm(ps[be][:, hh * 128:(hh + 1) * 128], abc[:, d_ * 4 + hh, :], tri, True, True,
                               [('abc', d_), 'cf'], ['ps%d' % be])
                        sgb = segb[d_]
                        act(sgb, ps[bd][:, 0:512], AF.Exp, ['ps%d' % bd], [('seg', d_)])
                        tt(r3(sgb, 4), r3(sgb, 4), CBm[ti % 2].unsqueeze(1).to_broadcast([128, 4, 128]), ALU.mult,
                           [('seg', d_), ('CBm', ti % 2)], [('seg', d_)])
                        cd = Cdec[d_]
                        act(cd, ps[be][:, 0:512], AF.Exp, ['ps%d' % be], [('Cdec', d_)])
                        tt(r3(cd, 4), r3(cd, 4), CT[:, sl].unsqueeze(1).to_broadcast([128, 4, 128]), ALU.mult,
                           [('Cdec', d_), ('cv', 3)], [('Cdec', d_)])
                    for hh in range(4):
                        o = ps[by[hh // 2]][(hh % 2) * 64:(hh % 2) * 64 + 64, tq * 128:(tq + 1) * 128]
                        for d_ in range(2):
                            mm(o, xdt[d_][:, hh * 64:(hh + 1) * 64], segb[d_][:, hh * 128:(hh + 1) * 128], d_ == 0, False,
                               [('xdt', d_), ('seg', d_)], ['ps%d' % by[hh // 2]])
                        for d_ in range(2):
                            mm(o, SsIn[d_][:, ti, hh * 64:(hh + 1) * 64], Cdec[d_][:, hh * 128:(hh + 1) * 128], False, d_ == 1,
                               [('SsIn', d_, ti), ('Cdec', d_)], ['ps%d' % by[hh // 2]])
                for c2 in range(2):
                    ch = g * 2 + c2
                    stt(tmpf[c2][:, :tn], xc[:, c2, t0:t0 + tn], CF('ssmD', l * 8 + ch, 1), ps[by[c2]][:, :tn], ALU.mult, ALU.add,
                        [('cv', c2), 'cf', 'ps%d' % by[c2]], [('tmpf', c2)])
                    tt(oS[:, ch, t0:t0 + tn], tmpf[c2][:, :tn], zg[:, c2, t0:t0 + tn], ALU.mult,
                       [('tmpf', c2), ('zg', c2, tbi)], [('oS', ch, tbi)])
                release(*by)
                bq = nextps()
                for c2 in range(2):
                    ch = g * 2 + c2
                    act(sqb[c2][:, :tn], oS[:, ch, t0:t0 + tn], AF.Square, [('oS', ch, tbi)], [('sq', c2)])
                    mm(ps[bq][:, :tn], ones_b, sqb[c2][:, :tn], c2 == 0, c2 == 1, [('sq', c2), 'cb'], ['ps%d' % bq])
                tt(ssacc[:, t0:t0 + tn], ssacc[:, t0:t0 + tn], ps[bq][:, :tn], ALU.add, ['ps%d' % bq, 'ssacc'], ['ssacc'])
        act(ssacc, ssacc, AF.Sqrt, ['ssacc'], ['ssacc'], scale=1.0 / 1024.0, bias=1e-6)
        S.op('dve', lambda e: e.reciprocal(out=ssacc, in_=ssacc), reads=['ssacc'], writes=['ssacc'])
        for ch in range(8):
            for tbi, (t0, tn) in enumerate(TB):
                stt(oS[:, ch, t0:t0 + tn], oS[:, ch, t0:t0 + tn], CF('ssmnorm', l * 8 + ch, 1), ssacc[:, t0:t0 + tn],
                    ALU.mult, ALU.mult, [('oS', ch, tbi), 'ssacc', 'cf'], [('oS', ch, tbi)])
        S.barrier()
        if KMIX == 2:
            return

        cosT = f32v(W0, T)
        sinT = f32v(W0 + 5120, T)
        qr = r3(bfv(W0 + 10240, 4 * T), 4)
        krT = bfv(W0 + 20480, T)
        q32 = f32v(W0 + 23040, T)
        qb16 = bfv(W0 + 28160, T)
        vtk = r3(bfv(W0 + 30720, NT * 256), NT)
        v32 = [f32v(W0 + 35840 + i * 1024, 256) for i in range(2)]
        cK = bfv(W0 + 37888, 512)
        cV = r3(bfv(W0 + 38912, 512), 4)
        PT = [bfv(W0 + 39936 + i * 1024, 512) for i in range(3)]
        k32 = f32v(W0 + 43008, T)
        rden = f32v(W0 + 48128, 512)
        sinkE = f32v(W0 + 50176, 8)
        dma('sp', f32v(W0, 2 * T), rope_d, [], ['rope'], 'ld_rope')
        act(sinkE, CF('sink', l * 8, 8), AF.Exp, ['cf'], ['sinkE'])
        if KATT == 0:
            S.barrier()
            return
        dma('pool', wasl[0], wv_d[l, 4], [], [('wa', 0)], 'ld_wa0')
        wv3 = r3(wasl[0], 16)
        for ti in range(NT):
            b = nextps()
            for kc in range(16):
                mm(ps[b][:, 0:256], hT[:, kc, ti * 128:(ti + 1) * 128], wv3[:, kc, :], kc == 0, kc == 15,
                   [('wa', 0), ('h', kc, ti // 4)], ['ps%d' % b])
            act(v32[ti % 2], ps[b][:, 0:256], AF.Copy, ['ps%d' % b], [('v32', ti % 2)])
            cp(vtk[:, ti, :], v32[ti % 2], [('v32', ti % 2)], [('vtk', ti)])
            if os.environ.get('KNOV') != '1':
                out_toks.append(dma('sp', ov_d[l, ti], v32[ti % 2], [('v32', ti % 2)], [], 'st_ov%d' % (ti % 2)))
        scale = 128 ** -0.5
        if KATT == 1:
            S.barrier()
            return

        def rope(dst, srckeys_w):
            for tbi, (t0, tn) in enumerate(TB):
                b = nextps()
                mm(ps[b][:, :tn], CB('pm'), qb16[:, t0:t0 + tn], True, True, ['cb', ('qb16', tbi)], ['ps%d' % b])
                tt(tmpf[0][:, :tn], q32[:, t0:t0 + tn], cosT[:, t0:t0 + tn], ALU.mult, [('q32', tbi), 'rope'], [('tmpf', 0)])
                tt(tmpf[1][:, :tn], ps[b][:, :tn], sinT[:, t0:t0 + tn], ALU.mult, ['ps%d' % b, 'rope'], [('tmpf', 1)])
                tt(dst[:, t0:t0 + tn], tmpf[0][:, :tn], tmpf[1][:, :tn], ALU.add, [('tmpf', 0), ('tmpf', 1)], [(srckeys_w, tbi)])

        for kv in range(2):
            def epi_qk(i, tbi, t0, tn, p, pk):
                act(q32[:, t0:t0 + tn], p[:, :tn], AF.Copy, [pk], [('q32', tbi)])
                cp(qb16[:, t0:t0 + tn], q32[:, t0:t0 + tn], [('q32', tbi)], [('qb16', tbi)])
            for j in range(4):
                proj_ws(wsl, 'wm', [wtile('aq%d' % (kv * 4 + j))], 16, h_rhs, h_keys, epi_qk)
                rope(qr[:, j, :], ('qr', j))

            def epi_kk(i, tbi, t0, tn, p, pk):
                act(q32[:, t0:t0 + tn], p[:, :tn], AF.Copy, [pk], [('q32', tbi)])
                act(k32[:, t0:t0 + tn], p[:, :tn], AF.Copy, [pk], [('k32', tbi)])
                cp(qb16[:, t0:t0 + tn], q32[:, t0:t0 + tn], [('q32', tbi)], [('qb16', tbi)])
            proj_ws(wsl, 'wm', [wtile('ak%d' % kv)], 16, h_rhs, h_keys, epi_kk)
            out_toks.append(dma('sp', ok_d[l, kv], k32, [('k32', 0), ('k32', 1), ('k32', 2)], [], 'st_ok'))
            rope(krT, 'krT')
            if KATT == 2:
                S.barrier()
                return
            dma('pool', cK, ctxk_d[l, kv], [], ['cK'], 'ld_cK')
            dma('pool', cV.rearrange("p a b -> p (a b)"), ctxv_d[l, kv], [], ['cV'], 'ld_cV')
            krkeys = [('krT', 0), ('krT', 1), ('krT', 2)]
            if KATT == 3:
                S.barrier()
                return
            for ti in range(NT):
                if KATT == 4 and ti == 1:
                    S.barrier()
                    return
                sl = slice(ti * 128, (ti + 1) * 128)
                qrhs = qr[:, :, sl]
                qkeys = [(('qr', j), ti // 4) for j in range(4)]
                bo = nextps(hold=True)
                bden = nextps(hold=True)
                chunks = [('c', c) for c in range(4)] if ti < 8 else []
                if ti > 0:
                    chunks.append(('l', ti - 1))
                chunks.append(('l', ti))
                if ti < NT - 1:
                    chunks.append(('l', ti + 1))
                for ci, (kind, c) in enumerate(chunks):
                    bs = nextps()
                    P = PT[ci % 3]
                    if kind == 'c':
                        mm(ps[bs][:, 0:512], cK[:, c * 128:(c + 1) * 128], qrhs, True, True, ['cK'] + qkeys, ['ps%d' % bs])
                        act(P, ps[bs][:, 0:512], AF.Exp, ['ps%d' % bs, 'cf'], [('PT', ci % 3)], scale=scale, bias=ctxbias)
                        vl = cV[:, c, :]
                        vkeys = ['cV']
                    else:
                        mm(ps[bs][:, 0:512], krT[:, c * 128:(c + 1) * 128], qrhs, True, True, krkeys + qkeys, ['ps%d' % bs])
                        act(P, ps[bs][:, 0:512], AF.Exp, ['ps%d' % bs], [('PT', ci % 3)], scale=scale)
                        if c != ti:
                            mi = (ti - 1) if c < ti else (9 + ti)
                            mk = CB('amask', mi * 128, 128)
                            tt(r3(P, 4), r3(P, 4), mk.unsqueeze(1).to_broadcast([128, 4, 128]), ALU.mult,
                               [('PT', ci % 3), 'cb'], [('PT', ci % 3)])
                        vl = vtk[:, c, kv * 128:(kv + 1) * 128]
                        vkeys = [('vtk', c)]
                    first = ci == 0
                    lastc = ci == len(chunks) - 1
                    mm(ps[bo][:, 0:512], vl, P, first, lastc, vkeys + [('PT', ci % 3)], ['ps%d' % bo])
                    mm(ps[bden][:, 0:512], ones_b, P, first, lastc, ['cb', ('PT', ci % 3)], ['ps%d' % bden])
                tt(r3(rden, 4), r3(ps[bden][:, 0:512], 4),
                   sinkE[:, kv * 4:kv * 4 + 4].unsqueeze(2).to_broadcast([128, 4, 128]), ALU.add, ['ps%d' % bden, 'sinkE'], ['rden'])
                S.op('dve', lambda e: e.reciprocal(out=rden, in_=rden), reads=['rden'], writes=['rden'])
                tt(oA[:, kv * 4:kv * 4 + 4, sl], r3(ps[bo][:, 0:512], 4), r3(rden, 4), ALU.mult, ['ps%d' % bo, 'rden'],
                   [('oAt', kv, ti)])
                release(bo, bden)
        S.barrier()
        if KMIX == 3:
            return

        mT = r3(bfv(O_W, 16 * T), 16)
        gsl = [bfv(O_W + 40960 + i * 4096, 2048) for i in range(2)]
        bsl = [bfv(O_X + 61440 + i * 2048, 1024) for i in range(4)]
        gat = [f32v(O_X + 61440 + 8192 + i * 2048, 512) for i in range(3)]
        osrc = [oG, oS, oA]
        mt2 = f32v(O_SCR, 512)
        cnt = proj_ws.cnt
        bg_start(l, 2, bfv(O_X + 75776, 2048))
        for c in range(16):
            for b_ in range(3):
                bg_step(1)
                s = cnt['wgl'] % 2
                cnt['wgl'] += 1
                dma('pool', gsl[s], wws_d[l, WSI['br%d_%d' % (b_, c)]], [], [('wg', s)], 'ld_wg%d' % s)
                g3 = r3(gsl[s], 16)
                s2 = cnt['wbl'] % 4
                cnt['wbl'] += 1
                dma('pool', bsl[s2], wbr_d[l, b_ * 16 + c], [], [('wb', s2)], 'ld_wb%d' % s2)
                b3 = r3(bsl[s2], 8)
                for tbi, (t0, tn) in enumerate(TB):
                    bg = nextps()
                    for kc in range(16):
                        mm(ps[bg][:, :tn], g3[:, kc, :], hT[:, kc, t0:t0 + tn], kc == 0, kc == 15,
                           [('wg', s), ('h', kc, tbi)], ['ps%d' % bg])
                    act(gat[tbi][:, :tn], ps[bg][:, :tn], AF.Sigmoid, ['ps%d' % bg], [('gat', tbi)])
                    bp = nextps()
                    for kc in range(8):
                        mm(ps[bp][:, :tn], b3[:, kc, :], osrc[b_][:, kc, t0:t0 + tn], kc == 0, kc == 7,
                           [('wb', s2)], ['ps%d' % bp])
                    if b_ == 0:
                        tt(macc[tbi][:, :tn], ps[bp][:, :tn], gat[tbi][:, :tn], ALU.mult,
                           ['ps%d' % bp, ('gat', tbi)], [('macc', tbi)])
                    else:
                        tt(mt2[:, :tn], ps[bp][:, :tn], gat[tbi][:, :tn], ALU.mult,
                           ['ps%d' % bp, ('gat', tbi)], ['mt2'])
                        if b_ == 1:
                            tt(macc[tbi][:, :tn], macc[tbi][:, :tn], mt2[:, :tn], ALU.add,
                               [('macc', tbi), 'mt2'], [('macc', tbi)])
                        else:
                            tt(mT[:, c, t0:t0 + tn], macc[tbi][:, :tn], mt2[:, :tn], ALU.add,
                               [('macc', tbi), 'mt2'], [('m', c, tbi)])
        bg_flush()
        S.barrier()
        for q in range(4):
            dma('sp', xflat[:, q * 4 * T:(q + 1) * 4 * T], xsp_d[:, q * 4 * T:(q + 1) * 4 * T], ['xsp%d' % q], xkeys(q), 'ld_x%d' % q)
        wosl = [bfv(O_W + 40960 + i * 4096, 2048) for i in range(2)]

        def epi_o(i, tbi, t0, tn, p, pk):
            grp = 0 if tbi < 2 else 1
            stt(xT[:, i, t0:t0 + tn], p[:, :tn], Gcol(1, i, grp), xT[:, i, t0:t0 + tn], ALU.mult, ALU.add,
                [pk, 'Gvec', ('x', i, tbi)], [('x', i, tbi)])
        proj_ws(wosl, 'wo', [wout_d[l, i] for i in range(16)], 16, lambda kc, t0, tn: mT[:, kc, t0:t0 + tn],
                lambda kc, tbi: [('m', kc, tbi)], epi_o)
        S.barrier()

    ones_f = f32v(O_MODV + 2048, 128)
    macc = [tmpf[0], tmpf[1], f32v(O_RSTD, 512)]
    memset(ones_f, 1.0, ['onesf'], eng='dve')

    import os
    STOP = int(os.environ.get('KSTOP', '9'))
    for l in range(L):
        if STOP >= 1 and l == 0:
            mod_part_fast(0, 0)
            S.barrier()
        if STOP >= 2:
            ffn_phase(l, 0)
            S.barrier()
        if STOP >= 3:
            mix_phase(l)
        if STOP >= 4:
            ffn_phase(l, 1)
            S.barrier()

    for tbi, (t0, tn) in enumerate(TB):
        rms_rstd(lambda kc: xT[:, kc, t0:t0 + tn], 16, t0, tn, lambda kc: [('x', kc, tbi)],
                 rstd[:, t0:t0 + tn], ('rstd', tbi), D)
        for kc in range(16):
            stt(xT[:, kc, t0:t0 + tn], xT[:, kc, t0:t0 + tn], CF('fnorm', kc, 1), rstd[:, t0:t0 + tn], ALU.mult, ALU.mult,
                [('x', kc, tbi), ('rstd', tbi), 'cf'], [('x', kc, tbi)])
    for q in range(4):
        out_toks.append(dma('sp', y_d[:, q * 4 * T:(q + 1) * 4 * T], xflat[:, q * 4 * T:(q + 1) * 4 * T], xkeys(q), [], 'st_y%d' % q))
    S.final_waits('sp', out_toks)
    S.emit()
    st.close()
    return nc


def _tiles_ws(W, KC):
    K, M = W.shape
    nm = M // 128
    return np.ascontiguousarray(W.reshape(KC, 128, nm, 128).transpose(2, 1, 0, 3)).reshape(nm, 128, KC * 128)


def _tile_cols(W, c0, n, pad_to):
    blk = W[:, c0:c0 + n]
    if n < pad_to:
        blk = np.concatenate([blk, np.zeros((W.shape[0], pad_to - n), W.dtype)], axis=1)
    KC = W.shape[0] // 128
    return np.ascontiguousarray(blk.reshape(KC, 128, pad_to).transpose(1, 0, 2)).reshape(128, KC * pad_to)


def _fm(v):
    v = np.asarray(v)
    C = v.shape[-1] // 128
    lead = v.shape[:-1]
    return np.ascontiguousarray(np.moveaxis(v.reshape(lead + (C, 128)), -1, 0))


def _core_slots(c):
    if c < 2:
        return [('s', c, i) for i in range(4)] + [('p', 30 + c, 0)]
    return [('p', 5 * (c - 2) + i, 0) for i in range(5)]


def _rope_tables(slots):
    half = 32
    freqs = (10000.0 ** (-np.arange(half, dtype=np.float32) / half)).astype(np.float32)
    cosT = np.ones((128, T), np.float32)
    sinT = np.zeros((128, T), np.float32)
    for si, (kind, idx, part) in enumerate(slots):
        if kind != 's':
            continue
        tpos = part * 256 + np.arange(256)
        row = (tpos // 64).astype(np.float32)
        col = (tpos % 64).astype(np.float32)
        for d in range(128):
            pos = row if d < 64 else col
            f = freqs[d % 32]
            ang = (pos * f).astype(np.float32)
            cosT[d, si * 256:(si + 1) * 256] = np.cos(ang)
            sgn = -1.0 if (d % 64) < 32 else 1.0
            sinT[d, si * 256:(si + 1) * 256] = sgn * np.sin(ang)
    return np.concatenate([cosT, sinT], axis=1)


_PROG = {}


NLAYERS = 2
CORES = list(range(NCORES))


def kernel(**inp):
    L = NLAYERS
    f32 = np.float32
    g = {k: np.asarray(v) for k, v in inp.items()}
    cfo, NCF = cf_layout(L)
    cbo, NCB = cb_layout(L)
    ar = np.arange(128)
    ident = np.eye(128, dtype=f32)
    triU = (ar[:, None] <= ar[None, :]).astype(f32)
    triL = (ar[:, None] >= ar[None, :]).astype(f32)

    shared = {}
    shared['wmod'] = np.stack([_tiles_ws(g['w_mod'][l], 16) for l in range(L)])
    wgu = np.empty((L, 2, 88, 128, 2048), f32)
    wd = np.empty((L, 2, 22, 128, 4096), f32)
    for l in range(L):
        for w, pre in enumerate(('ffn1', 'ffn2')):
            tg = _tiles_ws(g[pre + '_w_gate'][l], 16)
            tu = _tiles_ws(g[pre + '_w_up'][l], 16)
            wgu[l, w] = np.stack([tg, tu], axis=1).reshape(88, 128, 2048)
            Wd = g[pre + '_w_down'][l]
            wd[l, w] = np.ascontiguousarray(
                Wd.reshape(11, 4, 128, 2, 8, 128).transpose(0, 3, 2, 4, 1, 5)).reshape(22, 128, 4096)
    shared['wgu'] = wgu
    shared['wd'] = wd
    wws = np.empty((L, len(WS), 128, 2048), f32)
    wv = np.empty((L, 5, 128, 4096), f32)
    wdt = np.empty((L, 128, 512), f32)
    wbr = np.empty((L, 48, 128, 1024), f32)
    wout = np.empty((L, 16, 128, 2048), f32)
    for l in range(L):
        Win = g['w_in'][l]
        for i, (nm, c0, n) in enumerate(WS):
            wws[l, i] = _tile_cols(Win, c0, n, 128)
        for h in range(4):
            wv[l, h] = _tile_cols(Win, 1024 + h * 256, 256, 256)
        wv[l, 4] = _tile_cols(Win, 7488, 256, 256)
        wdt[l] = _tile_cols(Win, 6176, 32, 32)
        for b_, nm in enumerate(('w_br_gla', 'w_br_ssm', 'w_br_attn')):
            wbr[l, b_ * 16:(b_ + 1) * 16] = _tiles_ws(g[nm][l], 8)
        wout[l] = _tiles_ws(g['w_out'][l], 16)
    shared.update(wws=wws, wv=wv, wdt=wdt, wbr=wbr, wout=wout)

    in_maps = []
    for c in CORES:
        slots = _core_slots(c)
        is_s = c < 2
        toks = []
        for (kind, idx, part) in slots:
            if kind == 's':
                toks.append(g['x_sample'][idx, part * 256:(part + 1) * 256])
            else:
                toks.append(g['x_prompt'][idx])
        x = np.concatenate(toks, axis=0)
        xin = np.ascontiguousarray(x.T.reshape(16, 128, T).transpose(1, 0, 2)).reshape(128, 16 * T)
        cf = np.zeros((128, NCF), f32)

        def put(name, arr):
            o, n = cfo[name]
            cf[:, o:o + n] = np.asarray(arr, f32).reshape(128, n)
        put('ident', ident)
        put('triU', triU)
        put('triL', triL)
        put('ntriU', -triU)
        put('ntriL', -triL)
        put('mnegF', np.where(ar[:, None] <= ar[None, :], 0.0, NEG))
        put('mnegB', np.where(ar[:, None] >= ar[None, :], 0.0, NEG))
        put('cummask', np.broadcast_to((np.arange(T) % 128 != 0).astype(f32)[None, :], (128, T)))
        put('fs', np.full((128, 1), 1.0 if is_s else 0.0))
        put('ctxbias', np.full((128, 1), 0.0 if is_s else NEG))
        cA = g['c'][c] if is_s else g['c_ctx']
        put('cvec', np.stack([_fm(cA), _fm(g['c_ctx'])], axis=2))
        nw = np.stack([np.stack([_fm(g[k][l]) for k in ('ffn1_norm', 'mix_norm', 'ffn2_norm')], axis=1) for l in range(L)], axis=1)
        put('normw', nw)
        put('fnorm', _fm(g['final_norm']))
        put('bmod', np.stack([_fm(g['b_mod'][l]) for l in range(L)], axis=1))
        put('nbup', np.stack([_fm(g['gla_b_up'][l]) for l in range(L)], axis=1))
        put('gnorm', np.stack([_fm(g['gla_norm'][l]) for l in range(L)], axis=1))
        put('convw', np.stack([_fm(g['ssm_conv_w'][l]).transpose(0, 2, 1) for l in range(L)], axis=1))
        put('convb', np.stack([_fm(g['ssm_conv_b'][l]) for l in range(L)], axis=1))
        put('ssmD', np.stack([_fm(np.repeat(g['ssm_d'][l], 64)) for l in range(L)], axis=1))
        put('ssmnorm', np.stack([_fm(g['ssm_norm'][l]) for l in range(L)], axis=1))
        put('dtbias', np.broadcast_to(g['ssm_dt_bias'][:L].reshape(1, L * 32), (128, L * 32)))
        put('alog', np.broadcast_to(g['ssm_a_log'][:L].reshape(1, L * 32), (128, L * 32)))
        put('sink', np.broadcast_to(g['attn_sink'][:L].reshape(1, L * 8), (128, L * 8)))

        cbm = np.zeros((128, NCB), f32)

        def putb(name, arr):
            o, n = cbo[name]
            cbm[:, o:o + n] = np.asarray(arr, f32).reshape(128, n)
        putb('ones', np.ones((128, 128)))
        putb('identb', ident)
        perm = np.array([d + 32 if (d % 64) < 32 else d - 32 for d in range(128)])
        pm = np.zeros((128, 128), f32)
        pm[perm, np.arange(128)] = 1.0
        putb('pm', pm)
        putb('maskFB', np.concatenate([triU, triL], axis=1))
        am = np.zeros((128, 18, 128), f32)
        ones = np.ones((128, 128), f32)
        for j in range(NT):
            kind = slots[j // 2][0]
            if j >= 1:
                if kind == 's' and slots[(j - 1) // 2][0] == 's':
                    am[:, j - 1, :] = triL
                elif kind == 'p' and (j % 2 == 1):
                    am[:, j - 1, :] = ones
            if j <= NT - 2:
                if kind == 's' and slots[(j + 1) // 2][0] == 's':
                    am[:, 9 + j, :] = triU
                elif kind == 'p' and (j % 2 == 0):
                    am[:, 9 + j, :] = ones
        putb('amask', am)
        wup = np.zeros((128, L, 2, 512), f32)
        for l in range(L):
            for d_ in range(2):
                wup[d_ * 16:(d_ + 1) * 16, l, d_, :] = g['gla_w_up'][l, d_]
        putb('wup', wup)

        h0g = np.zeros((L, 2, 4, 128, 256), f32)
        h0s = np.zeros((L, 2, 128, 1024), f32)
        ctxk = np.zeros((L, 2, 128, 512), f32)
        ctxv = np.zeros((L, 2, 128, 512), f32)
        if is_s:
            for l in range(L):
                h0g[l] = g['state_gla'][c, l]
                h0s[l] = g['state_ssm'][c, l].transpose(0, 3, 1, 2).reshape(2, 128, 1024)
                ctxk[l] = g['cache_k'][c, l].transpose(1, 2, 0)
                ctxv[l] = g['cache_v'][c, l].reshape(4, 128, 2, 128).transpose(2, 1, 0, 3).reshape(2, 128, 512)
        m = dict(xin=xin, cf=cf, cb=cbm, rope=_rope_tables(slots), h0g=h0g, h0s=h0s, ctxk=ctxk, ctxv=ctxv)
        m.update(shared)
        in_maps.append(m)

    if L not in _PROG:
        _PROG[L] = build_program(L)
    import os
    if os.environ.get('KTRACE') == '1':
        res = run_bass_kernel_spmd(_PROG[L], in_maps, core_ids=list(range(len(CORES))), trace=True)
        print('EXEC_TIME_NS', res.exec_time_ns)
    else:
        res = run_bass_kernel_spmd(_PROG[L], in_maps, core_ids=list(range(len(CORES))))
    R = res.results

    y_prompt = np.zeros((32, 256, D), f32)
    y_sample = np.zeros((2, 1024, D), f32)
    nk = np.zeros((32, L, 256, 2, 128), f32)
    nv = np.zeros((32, L, 256, 2, 128), f32)
    ng = np.zeros((32, L, 2, 4, 128, 256), f32)
    ns = np.zeros((32, L, 2, 16, 64, 128), f32)
    for ci, c in enumerate(CORES):
        r = R[ci]
        y = r['yT'].reshape(128, 16, T).transpose(2, 1, 0).reshape(T, D)
        ok = r['ok'].reshape(L, 2, 128, T)
        ov = r['ov'].reshape(L, T, 2, 128)
        og = r['og'].reshape(L, 5, 2, 4, 128, 256)
        os_ = r['os'].reshape(L, 5, 2, 128, 16, 64)
        for si, (kind, idx, part) in enumerate(_core_slots(c)):
            sl = slice(si * 256, (si + 1) * 256)
            if kind == 's':
                y_sample[idx, part * 256:(part + 1) * 256] = y[sl]
            else:
                y_prompt[idx] = y[sl]
                nk[idx] = ok[:, :, :, sl].transpose(0, 3, 1, 2)
                nv[idx] = ov[:, sl]
                ng[idx] = og[:, si]
                ns[idx] = os_[:, si].transpose(0, 1, 3, 4, 2)
    return (y_prompt, y_sample, nk, nv, ng, ns)
```

```python
import math
import contextlib
import numpy as np
import concourse.bass as bass
import concourse.mybir as mybir
from concourse.bass_utils import run_bass_kernel_spmd

F32 = mybir.dt.float32
BF16 = mybir.dt.bfloat16
AF = mybir.ActivationFunctionType
ALU = mybir.AluOpType

EPOCH = 8192
NCORES = 8
T = 1280
NT = 10
D = 2048
DFF = 5632
TB = [(0, 512), (512, 512), (1024, 256)]
NEG = -30000.0


class Sched:
    ENGS = ('pe', 'act', 'dve', 'pool', 'sp')

    def __init__(self, nc):
        self.nc = nc
        self.ops = {e: [] for e in self.ENGS}
        self.count = {e: 0 for e in self.ENGS}
        self.last_w = {}
        self.readers = {}
        self.waited = {e: {} for e in self.ENGS}
        self.dma_count = {}
        self.semnames = set()
        self.last_tok = {}

    def _need(self, eng, is_dma_consumer, tok, raw):
        semkey, val, peng, pdma = tok
        if not pdma and not is_dma_consumer and peng == eng:
            if eng == 'pe':
                return False
        return True

    def _add_wait(self, eng, waits, tok):
        semkey, val = tok[0], tok[1]
        w = self.waited[eng]
        if w.get(semkey, 0) >= val:
            return
        w[semkey] = val
        if isinstance(semkey, tuple):
            pe_, ep = semkey
            for e2 in range(ep):
                w[(pe_, e2)] = EPOCH
        for i, (k, v) in enumerate(waits):
            if k == semkey:
                waits[i] = (k, max(v, val))
                return
        waits.append((semkey, val))

    def op(self, eng, fn, reads=(), writes=(), dma=None):
        is_dma = dma is not None
        waits = []
        for k in reads:
            t = self.last_w.get(k)
            if t is not None and self._need(eng, is_dma, t, True):
                self._add_wait(eng, waits, t)
        for k in writes:
            t = self.last_w.get(k)
            if t is not None and self._need(eng, is_dma, t, False):
                self._add_wait(eng, waits, t)
            for t in self.readers.get(k, {}).values():
                if self._need(eng, is_dma, t, False):
                    self._add_wait(eng, waits, t)
        if is_dma:
            n = self.dma_count.get(dma, 0) + 1
            self.dma_count[dma] = n
            tok = (dma, 16 * n, eng, True)
            inc = (dma, 16)
            rkey = dma
        else:
            idx = self.count[eng]
            self.count[eng] = idx + 1
            semkey = (eng, idx // EPOCH)
            tok = (semkey, idx % EPOCH + 1, eng, False)
            inc = (semkey, 1)
            rkey = eng
        self.semnames.add(inc[0])
        self.last_tok[inc[0] if is_dma else eng] = tok
        for k in writes:
            self.last_w[k] = tok
            self.readers[k] = {}
        for k in reads:
            self.readers.setdefault(k, {})[rkey] = tok
        self.ops[eng].append((fn, waits, inc))
        return tok

    def barrier(self):
        toks = list(self.last_tok.values())
        for eng in self.ENGS:
            waits = []
            for t in toks:
                if (not t[3]) and t[2] == eng:
                    continue
                self._add_wait(eng, waits, t)
            if waits:
                self.ops[eng].append((None, waits, None))
        self.last_w = {}
        self.readers = {}

    def final_waits(self, eng, toks):
        waits = []
        for t in toks:
            self._add_wait(eng, waits, t)
        self.ops[eng].append((None, waits, None))

    def emit(self):
        nc = self.nc
        with contextlib.ExitStack() as st:
            sems = {}
            for i, k in enumerate(sorted(self.semnames, key=str)):
                sems[k] = st.enter_context(nc.semaphore("sm%d" % i))
            block = st.enter_context(nc.Block())

            def run(engname):
                def body(e):
                    for fn, waits, inc in self.ops[engname]:
                        for (k, v) in waits:
                            e.wait_ge(sems[k], v)
                        if fn is not None:
                            fn(e).then_inc(sems[inc[0]], inc[1])
                return body
            block.tensor(run('pe'))
            block.scalar(run('act'))
            block.vector(run('dve'))
            block.gpsimd(run('pool'))
            block.sync(run('sp'))


def _layout(items):
    off = {}
    o = 0
    for name, n in items:
        off[name] = (o, n)
        o += n
    return off, o


def cf_layout(L):
    return _layout([
        ('ident', 128), ('triU', 128), ('triL', 128), ('ntriU', 128), ('ntriL', 128), ('mnegF', 128), ('mnegB', 128),
        ('cummask', 1280), ('fs', 1), ('ctxbias', 1), ('cvec', 32),
        ('normw', L * 48), ('fnorm', 16), ('bmod', L * 144), ('nbup', L * 8), ('gnorm', L * 2),
        ('convw', L * 80), ('convb', L * 16), ('ssmD', L * 8), ('ssmnorm', L * 8),
        ('dtbias', L * 32), ('alog', L * 32), ('sink', L * 8),
    ])


def cb_layout(L):
    return _layout([
        ('ones', 128), ('identb', 128), ('pm', 128), ('maskFB', 256), ('amask', 18 * 128),
        ('wup', L * 1024),
    ])


def ws_chunks():
    ch = [('gdown', 3072, 32)]
    for h in range(4):
        ch += [('gq%d' % h, h * 128, 128), ('gk%d' % h, 512 + h * 128, 128),
               ('gr%d_0' % h, 2048 + h * 256, 128), ('gr%d_1' % h, 2048 + h * 256 + 128, 128)]
    for g in range(4):
        ch += [('sx%d_0' % g, 4128 + g * 256, 128), ('sx%d_1' % g, 4128 + g * 256 + 128, 128),
               ('sB%d' % g, 5152 + g * 128, 128), ('sC%d' % g, 5664 + g * 128, 128),
               ('sz%d_0' % g, 3104 + g * 256, 128), ('sz%d_1' % g, 3104 + g * 256 + 128, 128)]
    for kv in range(2):
        for j in range(4):
            ch.append(('aq%d' % (kv * 4 + j), 6208 + (kv * 4 + j) * 128, 128))
        ch.append(('ak%d' % kv, 7232 + kv * 128, 128))
    for b in range(3):
        for c in range(16):
            ch.append(('br%d_%d' % (b, c), 7744 + b * 2048 + c * 128, 128))
    return ch


WS = ws_chunks()
WSI = {n: i for i, (n, _, _) in enumerate(WS)}

ARENA_B = 210944
O_CF = 0
O_CB = 12288
O_RSTD = 22528
O_MODV = 27648
O_SCR = 30208
O_X = 36352
O_H = 118272
O_W = 159232


def build_program(L):
    nc = bass.Bass("TRN2", target_bir_lowering=False)
    cfo, NCF = cf_layout(L)
    cbo, NCB = cb_layout(L)
    assert NCF * 4 <= O_CB and NCB * 2 <= O_RSTD - O_CB, (NCF, NCB)

    def din(name, shape):
        return nc.dram_tensor(name, shape, F32, kind="ExternalInput").ap()

    def dout(name, shape):
        return nc.dram_tensor(name, shape, F32, kind="ExternalOutput").ap()

    xin = din("xin", [128, 16 * T])
    cf_d = din("cf", [128, NCF])
    cb_d = din("cb", [128, NCB])
    rope_d = din("rope", [128, 2 * T])
    h0g_d = din("h0g", [L, 2, 4, 128, 256])
    h0s_d = din("h0s", [L, 2, 128, 1024])
    ctxk_d = din("ctxk", [L, 2, 128, 512])
    ctxv_d = din("ctxv", [L, 2, 128, 512])
    wmod_d = din("wmod", [L, 144, 128, 2048])
    wgu_d = din("wgu", [L, 2, 88, 128, 2048])
    wd_d = din("wd", [L, 2, 22, 128, 4096])
    wws_d = din("wws", [L, len(WS), 128, 2048])
    wv_d = din("wv", [L, 5, 128, 4096])
    wdt_d = din("wdt", [L, 128, 512])
    wbr_d = din("wbr", [L, 48, 128, 1024])
    wout_d = din("wout", [L, 16, 128, 2048])
    y_d = dout("yT", [128, 16 * T])
    ok_d = dout("ok", [L, 2, 128, T])
    ov_d = dout("ov", [L, NT, 128, 256])
    og_d = dout("og", [L, 5, 2, 4, 128, 256])
    os_d = dout("os", [L, 5, 2, 128, 1024])
    xsp_d = dout("xspill", [128, 16 * T])

    st = contextlib.ExitStack()
    arena = st.enter_context(nc.sbuf_tensor("arena", [128, ARENA_B // 4], F32))
    ps = [st.enter_context(nc.psum_tensor("ps%d" % i, [128, 512], F32)) for i in range(8)]
    S = Sched(nc)
    out_toks = []

    def f32v(off, n):
        assert off % 4 == 0
        return arena[:, off // 4: off // 4 + n]

    def bfv(off, n):
        assert off % 4 == 0 and n % 2 == 0
        return arena[:, off // 4: off // 4 + n // 2].bitcast(BF16)

    def r3(ap, a):
        return ap.rearrange("p (a b) -> p a b", a=a)

    def act(out, in_, func, r, w, **kw):
        S.op('act', lambda e: e.activation(out=out, in_=in_, func=func, **kw), reads=r, writes=w)

    def tt(out, a, b, op, r, w, eng='dve'):
        S.op(eng, lambda e: e.tensor_tensor(out=out, in0=a, in1=b, op=op), reads=r, writes=w)

    def ts(out, a, s1, s2, op0, op1, r, w):
        if s2 is None:
            S.op('dve', lambda e: e.tensor_scalar(out=out, in0=a, scalar1=s1, scalar2=None, op0=op0), reads=r, writes=w)
        else:
            S.op('dve', lambda e: e.tensor_scalar(out=out, in0=a, scalar1=s1, scalar2=s2, op0=op0, op1=op1), reads=r, writes=w)

    def stt(out, a, sc, b, op0, op1, r, w):
        S.op('dve', lambda e: e.scalar_tensor_tensor(out=out, in0=a, scalar=sc, in1=b, op0=op0, op1=op1), reads=r, writes=w)

    def cp(out, in_, r, w, eng='dve'):
        if eng == 'act':
            S.op(eng, lambda e: e.activation(out=out, in_=in_, func=AF.Copy), reads=r, writes=w)
        else:
            S.op(eng, lambda e: e.tensor_copy(out=out, in_=in_), reads=r, writes=w)

    def mm(out, lhsT, rhs, start, stop, r, w):
        S.op('pe', lambda e: e.matmul(out, lhsT=lhsT, rhs=rhs, start=start, stop=stop), reads=r, writes=w)

    def dma(q, out, in_, r, w, sem):
        return S.op(q, lambda e: e.dma_start(out=out, in_=in_), reads=r, writes=w, dma=sem)

    def memset(ap, val, w, eng='pool'):
        S.op(eng, lambda e: e.memset(ap, val), writes=w)

    cf = f32v(O_CF, NCF)
    cb = bfv(O_CB, NCB)

    def CF(name, i=0, n=None):
        o, m = cfo[name]
        n = m if n is None else n
        return cf[:, o + i: o + i + n]

    def CB(name, i=0, n=None):
        o, m = cbo[name]
        n = m if n is None else n
        return cb[:, o + i: o + i + n]

    rstd = f32v(O_RSTD, T)
    modv = r3(f32v(O_MODV, 288), 144)
    Avec = f32v(O_MODV + 1152, 96)
    Gvec = f32v(O_MODV + 1152 + 384, 96)
    siluc = f32v(O_MODV + 1152 + 768, 32)
    sqb = [bfv(O_SCR + i * 1024, 512) for i in range(2)]
    tmpf = [f32v(O_SCR + 2048 + i * 2048, 512) for i in range(2)]
    xT = r3(f32v(O_X, 16 * T), 16)
    hT = r3(bfv(O_H, 16 * T), 16)
    ones_b = CB('ones')
    ident_b = CB('identb')
    ident_f = CF('ident')
    fs = CF('fs')
    ctxbias = CF('ctxbias')

    dma('sp', cf, cf_d, [], ['cf'], 'ld_cf')
    dma('pool', cb, cb_d, [], ['cb'], 'ld_cb')
    def xkeys(q):
        return [('x', kc, tb_) for kc in range(q * 4, q * 4 + 4) for tb_ in range(3)]
    xflat = f32v(O_X, 16 * T)
    for q in range(4):
        dma('sp', xflat[:, q * 4 * T:(q + 1) * 4 * T], xin[:, q * 4 * T:(q + 1) * 4 * T], [], xkeys(q), 'ld_x%d' % q)
    act(siluc, CF('cvec'), AF.Silu, ['cf'], ['siluc'])
    ts(CF('nbup'), CF('nbup'), -1.0, None, ALU.mult, None, ['cf'], ['cf'])

    psrr = [0]
    held = set()

    def nextps(hold=False):
        while True:
            b = psrr[0] % 8
            psrr[0] += 1
            if b not in held:
                break
        if hold:
            held.add(b)
        return b

    def release(*bs):
        for b in bs:
            held.discard(b)

    scb = bfv(O_CF + 12032, 32)
    scb3 = r3(scb, 16)
    cp(scb, siluc, ['siluc'], ['scb'])

    def mod_finish(l, p, b):
        bm = CF('bmod', l * 144 + p * 48, 48)
        tt(modv[:, p * 48:(p + 1) * 48, :], r3(ps[b][:, 0:96], 48), bm.unsqueeze(2).to_broadcast([128, 48, 2]), ALU.add,
           ['ps%d' % b, 'cf'], ['modv'])
        release(b)
        i = p
        nw = CF('normw', l * 48 + i * 16, 16)
        stt(r3(Avec[:, i * 32:(i + 1) * 32], 16), modv[:, (3 * i + 1) * 16:(3 * i + 2) * 16, :], 1.0,
            nw.unsqueeze(2).to_broadcast([128, 16, 2]), ALU.add, ALU.mult, ['modv', 'cf'], ['Avec'])
        ts(r3(Gvec[:, i * 32:(i + 1) * 32], 16), modv[:, (3 * i + 2) * 16:(3 * i + 3) * 16, :],
           1.0 if i == 1 else 0.5, None, ALU.mult, None, ['modv'], ['Gvec'])

    def mod_part_fast(l, p):
        NS = 6
        slots = [bfv(O_W + i * 4096, 2048) for i in range(NS)]
        b = nextps(hold=True)
        for j in range(48):
            m = p * 48 + j
            s = j % NS
            dma('pool', slots[s], wmod_d[l, m], [], [('wmm', s)], 'ld_mod%d' % s)
            w3 = r3(slots[s], 16)
            for kc in range(16):
                mm(ps[b][:, 2 * j:2 * j + 2], w3[:, kc, :], scb3[:, kc, :], kc == 0, kc == 15,
                   [('wmm', s), 'scb'], ['ps%d' % b])
        mod_finish(l, p, b)

    bg = {'pending': [], 'bank': None}

    def bg_start(l, p, slot_ap):
        bg.update(l=l, p=p, slot=slot_ap, bank=nextps(hold=True), pending=list(range(48)))

    def bg_step(n=1):
        for _ in range(n):
            if bg['bank'] is None or not bg['pending']:
                return
            j = bg['pending'].pop(0)
            m = bg['p'] * 48 + j
            b = bg['bank']
            dma('pool', bg['slot'], wmod_d[bg['l'], m], [], ['bgw'], 'ld_bgw')
            w3 = r3(bg['slot'], 16)
            for kc in range(16):
                mm(ps[b][:, 2 * j:2 * j + 2], w3[:, kc, :], scb3[:, kc, :], kc == 0, kc == 15,
                   ['bgw', 'scb'], ['ps%d' % b])

    def bg_flush():
        if bg['bank'] is None:
            return
        bg_step(48)
        b = bg['bank']
        bg['bank'] = None
        mod_finish(bg['l'], bg['p'], b)

    def Acol(i, kc, grp):
        return Avec[:, i * 32 + kc * 2 + grp: i * 32 + kc * 2 + grp + 1]

    def Bcol(i, kc, grp):
        return modv[:, 3 * i * 16 + kc, grp:grp + 1]

    def Gcol(i, kc, grp):
        return Gvec[:, i * 32 + kc * 2 + grp: i * 32 + kc * 2 + grp + 1]

    def rms_rstd(src_fn, nchunks, t0, tn, rkeys_fn, out_ap, wkey, dim):
        b = nextps()
        for kc in range(nchunks):
            sq = sqb[kc % 2]
            act(sq[:, :tn], src_fn(kc), AF.Square, rkeys_fn(kc), [('sq', kc % 2)])
            mm(ps[b][:, :tn], ones_b, sq[:, :tn], kc == 0, kc == nchunks - 1, [('sq', kc % 2), 'cb'], ['ps%d' % b])
        act(out_ap, ps[b][:, :tn], AF.Sqrt, ['ps%d' % b], [wkey], scale=1.0 / dim, bias=1e-6)
        S.op('dve', lambda e: e.reciprocal(out=out_ap, in_=out_ap), reads=[wkey], writes=[wkey])

    def norm_to_h(i):
        for tbi, (t0, tn) in enumerate(TB):
            grp = 0 if tbi < 2 else 1
            rms_rstd(lambda kc: xT[:, kc, t0:t0 + tn], 16, t0, tn, lambda kc: [('x', kc, tbi)],
                     rstd[:, t0:t0 + tn], ('rstd', tbi), D)
            for kc in range(16):
                tf = tmpf[kc % 2]
                stt(tf[:, :tn], xT[:, kc, t0:t0 + tn], Acol(i, kc, grp), rstd[:, t0:t0 + tn], ALU.mult, ALU.mult,
                    [('x', kc, tbi), ('rstd', tbi), 'Avec'], [('tmpf', kc % 2)])
                act(hT[:, kc, t0:t0 + tn], tf[:, :tn], AF.Identity, [('tmpf', kc % 2), 'modv'], [('h', kc, tbi)],
                    bias=Bcol(i, kc, grp), scale=1.0)

    def proj_ws(wslots, wkey, dram_tiles, KCn, rhs_fn, rhs_keys_fn, epi, M=128, tbs=None):
        cnt = proj_ws.cnt
        for i, dt_ in enumerate(dram_tiles):
            s = cnt[wkey] % len(wslots)
            cnt[wkey] += 1
            wt = wslots[s]
            dma('pool', wt[:, :KCn * 128], dt_, [], [(wkey, s)], 'ld_%s%d' % (wkey, s))
            w3 = r3(wt[:, :KCn * 128], KCn)
            for tbi, (t0, tn) in enumerate(TB if tbs is None else tbs):
                b = nextps()
                for kc in range(KCn):
                    mm(ps[b][:M, :tn], w3[:, kc, 0:M], rhs_fn(kc, t0, tn), kc == 0, kc == KCn - 1,
                       [(wkey, s)] + rhs_keys_fn(kc, tbi), ['ps%d' % b])
                epi(i, tbi, t0, tn, ps[b], 'ps%d' % b)
    import collections
    proj_ws.cnt = collections.defaultdict(int)

    def h_rhs(kc, t0, tn):
        return hT[:, kc, t0:t0 + tn]

    def h_keys(kc, tbi):
        return [('h', kc, tbi)]

    def ffn_phase(l, which):
        i_sub = 0 if which == 0 else 2
        norm_to_h(i_sub)
        gT = r3(bfv(O_W, 4 * T), 4)
        sg = [bfv(O_W + 10240 + i * 2560, T) for i in range(2)]
        wsl = [bfv(O_W + 15360 + i * 4096, 2048) for i in range(3)]
        wdsl = [bfv(O_W + 31744 + i * 8192, 4096) for i in range(2)]
        if which == 0:
            bg_start(l, 1, bfv(O_W + 27648, 2048))
        elif l + 1 < L:
            bg_start(l + 1, 0, bfv(O_W + 27648, 2048))
        for g in range(11):
            for j in range(4):
                hc = g * 4 + j

                def epi(i, tbi, t0, tn, p, pk, j=j):
                    if i == 0:
                        act(sg[j % 2][:, t0:t0 + tn], p[:, :tn], AF.Silu, [pk], [('sg', j % 2, tbi)])
                    else:
                        tt(gT[:, j, t0:t0 + tn], p[:, :tn], sg[j % 2][:, t0:t0 + tn], ALU.mult,
                           [pk, ('sg', j % 2, tbi)], [('g', j, tbi)])
                proj_ws(wsl, 'wf', [wgu_d[l, which, 2 * hc], wgu_d[l, which, 2 * hc + 1]], 16, h_rhs, h_keys, epi)
                bg_step(1)
            for half in range(2):
                s = proj_ws.cnt['wd'] % 2
                proj_ws.cnt['wd'] += 1
                dma('pool', wdsl[s], wd_d[l, which, g * 2 + half], [], [('wd', s)], 'ld_wd%d' % s)
                w4 = wdsl[s].rearrange("p (m k j) -> p m k j", m=8, k=4)
                for mi in range(8):
                    mc = half * 8 + mi
                    for tbi, (t0, tn) in enumerate(TB):
                        grp = 0 if tbi < 2 else 1
                        b = nextps()
                        for k in range(4):
                            mm(ps[b][:, :tn], w4[:, mi, k, :], gT[:, k, t0:t0 + tn], k == 0, k == 3,
                               [('wd', s), ('g', k, tbi)], ['ps%d' % b])
                        stt(xT[:, mc, t0:t0 + tn], ps[b][:, :tn], Gcol(i_sub, mc, grp), xT[:, mc, t0:t0 + tn],
                            ALU.mult, ALU.add, ['ps%d' % b, 'Gvec', ('x', mc, tbi)], [('x', mc, tbi)])
                bg_step(1)
        bg_flush()

    import os
    KMIX = int(os.environ.get('KMIX', '9'))
    KATT = int(os.environ.get('KATT', '99'))

    def mix_phase(l):
        norm_to_h(1)
        for q in range(4):
            dma('sp', xsp_d[:, q * 4 * T:(q + 1) * 4 * T], xflat[:, q * 4 * T:(q + 1) * 4 * T], xkeys(q), ['xsp%d' % q], 'st_x%d' % q)
        S.barrier()
        if KMIX == 0:
            for q in range(4):
                dma('sp', xflat[:, q * 4 * T:(q + 1) * 4 * T], xsp_d[:, q * 4 * T:(q + 1) * 4 * T], ['xsp%d' % q], xkeys(q), 'ld_x%d' % q)
            S.barrier()
            return
        oG = r3(bfv(O_X, 8 * T), 8)
        oS = r3(bfv(O_X + 20480, 8 * T), 8)
        oA = r3(bfv(O_X + 40960, 8 * T), 8)
        wsl = [bfv(O_X + 61440 + i * 4096, 2048) for i in range(3)]
        wasl = [bfv(O_X + 61440 + 12288, 4096)]

        def wtile(name):
            return wws_d[l, WSI[name]]

        W0 = O_W
        gdT = f32v(W0, T)
        gdb = bfv(W0 + 5120, T)
        qT = f32v(W0 + 7680, T)
        kT = f32v(W0 + 12800, T)
        rg = r3(bfv(W0 + 17920, 2 * T), 2)
        vtok = r3(bfv(W0 + 23040, NT * 256), NT)
        la = f32v(W0 + 28160, T)
        ex = f32v(W0 + 33280, T)
        qk = [bfv(W0 + 38400 + i * 2560, T) for i in range(4)]
        sc1 = O_X + 20480
        SIn = [r3(bfv(sc1 + d_ * 5120, NT * 256), NT) for d_ in range(2)]
        S32 = [f32v(sc1 + 10240 + d_ * 1024, 256) for d_ in range(2)]
        Stmps = [f32v(sc1 + 12288, 256), f32v(sc1 + 31744, 256)]
        ktok = [bfv(sc1 + 13312 + d_ * 256, 128) for d_ in range(2)]
        ABt = [bfv(sc1 + 13824 + i * 512, 256) for i in range(2)]
        o32 = r3(f32v(sc1 + 14848, 2 * T), 2)
        rs2 = f32v(sc1 + 25088, T)
        Ecol = f32v(sc1 + 30208, 32)
        cendb = f32v(sc1 + 30336, 16)

        def epi_gd(i, tbi, t0, tn, p, pk):
            cp(gdb[0:32, t0:t0 + tn], p[0:32, :tn], [pk], [('gdb', tbi)])
        proj_ws(wsl, 'wm', [wtile('gdown')], 16, h_rhs, h_keys, epi_gd, M=32)
        lnscale = math.log(128 ** -0.5)
        for h in range(4):
            def epi_q(i, tbi, t0, tn, p, pk):
                act(qT[:, t0:t0 + tn], p[:, :tn], AF.Copy, [pk], [('qT', tbi)])

            def epi_k(i, tbi, t0, tn, p, pk):
                act(kT[:, t0:t0 + tn], p[:, :tn], AF.Copy, [pk], [('kT', tbi)])

            def epi_r(i, tbi, t0, tn, p, pk):
                act(rg[:, i, t0:t0 + tn], p[:, :tn], AF.Silu, [pk], [('rg', i, tbi)])
            proj_ws(wsl, 'wm', [wtile('gq%d' % h)], 16, h_rhs, h_keys, epi_q)
            proj_ws(wsl, 'wm', [wtile('gk%d' % h)], 16, h_rhs, h_keys, epi_k)
            proj_ws(wsl, 'wm', [wtile('gr%d_0' % h), wtile('gr%d_1' % h)], 16, h_rhs, h_keys, epi_r)
            dma('pool', wasl[0], wv_d[l, h], [], [('wa', 0)], 'ld_wa0')
            wv3 = r3(wasl[0], 16)
            for tt_ in range(NT):
                b = nextps()
                for kc in range(16):
                    mm(ps[b][:, 0:256], hT[:, kc, tt_ * 128:(tt_ + 1) * 128], wv3[:, kc, :], kc == 0, kc == 15,
                       [('wa', 0), ('h', kc, tt_ // 4)], ['ps%d' % b])
                cp(vtok[:, tt_, :], ps[b][:, 0:256], ['ps%d' % b], [('vtok', tt_)])
            for d_ in range(2):
                wup = CB('wup', l * 1024 + d_ * 512 + h * 128, 128)
                nb = CF('nbup', l * 8 + d_ * 4 + h, 1)
                for tbi, (t0, tn) in enumerate(TB):
                    b = nextps()
                    mm(ps[b][:, :tn], wup[0:32, :], gdb[0:32, t0:t0 + tn], True, True, ['cb', ('gdb', tbi)], ['ps%d' % b])
                    act(ex[:, t0:t0 + tn], ps[b][:, :tn], AF.Exp, ['ps%d' % b, 'cf'], ['ex'], scale=-1.0, bias=nb)
                    act(ex[:, t0:t0 + tn], ex[:, t0:t0 + tn], AF.Ln, ['ex'], ['ex'], bias=1.0, scale=1.0)
                ts(la, ex, -1.0 / 16.0, None, ALU.mult, None, ['ex'], ['la'])
                S.op('dve', lambda e: e.tensor_tensor_scan(out=ex, data0=CF('cummask'), data1=la, initial=0.0,
                                                           op0=ALU.mult, op1=ALU.add),
                     reads=['la', 'cf'], writes=['ex'])
                la3 = r3(la, NT)
                ex3 = r3(ex, NT)
                if d_ == 1:
                    tt(la3, la3, ex3, ALU.subtract, ['la', 'ex'], ['la'])
                    cp(cendb[:, 0:NT], ex3[:, :, 127], ['ex'], ['cendb'])
                    tt(ex3, la3, cendb[:, 0:NT].unsqueeze(2).to_broadcast([128, NT, 128]), ALU.add, ['la', 'ex', 'cendb'], ['ex'])
                    ckey = 'ex'
                    ecol = ex3[:, :, 0]
                else:
                    ckey = 'ex'
                    ecol = ex3[:, :, 127]
                act(Ecol[:, d_ * 16:d_ * 16 + NT], ecol, AF.Exp, [ckey], [('Ecol', d_)])
                act(la, ex, AF.Exp, [ckey], ['la'], bias=lnscale, scale=1.0)
                tt(qk[2 * d_], qT, la, ALU.mult, ['la', ('qT', 0), ('qT', 1), ('qT', 2)], [('qk', 2 * d_)])
                act(la, ex, AF.Exp, [ckey, ('qk', 2 * d_)], ['la'], scale=-1.0)
                tt(qk[2 * d_ + 1], kT, la, ALU.mult, ['la', ('kT', 0), ('kT', 1), ('kT', 2)], [('qk', 2 * d_ + 1)])
            for step in range(NT):
              for d_ in range(2):
                    ti = step if d_ == 0 else NT - 1 - step
                    kt_ = qk[2 * d_ + 1]
                    Stmp = Stmps[d_]
                    slot = ti // 2
                    first = (ti % 2 == 0) if d_ == 0 else (ti % 2 == 1)
                    if first:
                        chain = (slot in (1, 2, 3)) if d_ == 0 else (slot in (0, 1, 2))
                        has_h0 = (slot == 0) if d_ == 0 else (slot == 3)
                        if has_h0:
                            dma('sp', S32[d_], h0g_d[l, d_, h], [], [('S32', d_)], 'ld_h0%d' % d_)
                        elif chain:
                            ts(S32[d_], S32[d_], fs, None, ALU.mult, None, [('S32', d_), 'cf'], [('S32', d_)])
                        else:
                            memset(S32[d_], 0.0, [('S32', d_)], eng='dve')
                    cp(SIn[d_][:, ti, :], S32[d_], [('S32', d_)], [('SIn', d_, ti)], eng='act')
                    b = nextps()
                    mm(ps[b][:, 0:128], kt_[:, ti * 128:(ti + 1) * 128], ident_b, True, True, [('qk', 2 * d_ + 1), 'cb'], ['ps%d' % b])
                    cp(ktok[d_], ps[b][:, 0:128], ['ps%d' % b], [('ktok', d_)])
                    b2 = nextps()
                    mm(ps[b2][:, 0:256], ktok[d_], vtok[:, ti, :], True, True, [('ktok', d_), ('vtok', ti)], ['ps%d' % b2])
                    tt(Stmp, ps[b2][:, 0:256], S32[d_], ALU.add, ['ps%d' % b2, ('S32', d_)], [('Stmp', d_)])
                    ts(S32[d_], Stmp, Ecol[:, d_ * 16 + ti:d_ * 16 + ti + 1], None, ALU.mult, None,
                       [('Stmp', d_), ('Ecol', d_)], [('S32', d_)])
                    last = (ti % 2 == 1) if d_ == 0 else (ti % 2 == 0)
                    if last:
                        out_toks.append(dma('sp', og_d[l, slot, d_, h], S32[d_], [('S32', d_)], [], 'st_og%d' % d_))
            for tbi, (t0, tn) in enumerate(TB):
                ntl = tn // 128
                bo = [nextps(hold=True), nextps(hold=True)]
                for tq in range(ntl):
                    ti = t0 // 128 + tq
                    sl = slice(ti * 128, (ti + 1) * 128)
                    b = nextps()
                    mm(ps[b][:, 0:128], qk[1][:, sl], qk[0][:, sl], True, True, [('qk', 0), ('qk', 1)], ['ps%d' % b])
                    mm(ps[b][:, 128:256], qk[3][:, sl], qk[2][:, sl], True, True, [('qk', 2), ('qk', 3)], ['ps%d' % b])
                    AB = ABt[ti % 2]
                    tt(AB, ps[b][:, 0:256], CB('maskFB'), ALU.mult, ['ps%d' % b, 'cb'], [('AB', ti % 2)])
                    for vc in range(2):
                        o = ps[bo[vc]][:, tq * 128:(tq + 1) * 128]
                        vs = vtok[:, ti, vc * 128:(vc + 1) * 128]
                        mm(o, vs, AB[:, 0:128], True, False, [('vtok', ti), ('AB', ti % 2)], ['ps%d' % bo[vc]])
                        mm(o, vs, AB[:, 128:256], False, False, [('vtok', ti), ('AB', ti % 2)], ['ps%d' % bo[vc]])
                        mm(o, SIn[0][:, ti, vc * 128:(vc + 1) * 128], qk[0][:, sl], False, False,
                           [('SIn', 0, ti), ('qk', 0)], ['ps%d' % bo[vc]])
                        mm(o, SIn[1][:, ti, vc * 128:(vc + 1) * 128], qk[2][:, sl], False, True,
                           [('SIn', 1, ti), ('qk', 2)], ['ps%d' % bo[vc]])
                for vc in range(2):
                    act(o32[:, vc, t0:t0 + tn], ps[bo[vc]][:, :tn], AF.Copy, ['ps%d' % bo[vc]], [('o32', vc, tbi)])
                release(*bo)
                rms_rstd(lambda vc: o32[:, vc, t0:t0 + tn], 2, t0, tn, lambda vc: [('o32', vc, tbi)],
                         rs2[:, t0:t0 + tn], ('rs2', tbi), 256)
                for vc in range(2):
                    stt(o32[:, vc, t0:t0 + tn], o32[:, vc, t0:t0 + tn], CF('gnorm', l * 2 + vc, 1), rs2[:, t0:t0 + tn],
                        ALU.mult, ALU.mult, [('o32', vc, tbi), ('rs2', tbi), 'cf'], [('o32', vc, tbi)])
                    tt(oG[:, h * 2 + vc, t0:t0 + tn], o32[:, vc, t0:t0 + tn], rg[:, vc, t0:t0 + tn], ALU.mult,
                       [('o32', vc, tbi), ('rg', vc, tbi)], [('oG', h * 2 + vc, tbi)])
        S.barrier()
        if KMIX == 1:
            return

        dtt = r3(f32v(W0, NT * 32), NT)
        att = r3(f32v(W0 + 1280, NT * 32), NT)
        Abc = f32v(W0 + 2560, 32)
        abc = r3(f32v(W0 + 2688, 1024), 8)
        BtA = r3(bfv(W0 + 6784, NT * 128), NT)
        ahi = r3(bfv(W0 + 2688, 1024), 8)
        alo = r3(bfv(W0 + 2688 + 2048, 1024), 8)
        ntb = [bfv(W0 + 9344 + i * 256, 128) for i in range(4)]
        XtA = r3(bfv(O_RSTD, NT * 256), NT)
        xpad = r3(f32v(W0 + 10880, 5 * 260), 5)
        cacc = r3(f32v(W0 + 16080, T), 5)
        xc = r3(f32v(W0 + 21200, 2 * T), 2)
        BT = bfv(W0 + 31440, T)
        CT = bfv(W0 + 34000, T)
        zg = r3(bfv(W0 + 36560, 2 * T), 2)
        ssacc = f32v(W0 + 41680, T)
        cumt = f32v(W0 + 46800, 32)
        wdec = f32v(W0 + 46928, 8)
        cend = f32v(W0 + 46960, 8)
        Eend = f32v(W0 + 46992, 8)
        sc2 = O_X + 40960
        xtok = f32v(sc2, 256)
        Btok = bfv(sc2 + 1024, 128)
        xdt = [bfv(sc2 + 1280 + i * 512, 256) for i in range(4)]
        segb = [bfv(sc2 + 3328 + i * 1024, 512) for i in range(2)]
        CBm = [bfv(sc2 + 5376 + i * 256, 128) for i in range(2)]
        Cdec = [bfv(sc2 + 5888 + i * 1024, 512) for i in range(2)]
        SsIn = [r3(bfv(sc2 + 7936 + d_ * 5120, NT * 256), NT) for d_ in range(2)]
        Ss32 = [f32v(sc2 + 18176 + d_ * 1024, 256) for d_ in range(2)]
        wdtb = wasl[0]
        dma('pool', wdtb[:, 0:512], wdt_d[l], [], [('wa', 0)], 'ld_wa0')
        wdt3 = r3(wdtb[:, 0:512], 16)
        act(Abc, CF('alog', l * 32, 32), AF.Exp, ['cf'], ['Abc'])
        memset(ssacc, 0.0, ['ssacc'], eng='dve')
        for i_, nm_ in enumerate(('ntriU', 'ntriL', 'mnegF', 'mnegB')):
            cp(ntb[i_], CF(nm_), ['cf'], ['ntb'])
        for ti in range(NT):
            b = nextps()
            for kc in range(16):
                mm(ps[b][:, 0:32], hT[:, kc, ti * 128:(ti + 1) * 128], wdt3[:, kc, :], kc == 0, kc == 15,
                   [('wa', 0), ('h', kc, ti // 4)], ['ps%d' % b])
            tt(dtt[:, ti, :], ps[b][:, 0:32], CF('dtbias', l * 32, 32), ALU.add, ['ps%d' % b, 'cf'], [('dtt', ti)])
            act(dtt[:, ti, :], dtt[:, ti, :], AF.Exp, [('dtt', ti)], [('dtt', ti)])
            act(dtt[:, ti, :], dtt[:, ti, :], AF.Ln, [('dtt', ti)], [('dtt', ti)], bias=1.0, scale=1.0)
            stt(att[:, ti, :], dtt[:, ti, :], -1.0, Abc, ALU.mult, ALU.mult, [('dtt', ti), 'Abc'], [('att', ti)])
        memset(r3(f32v(W0 + 10880, 5 * 260), 5), 0.0, ['xpad'], eng='dve')
        for g in range(4):
            names = ['sx%d_0' % g, 'sx%d_1' % g, 'sB%d' % g, 'sC%d' % g]
            for ci, nm in enumerate(names):
                chan = [g * 2, g * 2 + 1, 8 + g, 12 + g][ci]

                def epi_c(i, tbi, t0, tn, p, pk):
                    for s_ in range(tn // 256):
                        slot = t0 // 256 + s_
                        act(xpad[:, slot, 2:258], p[:, s_ * 256:(s_ + 1) * 256], AF.Copy, [pk, 'xpad'], [('xpad', slot)])
                proj_ws(wsl, 'wm', [wtile(nm)], 16, h_rhs, h_keys, epi_c)
                allx = [('xpad', s_) for s_ in range(5)]
                ts(xpad[:, 1:4, 0:2], xpad[:, 0:3, 256:258], fs, None, ALU.mult, None, allx + ['cf'], ['xhalo'])
                ts(xpad[:, 0:3, 258:260], xpad[:, 1:4, 2:4], fs, None, ALU.mult, None, allx + ['cf', 'xhalo'], ['xhalo2'])
                cw = lambda j: CF('convw', l * 80 + chan * 5 + j, 1)
                ts(cacc, xpad[:, :, 0:256], cw(0), CF('convb', l * 16 + chan, 1), ALU.mult, ALU.add,
                   allx + ['xhalo', 'xhalo2', 'cf'], ['cacc'])
                for j in range(1, 5):
                    stt(cacc, xpad[:, :, j:j + 256], cw(j), cacc, ALU.mult, ALU.add, allx + ['xhalo', 'xhalo2', 'cacc'],
                        ['cacc'] + (allx if j == 4 else []))
                dst = [xc[:, 0, :], xc[:, 1, :], BT, CT][ci]
                act(dst, cacc.rearrange("p a b -> p (a b)"), AF.Silu, ['cacc'], [('cv', ci)])
            def epi_z(i, tbi, t0, tn, p, pk):
                act(zg[:, i, t0:t0 + tn], p[:, :tn], AF.Silu, [pk], [('zg', i, tbi)])
            proj_ws(wsl, 'wm', [wtile('sz%d_0' % g), wtile('sz%d_1' % g)], 16, h_rhs, h_keys, epi_z)
            for ti in range(NT):
                sl = slice(ti * 128, (ti + 1) * 128)
                b2 = nextps()
                for c2 in range(2):
                    mm(ps[b2][:, c2 * 128:(c2 + 1) * 128], xc[:, c2, sl], ident_f, True, True, [('cv', c2), 'cf'], ['ps%d' % b2])
                mm(ps[b2][:, 256:384], BT[:, sl], ident_b, True, True, [('cv', 2), 'cb'], ['ps%d' % b2])
                cp(XtA[:, ti, :], ps[b2][:, 0:256], ['ps%d' % b2], [('XtA', ti)], eng='act')
                cp(BtA[:, ti, :], ps[b2][:, 256:384], ['ps%d' % b2], [('BtA', ti)], eng='act')
            for step in range(NT):
                for d_ in range(2):
                    ti = step if d_ == 0 else NT - 1 - step
                    tri = CF('triU') if d_ == 0 else CF('triL')
                    slot = ti // 2
                    first = (ti % 2 == 0) if d_ == 0 else (ti % 2 == 1)
                    cu = cumt[:, d_ * 8:d_ * 8 + 8]
                    wd_ = wdec[:, d_ * 4:d_ * 4 + 4]
                    Ee = Eend[:, d_ * 4:d_ * 4 + 4]
                    if first:
                        chain = (slot in (1, 2, 3)) if d_ == 0 else (slot in (0, 1, 2))
                        has_h0 = (slot == 0) if d_ == 0 else (slot == 3)
                        if has_h0:
                            dma('sp', Ss32[d_], h0s_d[l, d_, :, g * 256:(g + 1) * 256], [], [('Ss32', d_)], 'ld_h0%d' % d_)
                        elif chain:
                            ts(Ss32[d_], Ss32[d_], fs, None, ALU.mult, None, [('Ss32', d_), 'cf'], [('Ss32', d_)])
                        else:
                            memset(Ss32[d_], 0.0, [('Ss32', d_)], eng='dve')
                    cp(SsIn[d_][:, ti, :], Ss32[d_], [('Ss32', d_)], [('SsIn', d_, ti)], eng='act')
                    a4 = att[:, ti, d_ * 16 + g * 4: d_ * 16 + g * 4 + 4]
                    b = nextps()
                    mm(ps[b][:, 0:4], tri, a4, True, True, ['cf', ('att', ti)], ['ps%d' % b])
                    mm(ps[b][:, 4:8], ones_f, a4, True, True, ['onesf', ('att', ti)], ['ps%d' % b])
                    cp(cu, ps[b][:, 0:8], ['ps%d' % b], [('cumt', d_)])
                    tt(wd_, cu[:, 4:8], cu[:, 0:4], ALU.subtract, [('cumt', d_)], [('wdec', d_)])
                    act(wd_, wd_, AF.Exp, [('wdec', d_)], [('wdec', d_)])
                    tt(wd_, wd_, dtt[:, ti, d_ * 16 + g * 4: d_ * 16 + g * 4 + 4], ALU.mult,
                       [('wdec', d_), ('dtt', ti)], [('wdec', d_)])
                    act(Ee, cu[:, 4:8], AF.Exp, [('cumt', d_)], [('Eend', d_)])
                    xd = xdt[2 + d_]
                    tt(r3(xd, 4), r3(XtA[:, ti, :], 4), wd_.unsqueeze(2).to_broadcast([128, 4, 64]), ALU.mult,
                       [('XtA', ti), ('wdec', d_)], [('xd', d_)])
                    b3 = nextps()
                    mm(ps[b3][:, 0:256], BtA[:, ti, :], xd, True, True, [('BtA', ti), ('xd', d_)], ['ps%d' % b3])
                    tt(r3(Ss32[d_], 4), r3(Ss32[d_], 4), Ee.unsqueeze(2).to_broadcast([128, 4, 64]), ALU.mult,
                       [('Ss32', d_), ('Eend', d_)], [('Ss32', d_)])
                    tt(Ss32[d_], Ss32[d_], ps[b3][:, 0:256], ALU.add, [('Ss32', d_), 'ps%d' % b3], [('Ss32', d_)])
                    last = (ti % 2 == 1) if d_ == 0 else (ti % 2 == 0)
                    if last:
                        out_toks.append(dma('sp', os_d[l, slot, d_, :, g * 256:(g + 1) * 256], Ss32[d_], [('Ss32', d_)], [], 'st_os%d' % d_))
            for tbi, (t0, tn) in enumerate(TB):
                ntl = tn // 128
                by = [nextps(hold=True), nextps(hold=True)]
                for tq in range(ntl):
                    ti = t0 // 128 + tq
                    sl = slice(ti * 128, (ti + 1) * 128)
                    for d_ in range(2):
                        tt(r3(xdt[d_], 4), r3(XtA[:, ti, :], 4),
                           dtt[:, ti, d_ * 16 + g * 4: d_ * 16 + g * 4 + 4].unsqueeze(2).to_broadcast([128, 4, 64]), ALU.mult,
                           [('XtA', ti), ('dtt', ti)], [('xdt', d_)])
                    bc = nextps()
                    mm(ps[bc][:, 0:128], BT[:, sl], CT[:, sl], True, True, [('cv', 2), ('cv', 3)], ['ps%d' % bc])
                    cp(CBm[ti % 2], ps[bc][:, 0:128], ['ps%d' % bc], [('CBm', ti % 2)], eng='act')
                    for d_ in range(2):
                        a4 = att[:, ti, d_ * 16 + g * 4: d_ * 16 + g * 4 + 4]
                        tri = CF('triU') if d_ == 0 else CF('triL')
                        ntri = CF('ntriU') if d_ == 0 else CF('ntriL')
                        mneg = CF('mnegF') if d_ == 0 else CF('mnegB')
                        a4b = a4.unsqueeze(2).to_broadcast([128, 4, 128])
                        cp(ahi[:, d_ * 4:d_ * 4 + 4, :], a4b, [('att', ti)], [('ahi', d_)])
                        tt(alo[:, d_ * 4:d_ * 4 + 4, :], a4b, ahi[:, d_ * 4:d_ * 4 + 4, :], ALU.subtract,
                           [('att', ti), ('ahi', d_)], [('alo', d_)])
                        triB = CB('maskFB', d_ * 128, 128)
                        ntriB = ntb[d_]
                        mnegB_ = ntb[2 + d_]
                        bd = nextps()
                        be = nextps()
                        for hh in range(4):
                            o = ps[bd][:, hh * 128:(hh + 1) * 128]
                            e_ = ps[be][:, hh * 128:(hh + 1) * 128]
                            h_ = ahi[:, d_ * 4 + hh, :]
                            l_ = alo[:, d_ * 4 + hh, :]
                            mm(o, h_, triB, True, False, [('ahi', d_), 'cb'], ['ps%d' % bd])
                            mm(o, l_, triB, False, False, [('alo', d_), 'cb'], ['ps%d' % bd])
                            mm(o, ntriB, h_, False, False, [('ahi', d_), 'ntb'], ['ps%d' % bd])
                            mm(o, ntriB, l_, False, False, [('alo', d_), 'ntb'], ['ps%d' % bd])
                            mm(o, ident_b, mnegB_, False, True, ['cb', 'ntb'], ['ps%d' % bd])
                            mm(e_, h_, triB, True, False, [('ahi', d_), 'cb'], ['ps%d' % be])
                            mm(e_, l_, triB, False, True, [('alo', d_), 'cb'], ['ps%d' % be])
                        sgb = segb[d_]
                        act(sgb, ps[bd][:, 0:512], AF.Exp, ['ps%d' % bd], [('seg', d_)])
                        tt(r3(sgb, 4), r3(sgb, 4), CBm[ti % 2].unsqueeze(1).to_broadcast([128, 4, 128]), ALU.mult,
                           [('seg', d_), ('CBm', ti % 2)], [('seg', d_)])
                        cd = Cdec[d_]
                        act(cd, ps[be][:, 0:512], AF.Exp, ['ps%d' % be], [('Cdec', d_)])
                        tt(r3(cd, 4), r3(cd, 4), CT[:, sl].unsqueeze(1).to_broadcast([128, 4, 128]), ALU.mult,
                           [('Cdec', d_), ('cv', 3)], [('Cdec', d_)])
                    for hh in range(4):
                        o = ps[by[hh // 2]][(hh % 2) * 64:(hh % 2) * 64 + 64, tq * 128:(tq + 1) * 128]
                        for d_ in range(2):
                            mm(o, xdt[d_][:, hh * 64:(hh + 1) * 64], segb[d_][:, hh * 128:(hh + 1) * 128], d_ == 0, False,
                               [('xdt', d_), ('seg', d_)], ['ps%d' % by[hh // 2]])
                        for d_ in range(2):
                            mm(o, SsIn[d_][:, ti, hh * 64:(hh + 1) * 64], Cdec[d_][:, hh * 128:(hh + 1) * 128], False, d_ == 1,
                               [('SsIn', d_, ti), ('Cdec', d_)], ['ps%d' % by[hh // 2]])
                for c2 in range(2):
                    ch = g * 2 + c2
                    stt(tmpf[c2][:, :tn], xc[:, c2, t0:t0 + tn], CF('ssmD', l * 8 + ch, 1), ps[by[c2]][:, :tn], ALU.mult, ALU.add,
                        [('cv', c2), 'cf', 'ps%d' % by[c2]], [('tmpf', c2)])
                    tt(oS[:, ch, t0:t0 + tn], tmpf[c2][:, :tn], zg[:, c2, t0:t0 + tn], ALU.mult,
                       [('tmpf', c2), ('zg', c2, tbi)], [('oS', ch, tbi)])
                release(*by)
                bq = nextps()
                for c2 in range(2):
                    ch = g * 2 + c2
                    act(sqb[c2][:, :tn], oS[:, ch, t0:t0 + tn], AF.Square, [('oS', ch, tbi)], [('sq', c2)])
                    mm(ps[bq][:, :tn], ones_b, sqb[c2][:, :tn], c2 == 0, c2 == 1, [('sq', c2), 'cb'], ['ps%d' % bq])
                tt(ssacc[:, t0:t0 + tn], ssacc[:, t0:t0 + tn], ps[bq][:, :tn], ALU.add, ['ps%d' % bq, 'ssacc'], ['ssacc'])
        act(ssacc, ssacc, AF.Sqrt, ['ssacc'], ['ssacc'], scale=1.0 / 1024.0, bias=1e-6)
        S.op('dve', lambda e: e.reciprocal(out=ssacc, in_=ssacc), reads=['ssacc'], writes=['ssacc'])
        for ch in range(8):
            for tbi, (t0, tn) in enumerate(TB):
                stt(oS[:, ch, t0:t0 + tn], oS[:, ch, t0:t0 + tn], CF('ssmnorm', l * 8 + ch, 1), ssacc[:, t0:t0 + tn],
                    ALU.mult, ALU.mult, [('oS', ch, tbi), 'ssacc', 'cf'], [('oS', ch, tbi)])
        S.barrier()
        if KMIX == 2:
            return

        cosT = f32v(W0, T)
        sinT = f32v(W0 + 5120, T)
        qr = r3(bfv(W0 + 10240, 4 * T), 4)
        krT = bfv(W0 + 20480, T)
        q32 = f32v(W0 + 23040, T)
        qb16 = bfv(W0 + 28160, T)
        vtk = r3(bfv(W0 + 30720, NT * 256), NT)
        v32 = [f32v(W0 + 35840 + i * 1024, 256) for i in range(2)]
        cK = bfv(W0 + 37888, 512)
        cV = r3(bfv(W0 + 38912, 512), 4)
        PT = [bfv(W0 + 39936 + i * 1024, 512) for i in range(3)]
        k32 = f32v(W0 + 43008, T)
        rden = f32v(W0 + 48128, 512)
        sinkE = f32v(W0 + 50176, 8)
        dma('sp', f32v(W0, 2 * T), rope_d, [], ['rope'], 'ld_rope')
        act(sinkE, CF('sink', l * 8, 8), AF.Exp, ['cf'], ['sinkE'])
        if KATT == 0:
            S.barrier()
            return
        dma('pool', wasl[0], wv_d[l, 4], [], [('wa', 0)], 'ld_wa0')
        wv3 = r3(wasl[0], 16)
        for ti in range(NT):
            b = nextps()
            for kc in range(16):
                mm(ps[b][:, 0:256], hT[:, kc, ti * 128:(ti + 1) * 128], wv3[:, kc, :], kc == 0, kc == 15,
                   [('wa', 0), ('h', kc, ti // 4)], ['ps%d' % b])
            act(v32[ti % 2], ps[b][:, 0:256], AF.Copy, ['ps%d' % b], [('v32', ti % 2)])
            cp(vtk[:, ti, :], v32[ti % 2], [('v32', ti % 2)], [('vtk', ti)])
            if os.environ.get('KNOV') != '1':
                out_toks.append(dma('sp', ov_d[l, ti], v32[ti % 2], [('v32', ti % 2)], [], 'st_ov%d' % (ti % 2)))
        scale = 128 ** -0.5
        if KATT == 1:
            S.barrier()
            return

        def rope(dst, srckeys_w):
            for tbi, (t0, tn) in enumerate(TB):
                b = nextps()
                mm(ps[b][:, :tn], CB('pm'), qb16[:, t0:t0 + tn], True, True, ['cb', ('qb16', tbi)], ['ps%d' % b])
                tt(tmpf[0][:, :tn], q32[:, t0:t0 + tn], cosT[:, t0:t0 + tn], ALU.mult, [('q32', tbi), 'rope'], [('tmpf', 0)])
                tt(tmpf[1][:, :tn], ps[b][:, :tn], sinT[:, t0:t0 + tn], ALU.mult, ['ps%d' % b, 'rope'], [('tmpf', 1)])
                tt(dst[:, t0:t0 + tn], tmpf[0][:, :tn], tmpf[1][:, :tn], ALU.add, [('tmpf', 0), ('tmpf', 1)], [(srckeys_w, tbi)])

        for kv in range(2):
            def epi_qk(i, tbi, t0, tn, p, pk):
                act(q32[:, t0:t0 + tn], p[:, :tn], AF.Copy, [pk], [('q32', tbi)])
                cp(qb16[:, t0:t0 + tn], q32[:, t0:t0 + tn], [('q32', tbi)], [('qb16', tbi)])
            for j in range(4):
                proj_ws(wsl, 'wm', [wtile('aq%d' % (kv * 4 + j))], 16, h_rhs, h_keys, epi_qk)
                rope(qr[:, j, :], ('qr', j))

            def epi_kk(i, tbi, t0, tn, p, pk):
                act(q32[:, t0:t0 + tn], p[:, :tn], AF.Copy, [pk], [('q32', tbi)])
                act(k32[:, t0:t0 + tn], p[:, :tn], AF.Copy, [pk], [('k32', tbi)])
                cp(qb16[:, t0:t0 + tn], q32[:, t0:t0 + tn], [('q32', tbi)], [('qb16', tbi)])
            proj_ws(wsl, 'wm', [wtile('ak%d' % kv)], 16, h_rhs, h_keys, epi_kk)
            out_toks.append(dma('sp', ok_d[l, kv], k32, [('k32', 0), ('k32', 1), ('k32', 2)], [], 'st_ok'))
            rope(krT, 'krT')
            if KATT == 2:
                S.barrier()
                return
            dma('pool', cK, ctxk_d[l, kv], [], ['cK'], 'ld_cK')
            dma('pool', cV.rearrange("p a b -> p (a b)"), ctxv_d[l, kv], [], ['cV'], 'ld_cV')
            krkeys = [('krT', 0), ('krT', 1), ('krT', 2)]
            if KATT == 3:
                S.barrier()
                return
            for ti in range(NT):
                if KATT == 4 and ti == 1:
                    S.barrier()
                    return
                sl = slice(ti * 128, (ti + 1) * 128)
                qrhs = qr[:, :, sl]
                qkeys = [(('qr', j), ti // 4) for j in range(4)]
                bo = nextps(hold=True)
                bden = nextps(hold=True)
                chunks = [('c', c) for c in range(4)] if ti < 8 else []
                if ti > 0:
                    chunks.append(('l', ti - 1))
                chunks.append(('l', ti))
                if ti < NT - 1:
                    chunks.append(('l', ti + 1))
                for ci, (kind, c) in enumerate(chunks):
                    bs = nextps()
                    P = PT[ci % 3]
                    if kind == 'c':
                        mm(ps[bs][:, 0:512], cK[:, c * 128:(c + 1) * 128], qrhs, True, True, ['cK'] + qkeys, ['ps%d' % bs])
                        act(P, ps[bs][:, 0:512], AF.Exp, ['ps%d' % bs, 'cf'], [('PT', ci % 3)], scale=scale, bias=ctxbias)
                        vl = cV[:, c, :]
                        vkeys = ['cV']
                    else:
                        mm(ps[bs][:, 0:512], krT[:, c * 128:(c + 1) * 128], qrhs, True, True, krkeys + qkeys, ['ps%d' % bs])
                        act(P, ps[bs][:, 0:512], AF.Exp, ['ps%d' % bs], [('PT', ci % 3)], scale=scale)
                        if c != ti:
                            mi = (ti - 1) if c < ti else (9 + ti)
                            mk = CB('amask', mi * 128, 128)
                            tt(r3(P, 4), r3(P, 4), mk.unsqueeze(1).to_broadcast([128, 4, 128]), ALU.mult,
                               [('PT', ci % 3), 'cb'], [('PT', ci % 3)])
                        vl = vtk[:, c, kv * 128:(kv + 1) * 128]
                        vkeys = [('vtk', c)]
                    first = ci == 0
                    lastc = ci == len(chunks) - 1
                    mm(ps[bo][:, 0:512], vl, P, first, lastc, vkeys + [('PT', ci % 3)], ['ps%d' % bo])
                    mm(ps[bden][:, 0:512], ones_b, P, first, lastc, ['cb', ('PT', ci % 3)], ['ps%d' % bden])
                tt(r3(rden, 4), r3(ps[bden][:, 0:512], 4),
                   sinkE[:, kv * 4:kv * 4 + 4].unsqueeze(2).to_broadcast([128, 4, 128]), ALU.add, ['ps%d' % bden, 'sinkE'], ['rden'])
                S.op('dve', lambda e: e.reciprocal(out=rden, in_=rden), reads=['rden'], writes=['rden'])
                tt(oA[:, kv * 4:kv * 4 + 4, sl], r3(ps[bo][:, 0:512], 4), r3(rden, 4), ALU.mult, ['ps%d' % bo, 'rden'],
                   [('oAt', kv, ti)])
                release(bo, bden)
        S.barrier()
        if KMIX == 3:
            return

        mT = r3(bfv(O_W, 16 * T), 16)
        gsl = [bfv(O_W + 40960 + i * 4096, 2048) for i in range(2)]
        bsl = [bfv(O_X + 61440 + i * 2048, 1024) for i in range(4)]
        gat = [f32v(O_X + 61440 + 8192 + i * 2048, 512) for i in range(3)]
        osrc = [oG, oS, oA]
        mt2 = f32v(O_SCR, 512)
        cnt = proj_ws.cnt
        bg_start(l, 2, bfv(O_X + 75776, 2048))
        for c in range(16):
            for b_ in range(3):
                bg_step(1)
                s = cnt['wgl'] % 2
                cnt['wgl'] += 1
                dma('pool', gsl[s], wws_d[l, WSI['br%d_%d' % (b_, c)]], [], [('wg', s)], 'ld_wg%d' % s)
                g3 = r3(gsl[s], 16)
                s2 = cnt['wbl'] % 4
                cnt['wbl'] += 1
                dma('pool', bsl[s2], wbr_d[l, b_ * 16 + c], [], [('wb', s2)], 'ld_wb%d' % s2)
                b3 = r3(bsl[s2], 8)
                for tbi, (t0, tn) in enumerate(TB):
                    bg = nextps()
                    for kc in range(16):
                        mm(ps[bg][:, :tn], g3[:, kc, :], hT[:, kc, t0:t0 + tn], kc == 0, kc == 15,
                           [('wg', s), ('h', kc, tbi)], ['ps%d' % bg])
                    act(gat[tbi][:, :tn], ps[bg][:, :tn], AF.Sigmoid, ['ps%d' % bg], [('gat', tbi)])
                    bp = nextps()
                    for kc in range(8):
                        mm(ps[bp][:, :tn], b3[:, kc, :], osrc[b_][:, kc, t0:t0 + tn], kc == 0, kc == 7,
                           [('wb', s2)], ['ps%d' % bp])
                    if b_ == 0:
                        tt(macc[tbi][:, :tn], ps[bp][:, :tn], gat[tbi][:, :tn], ALU.mult,
                           ['ps%d' % bp, ('gat', tbi)], [('macc', tbi)])
                    else:
                        tt(mt2[:, :tn], ps[bp][:, :tn], gat[tbi][:, :tn], ALU.mult,
                           ['ps%d' % bp, ('gat', tbi)], ['mt2'])
                        if b_ == 1:
                            tt(macc[tbi][:, :tn], macc[tbi][:, :tn], mt2[:, :tn], ALU.add,
                               [('macc', tbi), 'mt2'], [('macc', tbi)])
                        else:
                            tt(mT[:, c, t0:t0 + tn], macc[tbi][:, :tn], mt2[:, :tn], ALU.add,
                               [('macc', tbi), 'mt2'], [('m', c, tbi)])
        bg_flush()
        S.barrier()
        for q in range(4):
            dma('sp', xflat[:, q * 4 * T:(q + 1) * 4 * T], xsp_d[:, q * 4 * T:(q + 1) * 4 * T], ['xsp%d' % q], xkeys(q), 'ld_x%d' % q)
        wosl = [bfv(O_W + 40960 + i * 4096, 2048) for i in range(2)]

        def epi_o(i, tbi, t0, tn, p, pk):
            grp = 0 if tbi < 2 else 1
            stt(xT[:, i, t0:t0 + tn], p[:, :tn], Gcol(1, i, grp), xT[:, i, t0:t0 + tn], ALU.mult, ALU.add,
                [pk, 'Gvec', ('x', i, tbi)], [('x', i, tbi)])
        proj_ws(wosl, 'wo', [wout_d[l, i] for i in range(16)], 16, lambda kc, t0, tn: mT[:, kc, t0:t0 + tn],
                lambda kc, tbi: [('m', kc, tbi)], epi_o)
        S.barrier()

    ones_f = f32v(O_MODV + 2048, 128)
    macc = [tmpf[0], tmpf[1], f32v(O_RSTD, 512)]
    memset(ones_f, 1.0, ['onesf'], eng='dve')

    import os
    STOP = int(os.environ.get('KSTOP', '9'))
    for l in range(L):
        if STOP >= 1 and l == 0:
            mod_part_fast(0, 0)
            S.barrier()
        if STOP >= 2:
            ffn_phase(l, 0)
            S.barrier()
        if STOP >= 3:
            mix_phase(l)
        if STOP >= 4:
            ffn_phase(l, 1)
            S.barrier()

    for tbi, (t0, tn) in enumerate(TB):
        rms_rstd(lambda kc: xT[:, kc, t0:t0 + tn], 16, t0, tn, lambda kc: [('x', kc, tbi)],
                 rstd[:, t0:t0 + tn], ('rstd', tbi), D)
        for kc in range(16):
            stt(xT[:, kc, t0:t0 + tn], xT[:, kc, t0:t0 + tn], CF('fnorm', kc, 1), rstd[:, t0:t0 + tn], ALU.mult, ALU.mult,
                [('x', kc, tbi), ('rstd', tbi), 'cf'], [('x', kc, tbi)])
    for q in range(4):
        out_toks.append(dma('sp', y_d[:, q * 4 * T:(q + 1) * 4 * T], xflat[:, q * 4 * T:(q + 1) * 4 * T], xkeys(q), [], 'st_y%d' % q))
    S.final_waits('sp', out_toks)
    S.emit()
    st.close()
    return nc


def _tiles_ws(W, KC):
    K, M = W.shape
    nm = M // 128
    return np.ascontiguousarray(W.reshape(KC, 128, nm, 128).transpose(2, 1, 0, 3)).reshape(nm, 128, KC * 128)


def _tile_cols(W, c0, n, pad_to):
    blk = W[:, c0:c0 + n]
    if n < pad_to:
        blk = np.concatenate([blk, np.zeros((W.shape[0], pad_to - n), W.dtype)], axis=1)
    KC = W.shape[0] // 128
    return np.ascontiguousarray(blk.reshape(KC, 128, pad_to).transpose(1, 0, 2)).reshape(128, KC * pad_to)


def _fm(v):
    v = np.asarray(v)
    C = v.shape[-1] // 128
    lead = v.shape[:-1]
    return np.ascontiguousarray(np.moveaxis(v.reshape(lead + (C, 128)), -1, 0))


def _core_slots(c):
    if c < 2:
        return [('s', c, i) for i in range(4)] + [('p', 30 + c, 0)]
    return [('p', 5 * (c - 2) + i, 0) for i in range(5)]


def _rope_tables(slots):
    half = 32
    freqs = (10000.0 ** (-np.arange(half, dtype=np.float32) / half)).astype(np.float32)
    cosT = np.ones((128, T), np.float32)
    sinT = np.zeros((128, T), np.float32)
    for si, (kind, idx, part) in enumerate(slots):
        if kind != 's':
            continue
        tpos = part * 256 + np.arange(256)
        row = (tpos // 64).astype(np.float32)
        col = (tpos % 64).astype(np.float32)
        for d in range(128):
            pos = row if d < 64 else col
            f = freqs[d % 32]
            ang = (pos * f).astype(np.float32)
            cosT[d, si * 256:(si + 1) * 256] = np.cos(ang)
            sgn = -1.0 if (d % 64) < 32 else 1.0
            sinT[d, si * 256:(si + 1) * 256] = sgn * np.sin(ang)
    return np.concatenate([cosT, sinT], axis=1)


_PROG = {}


NLAYERS = 2
CORES = list(range(NCORES))


def kernel(**inp):
    L = NLAYERS
    f32 = np.float32
    g = {k: np.asarray(v) for k, v in inp.items()}
    cfo, NCF = cf_layout(L)
    cbo, NCB = cb_layout(L)
    ar = np.arange(128)
    ident = np.eye(128, dtype=f32)
    triU = (ar[:, None] <= ar[None, :]).astype(f32)
    triL = (ar[:, None] >= ar[None, :]).astype(f32)

    shared = {}
    shared['wmod'] = np.stack([_tiles_ws(g['w_mod'][l], 16) for l in range(L)])
    wgu = np.empty((L, 2, 88, 128, 2048), f32)
    wd = np.empty((L, 2, 22, 128, 4096), f32)
    for l in range(L):
        for w, pre in enumerate(('ffn1', 'ffn2')):
            tg = _tiles_ws(g[pre + '_w_gate'][l], 16)
            tu = _tiles_ws(g[pre + '_w_up'][l], 16)
            wgu[l, w] = np.stack([tg, tu], axis=1).reshape(88, 128, 2048)
            Wd = g[pre + '_w_down'][l]
            wd[l, w] = np.ascontiguousarray(
                Wd.reshape(11, 4, 128, 2, 8, 128).transpose(0, 3, 2, 4, 1, 5)).reshape(22, 128, 4096)
    shared['wgu'] = wgu
    shared['wd'] = wd
    wws = np.empty((L, len(WS), 128, 2048), f32)
    wv = np.empty((L, 5, 128, 4096), f32)
    wdt = np.empty((L, 128, 512), f32)
    wbr = np.empty((L, 48, 128, 1024), f32)
    wout = np.empty((L, 16, 128, 2048), f32)
    for l in range(L):
        Win = g['w_in'][l]
        for i, (nm, c0, n) in enumerate(WS):
            wws[l, i] = _tile_cols(Win, c0, n, 128)
        for h in range(4):
            wv[l, h] = _tile_cols(Win, 1024 + h * 256, 256, 256)
        wv[l, 4] = _tile_cols(Win, 7488, 256, 256)
        wdt[l] = _tile_cols(Win, 6176, 32, 32)
        for b_, nm in enumerate(('w_br_gla', 'w_br_ssm', 'w_br_attn')):
            wbr[l, b_ * 16:(b_ + 1) * 16] = _tiles_ws(g[nm][l], 8)
        wout[l] = _tiles_ws(g['w_out'][l], 16)
    shared.update(wws=wws, wv=wv, wdt=wdt, wbr=wbr, wout=wout)

    in_maps = []
    for c in CORES:
        slots = _core_slots(c)
        is_s = c < 2
        toks = []
        for (kind, idx, part) in slots:
            if kind == 's':
                toks.append(g['x_sample'][idx, part * 256:(part + 1) * 256])
            else:
                toks.append(g['x_prompt'][idx])
        x = np.concatenate(toks, axis=0)
        xin = np.ascontiguousarray(x.T.reshape(16, 128, T).transpose(1, 0, 2)).reshape(128, 16 * T)
        cf = np.zeros((128, NCF), f32)

        def put(name, arr):
            o, n = cfo[name]
            cf[:, o:o + n] = np.asarray(arr, f32).reshape(128, n)
        put('ident', ident)
        put('triU', triU)
        put('triL', triL)
        put('ntriU', -triU)
        put('ntriL', -triL)
        put('mnegF', np.where(ar[:, None] <= ar[None, :], 0.0, NEG))
        put('mnegB', np.where(ar[:, None] >= ar[None, :], 0.0, NEG))
        put('cummask', np.broadcast_to((np.arange(T) % 128 != 0).astype(f32)[None, :], (128, T)))
        put('fs', np.full((128, 1), 1.0 if is_s else 0.0))
        put('ctxbias', np.full((128, 1), 0.0 if is_s else NEG))
        cA = g['c'][c] if is_s else g['c_ctx']
        put('cvec', np.stack([_fm(cA), _fm(g['c_ctx'])], axis=2))
        nw = np.stack([np.stack([_fm(g[k][l]) for k in ('ffn1_norm', 'mix_norm', 'ffn2_norm')], axis=1) for l in range(L)], axis=1)
        put('normw', nw)
        put('fnorm', _fm(g['final_norm']))
        put('bmod', np.stack([_fm(g['b_mod'][l]) for l in range(L)], axis=1))
        put('nbup', np.stack([_fm(g['gla_b_up'][l]) for l in range(L)], axis=1))
        put('gnorm', np.stack([_fm(g['gla_norm'][l]) for l in range(L)], axis=1))
        put('convw', np.stack([_fm(g['ssm_conv_w'][l]).transpose(0, 2, 1) for l in range(L)], axis=1))
        put('convb', np.stack([_fm(g['ssm_conv_b'][l]) for l in range(L)], axis=1))
        put('ssmD', np.stack([_fm(np.repeat(g['ssm_d'][l], 64)) for l in range(L)], axis=1))
        put('ssmnorm', np.stack([_fm(g['ssm_norm'][l]) for l in range(L)], axis=1))
        put('dtbias', np.broadcast_to(g['ssm_dt_bias'][:L].reshape(1, L * 32), (128, L * 32)))
        put('alog', np.broadcast_to(g['ssm_a_log'][:L].reshape(1, L * 32), (128, L * 32)))
        put('sink', np.broadcast_to(g['attn_sink'][:L].reshape(1, L * 8), (128, L * 8)))

        cbm = np.zeros((128, NCB), f32)

        def putb(name, arr):
            o, n = cbo[name]
            cbm[:, o:o + n] = np.asarray(arr, f32).reshape(128, n)
        putb('ones', np.ones((128, 128)))
        putb('identb', ident)
        perm = np.array([d + 32 if (d % 64) < 32 else d - 32 for d in range(128)])
        pm = np.zeros((128, 128), f32)
        pm[perm, np.arange(128)] = 1.0
        putb('pm', pm)
        putb('maskFB', np.concatenate([triU, triL], axis=1))
        am = np.zeros((128, 18, 128), f32)
        ones = np.ones((128, 128), f32)
        for j in range(NT):
            kind = slots[j // 2][0]
            if j >= 1:
                if kind == 's' and slots[(j - 1) // 2][0] == 's':
                    am[:, j - 1, :] = triL
                elif kind == 'p' and (j % 2 == 1):
                    am[:, j - 1, :] = ones
            if j <= NT - 2:
                if kind == 's' and slots[(j + 1) // 2][0] == 's':
                    am[:, 9 + j, :] = triU
                elif kind == 'p' and (j % 2 == 0):
                    am[:, 9 + j, :] = ones
        putb('amask', am)
        wup = np.zeros((128, L, 2, 512), f32)
        for l in range(L):
            for d_ in range(2):
                wup[d_ * 16:(d_ + 1) * 16, l, d_, :] = g['gla_w_up'][l, d_]
        putb('wup', wup)

        h0g = np.zeros((L, 2, 4, 128, 256), f32)
        h0s = np.zeros((L, 2, 128, 1024), f32)
        ctxk = np.zeros((L, 2, 128, 512), f32)
        ctxv = np.zeros((L, 2, 128, 512), f32)
        if is_s:
            for l in range(L):
                h0g[l] = g['state_gla'][c, l]
                h0s[l] = g['state_ssm'][c, l].transpose(0, 3, 1, 2).reshape(2, 128, 1024)
                ctxk[l] = g['cache_k'][c, l].transpose(1, 2, 0)
                ctxv[l] = g['cache_v'][c, l].reshape(4, 128, 2, 128).transpose(2, 1, 0, 3).reshape(2, 128, 512)
        m = dict(xin=xin, cf=cf, cb=cbm, rope=_rope_tables(slots), h0g=h0g, h0s=h0s, ctxk=ctxk, ctxv=ctxv)
        m.update(shared)
        in_maps.append(m)

    if L not in _PROG:
        _PROG[L] = build_program(L)
    import os
    if os.environ.get('KTRACE') == '1':
        res = run_bass_kernel_spmd(_PROG[L], in_maps, core_ids=list(range(len(CORES))), trace=True)
        print('EXEC_TIME_NS', res.exec_time_ns)
    else:
        res = run_bass_kernel_spmd(_PROG[L], in_maps, core_ids=list(range(len(CORES))))
    R = res.results

    y_prompt = np.zeros((32, 256, D), f32)
    y_sample = np.zeros((2, 1024, D), f32)
    nk = np.zeros((32, L, 256, 2, 128), f32)
    nv = np.zeros((32, L, 256, 2, 128), f32)
    ng = np.zeros((32, L, 2, 4, 128, 256), f32)
    ns = np.zeros((32, L, 2, 16, 64, 128), f32)
    for ci, c in enumerate(CORES):
        r = R[ci]
        y = r['yT'].reshape(128, 16, T).transpose(2, 1, 0).reshape(T, D)
        ok = r['ok'].reshape(L, 2, 128, T)
        ov = r['ov'].reshape(L, T, 2, 128)
        og = r['og'].reshape(L, 5, 2, 4, 128, 256)
        os_ = r['os'].reshape(L, 5, 2, 128, 16, 64)
        for si, (kind, idx, part) in enumerate(_core_slots(c)):
            sl = slice(si * 256, (si + 1) * 256)
            if kind == 's':
                y_sample[idx, part * 256:(part + 1) * 256] = y[sl]
            else:
                y_prompt[idx] = y[sl]
                nk[idx] = ok[:, :, :, sl].transpose(0, 3, 1, 2)
                nv[idx] = ov[:, sl]
                ng[idx] = og[:, si]
                ns[idx] = os_[:, si].transpose(0, 1, 3, 4, 2)
    return (y_prompt, y_sample, nk, nv, ng, ns)
```

```python
import math
import contextlib
import numpy as np
import concourse.bass as bass
import concourse.mybir as mybir
from concourse.bass_utils import run_bass_kernel_spmd

F32 = mybir.dt.float32
BF16 = mybir.dt.bfloat16
AF = mybir.ActivationFunctionType
ALU = mybir.AluOpType

EPOCH = 8192
NCORES = 8
T = 1280
NT = 10
D = 2048
DFF = 5632
TB = [(0, 512), (512, 512), (1024, 256)]
NEG = -30000.0


class Sched:
    ENGS = ('pe', 'act', 'dve', 'pool', 'sp')

    def __init__(self, nc):
        self.nc = nc
        self.ops = {e: [] for e in self.ENGS}
        self.count = {e: 0 for e in self.ENGS}
        self.last_w = {}
        self.readers = {}
        self.waited = {e: {} for e in self.ENGS}
        self.dma_count = {}
        self.semnames = set()
        self.last_tok = {}

    def _need(self, eng, is_dma_consumer, tok, raw):
        semkey, val, peng, pdma = tok
        if not pdma and not is_dma_consumer and peng == eng:
            if eng == 'pe':
                return False
        return True

    def _add_wait(self, eng, waits, tok):
        semkey, val = tok[0], tok[1]
        w = self.waited[eng]
        if w.get(semkey, 0) >= val:
            return
        w[semkey] = val
        if isinstance(semkey, tuple):
            pe_, ep = semkey
            for e2 in range(ep):
                w[(pe_, e2)] = EPOCH
        for i, (k, v) in enumerate(waits):
            if k == semkey:
                waits[i] = (k, max(v, val))
                return
        waits.append((semkey, val))

    def op(self, eng, fn, reads=(), writes=(), dma=None):
        is_dma = dma is not None
        waits = []
        for k in reads:
            t = self.last_w.get(k)
            if t is not None and self._need(eng, is_dma, t, True):
                self._add_wait(eng, waits, t)
        for k in writes:
            t = self.last_w.get(k)
            if t is not None and self._need(eng, is_dma, t, False):
                self._add_wait(eng, waits, t)
            for t in self.readers.get(k, {}).values():
                if self._need(eng, is_dma, t, False):
                    self._add_wait(eng, waits, t)
        if is_dma:
            n = self.dma_count.get(dma, 0) + 1
            self.dma_count[dma] = n
            tok = (dma, 16 * n, eng, True)
            inc = (dma, 16)
            rkey = dma
        else:
            idx = self.count[eng]
            self.count[eng] = idx + 1
            semkey = (eng, idx // EPOCH)
            tok = (semkey, idx % EPOCH + 1, eng, False)
            inc = (semkey, 1)
            rkey = eng
        self.semnames.add(inc[0])
        self.last_tok[inc[0] if is_dma else eng] = tok
        for k in writes:
            self.last_w[k] = tok
            self.readers[k] = {}
        for k in reads:
            self.readers.setdefault(k, {})[rkey] = tok
        self.ops[eng].append((fn, waits, inc))
        return tok

    def barrier(self):
        toks = list(self.last_tok.values())
        for eng in self.ENGS:
            waits = []
            for t in toks:
                if (not t[3]) and t[2] == eng:
                    continue
                self._add_wait(eng, waits, t)
            if waits:
                self.ops[eng].append((None, waits, None))
        self.last_w = {}
        self.readers = {}

    def final_waits(self, eng, toks):
        waits = []
        for t in toks:
            self._add_wait(eng, waits, t)
        self.ops[eng].append((None, waits, None))

    def emit(self):
        nc = self.nc
        with contextlib.ExitStack() as st:
            sems = {}
            for i, k in enumerate(sorted(self.semnames, key=str)):
                sems[k] = st.enter_context(nc.semaphore("sm%d" % i))
            block = st.enter_context(nc.Block())

            def run(engname):
                def body(e):
                    for fn, waits, inc in self.ops[engname]:
                        for (k, v) in waits:
                            e.wait_ge(sems[k], v)
                        if fn is not None:
                            fn(e).then_inc(sems[inc[0]], inc[1])
                return body
            block.tensor(run('pe'))
            block.scalar(run('act'))
            block.vector(run('dve'))
            block.gpsimd(run('pool'))
            block.sync(run('sp'))


def _layout(items):
    off = {}
    o = 0
    for name, n in items:
        off[name] = (o, n)
        o += n
    return off, o


def cf_layout(L):
    return _layout([
        ('ident', 128), ('triU', 128), ('triL', 128), ('ntriU', 128), ('ntriL', 128), ('mnegF', 128), ('mnegB', 128),
        ('cummask', 1280), ('fs', 1), ('ctxbias', 1), ('cvec', 32),
        ('normw', L * 48), ('fnorm', 16), ('bmod', L * 144), ('nbup', L * 8), ('gnorm', L * 2),
        ('convw', L * 80), ('convb', L * 16), ('ssmD', L * 8), ('ssmnorm', L * 8),
        ('dtbias', L * 32), ('alog', L * 32), ('sink', L * 8),
    ])


def cb_layout(L):
    return _layout([
        ('ones', 128), ('identb', 128), ('pm', 128), ('maskFB', 256), ('amask', 18 * 128),
        ('wup', L * 1024),
    ])


def ws_chunks():
    ch = [('gdown', 3072, 32)]
    for h in range(4):
        ch += [('gq%d' % h, h * 128, 128), ('gk%d' % h, 512 + h * 128, 128),
               ('gr%d_0' % h, 2048 + h * 256, 128), ('gr%d_1' % h, 2048 + h * 256 + 128, 128)]
    for g in range(4):
        ch += [('sx%d_0' % g, 4128 + g * 256, 128), ('sx%d_1' % g, 4128 + g * 256 + 128, 128),
               ('sB%d' % g, 5152 + g * 128, 128), ('sC%d' % g, 5664 + g * 128, 128),
               ('sz%d_0' % g, 3104 + g * 256, 128), ('sz%d_1' % g, 3104 + g * 256 + 128, 128)]
    for kv in range(2):
        for j in range(4):
            ch.append(('aq%d' % (kv * 4 + j), 6208 + (kv * 4 + j) * 128, 128))
        ch.append(('ak%d' % kv, 7232 + kv * 128, 128))
    for b in range(3):
        for c in range(16):
            ch.append(('br%d_%d' % (b, c), 7744 + b * 2048 + c * 128, 128))
    return ch


WS = ws_chunks()
WSI = {n: i for i, (n, _, _) in enumerate(WS)}

ARENA_B = 210944
O_CF = 0
O_CB = 12288
O_RSTD = 22528
O_MODV = 27648
O_SCR = 30208
O_X = 36352
O_H = 118272
O_W = 159232


def build_program(L):
    nc = bass.Bass("TRN2", target_bir_lowering=False)
    cfo, NCF = cf_layout(L)
    cbo, NCB = cb_layout(L)
    assert NCF * 4 <= O_CB and NCB * 2 <= O_RSTD - O_CB, (NCF, NCB)

    def din(name, shape):
        return nc.dram_tensor(name, shape, F32, kind="ExternalInput").ap()

    def dout(name, shape):
        return nc.dram_tensor(name, shape, F32, kind="ExternalOutput").ap()

    xin = din("xin", [128, 16 * T])
    cf_d = din("cf", [128, NCF])
    cb_d = din("cb", [128, NCB])
    rope_d = din("rope", [128, 2 * T])
    h0g_d = din("h0g", [L, 2, 4, 128, 256])
    h0s_d = din("h0s", [L, 2, 128, 1024])
    ctxk_d = din("ctxk", [L, 2, 128, 512])
    ctxv_d = din("ctxv", [L, 2, 128, 512])
    wmod_d = din("wmod", [L, 144, 128, 2048])
    wgu_d = din("wgu", [L, 2, 88, 128, 2048])
    wd_d = din("wd", [L, 2, 22, 128, 4096])
    wws_d = din("wws", [L, len(WS), 128, 2048])
    wv_d = din("wv", [L, 5, 128, 4096])
    wdt_d = din("wdt", [L, 128, 512])
    wbr_d = din("wbr", [L, 48, 128, 1024])
    wout_d = din("wout", [L, 16, 128, 2048])
    y_d = dout("yT", [128, 16 * T])
    ok_d = dout("ok", [L, 2, 128, T])
    ov_d = dout("ov", [L, NT, 128, 256])
    og_d = dout("og", [L, 5, 2, 4, 128, 256])
    os_d = dout("os", [L, 5, 2, 128, 1024])
    xsp_d = dout("xspill", [128, 16 * T])

    st = contextlib.ExitStack()
    arena = st.enter_context(nc.sbuf_tensor("arena", [128, ARENA_B // 4], F32))
    ps = [st.enter_context(nc.psum_tensor("ps%d" % i, [128, 512], F32)) for i in range(8)]
    S = Sched(nc)
    out_toks = []

    def f32v(off, n):
        assert off % 4 == 0
        return arena[:, off // 4: off // 4 + n]

    def bfv(off, n):
        assert off % 4 == 0 and n % 2 == 0
        return arena[:, off // 4: off // 4 + n // 2].bitcast(BF16)

    def r3(ap, a):
        return ap.rearrange("p (a b) -> p a b", a=a)

    def act(out, in_, func, r, w, **kw):
        S.op('act', lambda e: e.activation(out=out, in_=in_, func=func, **kw), reads=r, writes=w)

    def tt(out, a, b, op, r, w, eng='dve'):
        S.op(eng, lambda e: e.tensor_tensor(out=out, in0=a, in1=b, op=op), reads=r, writes=w)

    def ts(out, a, s1, s2, op0, op1, r, w):
        if s2 is None:
            S.op('dve', lambda e: e.tensor_scalar(out=out, in0=a, scalar1=s1, scalar2=None, op0=op0), reads=r, writes=w)
        else:
            S.op('dve', lambda e: e.tensor_scalar(out=out, in0=a, scalar1=s1, scalar2=s2, op0=op0, op1=op1), reads=r, writes=w)

    def stt(out, a, sc, b, op0, op1, r, w):
        S.op('dve', lambda e: e.scalar_tensor_tensor(out=out, in0=a, scalar=sc, in1=b, op0=op0, op1=op1), reads=r, writes=w)

    def cp(out, in_, r, w, eng='dve'):
        if eng == 'act':
            S.op(eng, lambda e: e.activation(out=out, in_=in_, func=AF.Copy), reads=r, writes=w)
        else:
            S.op(eng, lambda e: e.tensor_copy(out=out, in_=in_), reads=r, writes=w)

    def mm(out, lhsT, rhs, start, stop, r, w):
        S.op('pe', lambda e: e.matmul(out, lhsT=lhsT, rhs=rhs, start=start, stop=stop), reads=r, writes=w)

    def dma(q, out, in_, r, w, sem):
        return S.op(q, lambda e: e.dma_start(out=out, in_=in_), reads=r, writes=w, dma=sem)

    def memset(ap, val, w, eng='pool'):
        S.op(eng, lambda e: e.memset(ap, val), writes=w)

    cf = f32v(O_CF, NCF)
    cb = bfv(O_CB, NCB)

    def CF(name, i=0, n=None):
        o, m = cfo[name]
        n = m if n is None else n
        return cf[:, o + i: o + i + n]

    def CB(name, i=0, n=None):
        o, m = cbo[name]
        n = m if n is None else n
        return cb[:, o + i: o + i + n]

    rstd = f32v(O_RSTD, T)
    modv = r3(f32v(O_MODV, 288), 144)
    Avec = f32v(O_MODV + 1152, 96)
    Gvec = f32v(O_MODV + 1152 + 384, 96)
    siluc = f32v(O_MODV + 1152 + 768, 32)
    sqb = [bfv(O_SCR + i * 1024, 512) for i in range(2)]
    tmpf = [f32v(O_SCR + 2048 + i * 2048, 512) for i in range(2)]
    xT = r3(f32v(O_X, 16 * T), 16)
    hT = r3(bfv(O_H, 16 * T), 16)
    ones_b = CB('ones')
    ident_b = CB('identb')
    ident_f = CF('ident')
    fs = CF('fs')
    ctxbias = CF('ctxbias')

    dma('sp', cf, cf_d, [], ['cf'], 'ld_cf')
    dma('pool', cb, cb_d, [], ['cb'], 'ld_cb')
    def xkeys(q):
        return [('x', kc, tb_) for kc in range(q * 4, q * 4 + 4) for tb_ in range(3)]
    xflat = f32v(O_X, 16 * T)
    for q in range(4):
        dma('sp', xflat[:, q * 4 * T:(q + 1) * 4 * T], xin[:, q * 4 * T:(q + 1) * 4 * T], [], xkeys(q), 'ld_x%d' % q)
    act(siluc, CF('cvec'), AF.Silu, ['cf'], ['siluc'])
    ts(CF('nbup'), CF('nbup'), -1.0, None, ALU.mult, None, ['cf'], ['cf'])

    psrr = [0]
    held = set()

    def nextps(hold=False):
        while True:
            b = psrr[0] % 8
            psrr[0] += 1
            if b not in held:
                break
        if hold:
            held.add(b)
        return b

    def release(*bs):
        for b in bs:
            held.discard(b)

    scb = bfv(O_CF + 12032, 32)
    scb3 = r3(scb, 16)
    cp(scb, siluc, ['siluc'], ['scb'])

    def mod_finish(l, p, b):
        bm = CF('bmod', l * 144 + p * 48, 48)
        tt(modv[:, p * 48:(p + 1) * 48, :], r3(ps[b][:, 0:96], 48), bm.unsqueeze(2).to_broadcast([128, 48, 2]), ALU.add,
           ['ps%d' % b, 'cf'], ['modv'])
        release(b)
        i = p
        nw = CF('normw', l * 48 + i * 16, 16)
        stt(r3(Avec[:, i * 32:(i + 1) * 32], 16), modv[:, (3 * i + 1) * 16:(3 * i + 2) * 16, :], 1.0,
            nw.unsqueeze(2).to_broadcast([128, 16, 2]), ALU.add, ALU.mult, ['modv', 'cf'], ['Avec'])
        ts(r3(Gvec[:, i * 32:(i + 1) * 32], 16), modv[:, (3 * i + 2) * 16:(3 * i + 3) * 16, :],
           1.0 if i == 1 else 0.5, None, ALU.mult, None, ['modv'], ['Gvec'])

    def mod_part_fast(l, p):
        NS = 6
        slots = [bfv(O_W + i * 4096, 2048) for i in range(NS)]
        b = nextps(hold=True)
        for j in range(48):
            m = p * 48 + j
            s = j % NS
            dma('pool', slots[s], wmod_d[l, m], [], [('wmm', s)], 'ld_mod%d' % s)
            w3 = r3(slots[s], 16)
            for kc in range(16):
                mm(ps[b][:, 2 * j:2 * j + 2], w3[:, kc, :], scb3[:, kc, :], kc == 0, kc == 15,
                   [('wmm', s), 'scb'], ['ps%d' % b])
        mod_finish(l, p, b)

    bg = {'pending': [], 'bank': None}

    def bg_start(l, p, slot_ap):
        bg.update(l=l, p=p, slot=slot_ap, bank=nextps(hold=True), pending=list(range(48)))

    def bg_step(n=1):
        for _ in range(n):
            if bg['bank'] is None or not bg['pending']:
                return
            j = bg['pending'].pop(0)
            m = bg['p'] * 48 + j
            b = bg['bank']
            dma('pool', bg['slot'], wmod_d[bg['l'], m], [], ['bgw'], 'ld_bgw')
            w3 = r3(bg['slot'], 16)
            for kc in range(16):
                mm(ps[b][:, 2 * j:2 * j + 2], w3[:, kc, :], scb3[:, kc, :], kc == 0, kc == 15,
                   ['bgw', 'scb'], ['ps%d' % b])

    def bg_flush():
        if bg['bank'] is None:
            return
        bg_step(48)
        b = bg['bank']
        bg['bank'] = None
        mod_finish(bg['l'], bg['p'], b)

    def Acol(i, kc, grp):
        return Avec[:, i * 32 + kc * 2 + grp: i * 32 + kc * 2 + grp + 1]

    def Bcol(i, kc, grp):
        return modv[:, 3 * i * 16 + kc, grp:grp + 1]

    def Gcol(i, kc, grp):
        return Gvec[:, i * 32 + kc * 2 + grp: i * 32 + kc * 2 + grp + 1]

    def rms_rstd(src_fn, nchunks, t0, tn, rkeys_fn, out_ap, wkey, dim):
        b = nextps()
        for kc in range(nchunks):
            sq = sqb[kc % 2]
            act(sq[:, :tn], src_fn(kc), AF.Square, rkeys_fn(kc), [('sq', kc % 2)])
            mm(ps[b][:, :tn], ones_b, sq[:, :tn], kc == 0, kc == nchunks - 1, [('sq', kc % 2), 'cb'], ['ps%d' % b])
        act(out_ap, ps[b][:, :tn], AF.Sqrt, ['ps%d' % b], [wkey], scale=1.0 / dim, bias=1e-6)
        S.op('dve', lambda e: e.reciprocal(out=out_ap, in_=out_ap), reads=[wkey], writes=[wkey])

    def norm_to_h(i):
        for tbi, (t0, tn) in enumerate(TB):
            grp = 0 if tbi < 2 else 1
            rms_rstd(lambda kc: xT[:, kc, t0:t0 + tn], 16, t0, tn, lambda kc: [('x', kc, tbi)],
                     rstd[:, t0:t0 + tn], ('rstd', tbi), D)
            for kc in range(16):
                tf = tmpf[kc % 2]
                stt(tf[:, :tn], xT[:, kc, t0:t0 + tn], Acol(i, kc, grp), rstd[:, t0:t0 + tn], ALU.mult, ALU.mult,
                    [('x', kc, tbi), ('rstd', tbi), 'Avec'], [('tmpf', kc % 2)])
                act(hT[:, kc, t0:t0 + tn], tf[:, :tn], AF.Identity, [('tmpf', kc % 2), 'modv'], [('h', kc, tbi)],
                    bias=Bcol(i, kc, grp), scale=1.0)

    def proj_ws(wslots, wkey, dram_tiles, KCn, rhs_fn, rhs_keys_fn, epi, M=128, tbs=None):
        cnt = proj_ws.cnt
        for i, dt_ in enumerate(dram_tiles):
            s = cnt[wkey] % len(wslots)
            cnt[wkey] += 1
            wt = wslots[s]
            dma('pool', wt[:, :KCn * 128], dt_, [], [(wkey, s)], 'ld_%s%d' % (wkey, s))
            w3 = r3(wt[:, :KCn * 128], KCn)
            for tbi, (t0, tn) in enumerate(TB if tbs is None else tbs):
                b = nextps()
                for kc in range(KCn):
                    mm(ps[b][:M, :tn], w3[:, kc, 0:M], rhs_fn(kc, t0, tn), kc == 0, kc == KCn - 1,
                       [(wkey, s)] + rhs_keys_fn(kc, tbi), ['ps%d' % b])
                epi(i, tbi, t0, tn, ps[b], 'ps%d' % b)
    import collections
    proj_ws.cnt = collections.defaultdict(int)

    def h_rhs(kc, t0, tn):
        return hT[:, kc, t0:t0 + tn]

    def h_keys(kc, tbi):
        return [('h', kc, tbi)]

    def ffn_phase(l, which):
        i_sub = 0 if which == 0 else 2
        norm_to_h(i_sub)
        gT = r3(bfv(O_W, 4 * T), 4)
        sg = [bfv(O_W + 10240 + i * 2560, T) for i in range(2)]
        wsl = [bfv(O_W + 15360 + i * 4096, 2048) for i in range(3)]
        wdsl = [bfv(O_W + 31744 + i * 8192, 4096) for i in range(2)]
        if which == 0:
            bg_start(l, 1, bfv(O_W + 27648, 2048))
        elif l + 1 < L:
            bg_start(l + 1, 0, bfv(O_W + 27648, 2048))
        for g in range(11):
            for j in range(4):
                hc = g * 4 + j

                def epi(i, tbi, t0, tn, p, pk, j=j):
                    if i == 0:
                        act(sg[j % 2][:, t0:t0 + tn], p[:, :tn], AF.Silu, [pk], [('sg', j % 2, tbi)])
                    else:
                        tt(gT[:, j, t0:t0 + tn], p[:, :tn], sg[j % 2][:, t0:t0 + tn], ALU.mult,
                           [pk, ('sg', j % 2, tbi)], [('g', j, tbi)])
                proj_ws(wsl, 'wf', [wgu_d[l, which, 2 * hc], wgu_d[l, which, 2 * hc + 1]], 16, h_rhs, h_keys, epi)
                bg_step(1)
            for half in range(2):
                s = proj_ws.cnt['wd'] % 2
                proj_ws.cnt['wd'] += 1
                dma('pool', wdsl[s], wd_d[l, which, g * 2 + half], [], [('wd', s)], 'ld_wd%d' % s)
                w4 = wdsl[s].rearrange("p (m k j) -> p m k j", m=8, k=4)
                for mi in range(8):
                    mc = half * 8 + mi
                    for tbi, (t0, tn) in enumerate(TB):
                        grp = 0 if tbi < 2 else 1
                        b = nextps()
                        for k in range(4):
                            mm(ps[b][:, :tn], w4[:, mi, k, :], gT[:, k, t0:t0 + tn], k == 0, k == 3,
                               [('wd', s), ('g', k, tbi)], ['ps%d' % b])
                        stt(xT[:, mc, t0:t0 + tn], ps[b][:, :tn], Gcol(i_sub, mc, grp), xT[:, mc, t0:t0 + tn],
                            ALU.mult, ALU.add, ['ps%d' % b, 'Gvec', ('x', mc, tbi)], [('x', mc, tbi)])
                bg_step(1)
        bg_flush()

    import os
    KMIX = int(os.environ.get('KMIX', '9'))
    KATT = int(os.environ.get('KATT', '99'))

    def mix_phase(l):
        norm_to_h(1)
        for q in range(4):
            dma('sp', xsp_d[:, q * 4 * T:(q + 1) * 4 * T], xflat[:, q * 4 * T:(q + 1) * 4 * T], xkeys(q), ['xsp%d' % q], 'st_x%d' % q)
        S.barrier()
        if KMIX == 0:
            for q in range(4):
                dma('sp', xflat[:, q * 4 * T:(q + 1) * 4 * T], xsp_d[:, q * 4 * T:(q + 1) * 4 * T], ['xsp%d' % q], xkeys(q), 'ld_x%d' % q)
            S.barrier()
            return
        oG = r3(bfv(O_X, 8 * T), 8)
        oS = r3(bfv(O_X + 20480, 8 * T), 8)
        oA = r3(bfv(O_X + 40960, 8 * T), 8)
        wsl = [bfv(O_X + 61440 + i * 4096, 2048) for i in range(3)]
        wasl = [bfv(O_X + 61440 + 12288, 4096)]

        def wtile(name):
            return wws_d[l, WSI[name]]

        W0 = O_W
        gdT = f32v(W0, T)
        gdb = bfv(W0 + 5120, T)
        qT = f32v(W0 + 7680, T)
        kT = f32v(W0 + 12800, T)
        rg = r3(bfv(W0 + 17920, 2 * T), 2)
        vtok = r3(bfv(W0 + 23040, NT * 256), NT)
        la = f32v(W0 + 28160, T)
        ex = f32v(W0 + 33280, T)
        qk = [bfv(W0 + 38400 + i * 2560, T) for i in range(4)]
        sc1 = O_X + 20480
        SIn = [r3(bfv(sc1 + d_ * 5120, NT * 256), NT) for d_ in range(2)]
        S32 = [f32v(sc1 + 10240 + d_ * 1024, 256) for d_ in range(2)]
        Stmps = [f32v(sc1 + 12288, 256), f32v(sc1 + 31744, 256)]
        ktok = [bfv(sc1 + 13312 + d_ * 256, 128) for d_ in range(2)]
        ABt = [bfv(sc1 + 13824 + i * 512, 256) for i in range(2)]
        o32 = r3(f32v(sc1 + 14848, 2 * T), 2)
        rs2 = f32v(sc1 + 25088, T)
        Ecol = f32v(sc1 + 30208, 32)
        cendb = f32v(sc1 + 30336, 16)

        def epi_gd(i, tbi, t0, tn, p, pk):
            cp(gdb[0:32, t0:t0 + tn], p[0:32, :tn], [pk], [('gdb', tbi)])
        proj_ws(wsl, 'wm', [wtile('gdown')], 16, h_rhs, h_keys, epi_gd, M=32)
        lnscale = math.log(128 ** -0.5)
        for h in range(4):
            def epi_q(i, tbi, t0, tn, p, pk):
                act(qT[:, t0:t0 + tn], p[:, :tn], AF.Copy, [pk], [('qT', tbi)])

            def epi_k(i, tbi, t0, tn, p, pk):
                act(kT[:, t0:t0 + tn], p[:, :tn], AF.Copy, [pk], [('kT', tbi)])

            def epi_r(i, tbi, t0, tn, p, pk):
                act(rg[:, i, t0:t0 + tn], p[:, :tn], AF.Silu, [pk], [('rg', i, tbi)])
            proj_ws(wsl, 'wm', [wtile('gq%d' % h)], 16, h_rhs, h_keys, epi_q)
            proj_ws(wsl, 'wm', [wtile('gk%d' % h)], 16, h_rhs, h_keys, epi_k)
            proj_ws(wsl, 'wm', [wtile('gr%d_0' % h), wtile('gr%d_1' % h)], 16, h_rhs, h_keys, epi_r)
            dma('pool', wasl[0], wv_d[l, h], [], [('wa', 0)], 'ld_wa0')
            wv3 = r3(wasl[0], 16)
            for tt_ in range(NT):
                b = nextps()
                for kc in range(16):
                    mm(ps[b][:, 0:256], hT[:, kc, tt_ * 128:(tt_ + 1) * 128], wv3[:, kc, :], kc == 0, kc == 15,
                       [('wa', 0), ('h', kc, tt_ // 4)], ['ps%d' % b])
                cp(vtok[:, tt_, :], ps[b][:, 0:256], ['ps%d' % b], [('vtok', tt_)])
            for d_ in range(2):
                wup = CB('wup', l * 1024 + d_ * 512 + h * 128, 128)
                nb = CF('nbup', l * 8 + d_ * 4 + h, 1)
                for tbi, (t0, tn) in enumerate(TB):
                    b = nextps()
                    mm(ps[b][:, :tn], wup[0:32, :], gdb[0:32, t0:t0 + tn], True, True, ['cb', ('gdb', tbi)], ['ps%d' % b])
                    act(ex[:, t0:t0 + tn], ps[b][:, :tn], AF.Exp, ['ps%d' % b, 'cf'], ['ex'], scale=-1.0, bias=nb)
                    act(ex[:, t0:t0 + tn], ex[:, t0:t0 + tn], AF.Ln, ['ex'], ['ex'], bias=1.0, scale=1.0)
                ts(la, ex, -1.0 / 16.0, None, ALU.mult, None, ['ex'], ['la'])
                S.op('dve', lambda e: e.tensor_tensor_scan(out=ex, data0=CF('cummask'), data1=la, initial=0.0,
                                                           op0=ALU.mult, op1=ALU.add),
                     reads=['la', 'cf'], writes=['ex'])
                la3 = r3(la, NT)
                ex3 = r3(ex, NT)
                if d_ == 1:
                    tt(la3, la3, ex3, ALU.subtract, ['la', 'ex'], ['la'])
                    cp(cendb[:, 0:NT], ex3[:, :, 127], ['ex'], ['cendb'])
                    tt(ex3, la3, cendb[:, 0:NT].unsqueeze(2).to_broadcast([128, NT, 128]), ALU.add, ['la', 'ex', 'cendb'], ['ex'])
                    ckey = 'ex'
                    ecol = ex3[:, :, 0]
                else:
                    ckey = 'ex'
                    ecol = ex3[:, :, 127]
                act(Ecol[:, d_ * 16:d_ * 16 + NT], ecol, AF.Exp, [ckey], [('Ecol', d_)])
                act(la, ex, AF.Exp, [ckey], ['la'], bias=lnscale, scale=1.0)
                tt(qk[2 * d_], qT, la, ALU.mult, ['la', ('qT', 0), ('qT', 1), ('qT', 2)], [('qk', 2 * d_)])
                act(la, ex, AF.Exp, [ckey, ('qk', 2 * d_)], ['la'], scale=-1.0)
                tt(qk[2 * d_ + 1], kT, la, ALU.mult, ['la', ('kT', 0), ('kT', 1), ('kT', 2)], [('qk', 2 * d_ + 1)])
            for step in range(NT):
              for d_ in range(2):
                    ti = step if d_ == 0 else NT - 1 - step
                    kt_ = qk[2 * d_ + 1]
                    Stmp = Stmps[d_]
                    slot = ti // 2
                    first = (ti % 2 == 0) if d_ == 0 else (ti % 2 == 1)
                    if first:
                        chain = (slot in (1, 2, 3)) if d_ == 0 else (slot in (0, 1, 2))
                        has_h0 = (slot == 0) if d_ == 0 else (slot == 3)
                        if has_h0:
                            dma('sp', S32[d_], h0g_d[l, d_, h], [], [('S32', d_)], 'ld_h0%d' % d_)
                        elif chain:
                            ts(S32[d_], S32[d_], fs, None, ALU.mult, None, [('S32', d_), 'cf'], [('S32', d_)])
                        else:
                            memset(S32[d_], 0.0, [('S32', d_)], eng='dve')
                    cp(SIn[d_][:, ti, :], S32[d_], [('S32', d_)], [('SIn', d_, ti)], eng='act')
                    b = nextps()
                    mm(ps[b][:, 0:128], kt_[:, ti * 128:(ti + 1) * 128], ident_b, True, True, [('qk', 2 * d_ + 1), 'cb'], ['ps%d' % b])
                    cp(ktok[d_], ps[b][:, 0:128], ['ps%d' % b], [('ktok', d_)])
                    b2 = nextps()
                    mm(ps[b2][:, 0:256], ktok[d_], vtok[:, ti, :], True, True, [('ktok', d_), ('vtok', ti)], ['ps%d' % b2])
                    tt(Stmp, ps[b2][:, 0:256], S32[d_], ALU.add, ['ps%d' % b2, ('S32', d_)], [('Stmp', d_)])
                    ts(S32[d_], Stmp, Ecol[:, d_ * 16 + ti:d_ * 16 + ti + 1], None, ALU.mult, None,
                       [('Stmp', d_), ('Ecol', d_)], [('S32', d_)])
                    last = (ti % 2 == 1) if d_ == 0 else (ti % 2 == 0)
                    if last:
                        out_toks.append(dma('sp', og_d[l, slot, d_, h], S32[d_], [('S32', d_)], [], 'st_og%d' % d_))
            for tbi, (t0, tn) in enumerate(TB):
                ntl = tn // 128
                bo = [nextps(hold=True), nextps(hold=True)]
                for tq in range(ntl):
                    ti = t0 // 128 + tq
                    sl = slice(ti * 128, (ti + 1) * 128)
                    b = nextps()
                    mm(ps[b][:, 0:128], qk[1][:, sl], qk[0][:, sl], True, True, [('qk', 0), ('qk', 1)], ['ps%d' % b])
                    mm(ps[b][:, 128:256], qk[3][:, sl], qk[2][:, sl], True, True, [('qk', 2), ('qk', 3)], ['ps%d' % b])
                    AB = ABt[ti % 2]
                    tt(AB, ps[b][:, 0:256], CB('maskFB'), ALU.mult, ['ps%d' % b, 'cb'], [('AB', ti % 2)])
                    for vc in range(2):
                        o = ps[bo[vc]][:, tq * 128:(tq + 1) * 128]
                        vs = vtok[:, ti, vc * 128:(vc + 1) * 128]
                        mm(o, vs, AB[:, 0:128], True, False, [('vtok', ti), ('AB', ti % 2)], ['ps%d' % bo[vc]])
                        mm(o, vs, AB[:, 128:256], False, False, [('vtok', ti), ('AB', ti % 2)], ['ps%d' % bo[vc]])
                        mm(o, SIn[0][:, ti, vc * 128:(vc + 1) * 128], qk[0][:, sl], False, False,
                           [('SIn', 0, ti), ('qk', 0)], ['ps%d' % bo[vc]])
                        mm(o, SIn[1][:, ti, vc * 128:(vc + 1) * 128], qk[2][:, sl], False, True,
                           [('SIn', 1, ti), ('qk', 2)], ['ps%d' % bo[vc]])
                for vc in range(2):
                    act(o32[:, vc, t0:t0 + tn], ps[bo[vc]][:, :tn], AF.Copy, ['ps%d' % bo[vc]], [('o32', vc, tbi)])
                release(*bo)
                rms_rstd(lambda vc: o32[:, vc, t0:t0 + tn], 2, t0, tn, lambda vc: [('o32', vc, tbi)],
                         rs2[:, t0:t0 + tn], ('rs2', tbi), 256)
                for vc in range(2):
                    stt(o32[:, vc, t0:t0 + tn], o32[:, vc, t0:t0 + tn], CF('gnorm', l * 2 + vc, 1), rs2[:, t0:t0 + tn],
                        ALU.mult, ALU.mult, [('o32', vc, tbi), ('rs2', tbi), 'cf'], [('o32', vc, tbi)])
                    tt(oG[:, h * 2 + vc, t0:t0 + tn], o32[:, vc, t0:t0 + tn], rg[:, vc, t0:t0 + tn], ALU.mult,
                       [('o32', vc, tbi), ('rg', vc, tbi)], [('oG', h * 2 + vc, tbi)])
        S.barrier()
        if KMIX == 1:
            return

        dtt = r3(f32v(W0, NT * 32), NT)
        att = r3(f32v(W0 + 1280, NT * 32), NT)
        Abc = f32v(W0 + 2560, 32)
        abc = r3(f32v(W0 + 2688, 1024), 8)
        BtA = r3(bfv(W0 + 47024, NT * 128), NT)
        ahi = r3(bfv(W0 + 2688, 512), 4)
        alo = r3(bfv(W0 + 2688 + 1024, 512), 4)
        ntb = [bfv(W0 + 49584 + i * 256, 128) for i in range(4)]
        xpad2 = r3(f32v(W0 + 4736, 5 * 260), 5)
        XtA = r3(bfv(O_RSTD, NT * 256), NT)
        xpad = r3(f32v(W0 + 10880, 5 * 260), 5)
        cacc = r3(f32v(W0 + 16080, T), 5)
        xc = r3(f32v(W0 + 21200, 2 * T), 2)
        BT = bfv(W0 + 31440, T)
        CT = bfv(W0 + 34000, T)
        zg = r3(bfv(W0 + 36560, 2 * T), 2)
        ssacc = f32v(W0 + 41680, T)
        cumt = f32v(W0 + 46800, 32)
        wdec = f32v(W0 + 46928, 8)
        cend = f32v(W0 + 46960, 8)
        Eend = f32v(W0 + 46992, 8)
        sc2 = O_X + 40960
        xtok = f32v(sc2, 256)
        Btok = bfv(sc2 + 1024, 128)
        xdt = [bfv(sc2 + 1280 + i * 512, 256) for i in range(4)]
        segb = [bfv(sc2 + 3328 + i * 1024, 512) for i in range(2)]
        CBm = [bfv(sc2 + 5376 + i * 256, 128) for i in range(2)]
        Cdec = [bfv(sc2 + 5888 + i * 1024, 512) for i in range(2)]
        SsIn = [r3(bfv(sc2 + 7936 + d_ * 5120, NT * 256), NT) for d_ in range(2)]
        Ss32 = [f32v(sc2 + 18176 + d_ * 1024, 256) for d_ in range(2)]
        wdtb = wasl[0]
        dma('pool', wdtb[:, 0:512], wdt_d[l], [], [('wa', 0)], 'ld_wa0')
        wdt3 = r3(wdtb[:, 0:512], 16)
        act(Abc, CF('alog', l * 32, 32), AF.Exp, ['cf'], ['Abc'])
        memset(ssacc, 0.0, ['ssacc'], eng='dve')
        for i_, nm_ in enumerate(('ntriU', 'ntriL', 'mnegF', 'mnegB')):
            cp(ntb[i_], CF(nm_), ['cf'], ['ntb'])
        for ti in range(NT):
            b = nextps()
            for kc in range(16):
                mm(ps[b][:, 0:32], hT[:, kc, ti * 128:(ti + 1) * 128], wdt3[:, kc, :], kc == 0, kc == 15,
                   [('wa', 0), ('h', kc, ti // 4)], ['ps%d' % b])
            tt(dtt[:, ti, :], ps[b][:, 0:32], CF('dtbias', l * 32, 32), ALU.add, ['ps%d' % b, 'cf'], [('dtt', ti)])
            act(dtt[:, ti, :], dtt[:, ti, :], AF.Exp, [('dtt', ti)], [('dtt', ti)])
            act(dtt[:, ti, :], dtt[:, ti, :], AF.Ln, [('dtt', ti)], [('dtt', ti)], bias=1.0, scale=1.0)
            stt(att[:, ti, :], dtt[:, ti, :], -1.0, Abc, ALU.mult, ALU.mult, [('dtt', ti), 'Abc'], [('att', ti)])
        memset(r3(f32v(W0 + 10880, 5 * 260), 5), 0.0, ['xpad'], eng='dve')
        memset(xpad2, 0.0, ['xpad'], eng='dve')
        xpads = [xpad, xpad2]
        ccnt = [0]
        for g in range(4):
            names = ['sx%d_0' % g, 'sx%d_1' % g, 'sB%d' % g, 'sC%d' % g]
            for ci, nm in enumerate(names):
                chan = [g * 2, g * 2 + 1, 8 + g, 12 + g][ci]

                kx = ccnt[0] % 2
                ccnt[0] += 1
                xp = xpads[kx]

                def epi_c(i, tbi, t0, tn, p, pk, xp=xp, kx=kx):
                    for s_ in range(tn // 256):
                        slot = t0 // 256 + s_
                        act(xp[:, slot, 2:258], p[:, s_ * 256:(s_ + 1) * 256], AF.Copy, [pk, 'xpad'], [('xpad', kx, slot)])
                proj_ws(wsl, 'wm', [wtile(nm)], 16, h_rhs, h_keys, epi_c)
                allx = [('xpad', kx, s_) for s_ in range(5)]
                hk = [('xhalo', kx), ('xhalo2', kx)]
                ts(xp[:, 1:4, 0:2], xp[:, 0:3, 256:258], fs, None, ALU.mult, None, allx + ['cf'], [hk[0]])
                ts(xp[:, 0:3, 258:260], xp[:, 1:4, 2:4], fs, None, ALU.mult, None, allx + ['cf', hk[0]], [hk[1]])
                cw = lambda j: CF('convw', l * 80 + chan * 5 + j, 1)
                ts(cacc, xp[:, :, 0:256], cw(0), CF('convb', l * 16 + chan, 1), ALU.mult, ALU.add,
                   allx + hk + ['cf'], ['cacc'])
                for j in range(1, 5):
                    stt(cacc, xp[:, :, j:j + 256], cw(j), cacc, ALU.mult, ALU.add, allx + hk + ['cacc'],
                        ['cacc'] + (allx if j == 4 else []))
                dst = [xc[:, 0, :], xc[:, 1, :], BT, CT][ci]
                act(dst, cacc.rearrange("p a b -> p (a b)"), AF.Silu, ['cacc'], [('cv', ci)])
            def epi_z(i, tbi, t0, tn, p, pk):
                act(zg[:, i, t0:t0 + tn], p[:, :tn], AF.Silu, [pk], [('zg', i, tbi)])
            proj_ws(wsl, 'wm', [wtile('sz%d_0' % g), wtile('sz%d_1' % g)], 16, h_rhs, h_keys, epi_z)
            for ti in range(NT):
                sl = slice(ti * 128, (ti + 1) * 128)
                b2 = nextps()
                for c2 in range(2):
                    mm(ps[b2][:, c2 * 128:(c2 + 1) * 128], xc[:, c2, sl], ident_f, True, True, [('cv', c2), 'cf'], ['ps%d' % b2])
                mm(ps[b2][:, 256:384], BT[:, sl], ident_b, True, True, [('cv', 2), 'cb'], ['ps%d' % b2])
                cp(XtA[:, ti, :], ps[b2][:, 0:256], ['ps%d' % b2], [('XtA', ti)], eng='act')
                cp(BtA[:, ti, :], ps[b2][:, 256:384], ['ps%d' % b2], [('BtA', ti)], eng='act')
            for step in range(NT):
                for d_ in range(2):
                    ti = step if d_ == 0 else NT - 1 - step
                    tri = CF('triU') if d_ == 0 else CF('triL')
                    slot = ti // 2
                    first = (ti % 2 == 0) if d_ == 0 else (ti % 2 == 1)
                    cu = cumt[:, d_ * 8:d_ * 8 + 8]
                    wd_ = wdec[:, d_ * 4:d_ * 4 + 4]
                    Ee = Eend[:, d_ * 4:d_ * 4 + 4]
                    if first:
                        chain = (slot in (1, 2, 3)) if d_ == 0 else (slot in (0, 1, 2))
                        has_h0 = (slot == 0) if d_ == 0 else (slot == 3)
                        if has_h0:
                            dma('sp', Ss32[d_], h0s_d[l, d_, :, g * 256:(g + 1) * 256], [], [('Ss32', d_)], 'ld_h0%d' % d_)
                        elif chain:
                            ts(Ss32[d_], Ss32[d_], fs, None, ALU.mult, None, [('Ss32', d_), 'cf'], [('Ss32', d_)])
                        else:
                            memset(Ss32[d_], 0.0, [('Ss32', d_)], eng='dve')
                    cp(SsIn[d_][:, ti, :], Ss32[d_], [('Ss32', d_)], [('SsIn', d_, ti)], eng='act')
                    a4 = att[:, ti, d_ * 16 + g * 4: d_ * 16 + g * 4 + 4]
                    b = nextps()
                    mm(ps[b][:, 0:4], tri, a4, True, True, ['cf', ('att', ti)], ['ps%d' % b])
                    mm(ps[b][:, 4:8], ones_f, a4, True, True, ['onesf', ('att', ti)], ['ps%d' % b])
                    cp(cu, ps[b][:, 0:8], ['ps%d' % b], [('cumt', d_)])
                    tt(wd_, cu[:, 4:8], cu[:, 0:4], ALU.subtract, [('cumt', d_)], [('wdec', d_)])
                    act(wd_, wd_, AF.Exp, [('wdec', d_)], [('wdec', d_)])
                    tt(wd_, wd_, dtt[:, ti, d_ * 16 + g * 4: d_ * 16 + g * 4 + 4], ALU.mult,
                       [('wdec', d_), ('dtt', ti)], [('wdec', d_)])
                    act(Ee, cu[:, 4:8], AF.Exp, [('cumt', d_)], [('Eend', d_)])
                    xd = xdt[2 + d_]
                    tt(r3(xd, 4), r3(XtA[:, ti, :], 4), wd_.unsqueeze(2).to_broadcast([128, 4, 64]), ALU.mult,
                       [('XtA', ti), ('wdec', d_)], [('xd', d_)])
                    b3 = nextps()
                    mm(ps[b3][:, 0:256], BtA[:, ti, :], xd, True, True, [('BtA', ti), ('xd', d_)], ['ps%d' % b3])
                    tt(r3(Ss32[d_], 4), r3(Ss32[d_], 4), Ee.unsqueeze(2).to_broadcast([128, 4, 64]), ALU.mult,
                       [('Ss32', d_), ('Eend', d_)], [('Ss32', d_)])
                    tt(Ss32[d_], Ss32[d_], ps[b3][:, 0:256], ALU.add, [('Ss32', d_), 'ps%d' % b3], [('Ss32', d_)])
                    last = (ti % 2 == 1) if d_ == 0 else (ti % 2 == 0)
                    if last:
                        out_toks.append(dma('sp', os_d[l, slot, d_, :, g * 256:(g + 1) * 256], Ss32[d_], [('Ss32', d_)], [], 'st_os%d' % d_))
            for tbi, (t0, tn) in enumerate(TB):
                ntl = tn // 128
                by = [nextps(hold=True), nextps(hold=True)]
                for tq in range(ntl):
                    ti = t0 // 128 + tq
                    sl = slice(ti * 128, (ti + 1) * 128)
                    for d_ in range(2):
                        tt(r3(xdt[d_], 4), r3(XtA[:, ti, :], 4),
                           dtt[:, ti, d_ * 16 + g * 4: d_ * 16 + g * 4 + 4].unsqueeze(2).to_broadcast([128, 4, 64]), ALU.mult,
                           [('XtA', ti), ('dtt', ti)], [('xdt', d_)])
                    bc = nextps()
                    mm(ps[bc][:, 0:128], BT[:, sl], CT[:, sl], True, True, [('cv', 2), ('cv', 3)], ['ps%d' % bc])
                    cp(CBm[ti % 2], ps[bc][:, 0:128], ['ps%d' % bc], [('CBm', ti % 2)], eng='act')
                    for d_ in range(2):
                        a4 = att[:, ti, d_ * 16 + g * 4: d_ * 16 + g * 4 + 4]
                        tri = CF('triU') if d_ == 0 else CF('triL')
                        ntri = CF('ntriU') if d_ == 0 else CF('ntriL')
                        mneg = CF('mnegF') if d_ == 0 else CF('mnegB')
                        a4b = a4.unsqueeze(2).to_broadcast([128, 4, 128])
                        cp(ahi, a4b, [('att', ti)], ['ahi'])
                        tt(alo, a4b, ahi, ALU.subtract, [('att', ti), 'ahi'], ['alo'])
                        triB = CB('maskFB', d_ * 128, 128)
                        ntriB = ntb[d_]
                        mnegB_ = ntb[2 + d_]
                        bd = nextps()
                        be = nextps()
                        for hh in range(4):
                            o = ps[bd][:, hh * 128:(hh + 1) * 128]
                            e_ = ps[be][:, hh * 128:(hh + 1) * 128]
                            h_ = ahi[:, hh, :]
                            l_ = alo[:, hh, :]
                            mm(o, h_, triB, True, False, ['ahi', 'cb'], ['ps%d' % bd])
                            mm(o, l_, triB, False, False, ['alo', 'cb'], ['ps%d' % bd])
                            mm(o, ntriB, h_, False, False, ['ahi', 'ntb'], ['ps%d' % bd])
                            mm(o, ntriB, l_, False, False, ['alo', 'ntb'], ['ps%d' % bd])
                            mm(o, ident_b, mnegB_, False, True, ['cb', 'ntb'], ['ps%d' % bd])
                            mm(e_, h_, triB, True, False, ['ahi', 'cb'], ['ps%d' % be])
                            mm(e_, l_, triB, False, True, ['alo', 'cb'], ['ps%d' % be])
                        sgb = segb[d_]
                        act(sgb, ps[bd][:, 0:512], AF.Exp, ['ps%d' % bd], [('seg', d_)])
                        tt(r3(sgb, 4), r3(sgb, 4), CBm[ti % 2].unsqueeze(1).to_broadcast([128, 4, 128]), ALU.mult,
                           [('seg', d_), ('CBm', ti % 2)], [('seg', d_)])
                        cd = Cdec[d_]
                        act(cd, ps[be][:, 0:512], AF.Exp, ['ps%d' % be], [('Cdec', d_)])
                        tt(r3(cd, 4), r3(cd, 4), CT[:, sl].unsqueeze(1).to_broadcast([128, 4, 128]), ALU.mult,
                           [('Cdec', d_), ('cv', 3)], [('Cdec', d_)])
                    for hh in range(4):
                        o = ps[by[hh // 2]][(hh % 2) * 64:(hh % 2) * 64 + 64, tq * 128:(tq + 1) * 128]
                        for d_ in range(2):
                            mm(o, xdt[d_][:, hh * 64:(hh + 1) * 64], segb[d_][:, hh * 128:(hh + 1) * 128], d_ == 0, False,
                               [('xdt', d_), ('seg', d_)], ['ps%d' % by[hh // 2]])
                        for d_ in range(2):
                            mm(o, SsIn[d_][:, ti, hh * 64:(hh + 1) * 64], Cdec[d_][:, hh * 128:(hh + 1) * 128], False, d_ == 1,
                               [('SsIn', d_, ti), ('Cdec', d_)], ['ps%d' % by[hh // 2]])
                for c2 in range(2):
                    ch = g * 2 + c2
                    stt(tmpf[c2][:, :tn], xc[:, c2, t0:t0 + tn], CF('ssmD', l * 8 + ch, 1), ps[by[c2]][:, :tn], ALU.mult, ALU.add,
                        [('cv', c2), 'cf', 'ps%d' % by[c2]], [('tmpf', c2)])
                    tt(oS[:, ch, t0:t0 + tn], tmpf[c2][:, :tn], zg[:, c2, t0:t0 + tn], ALU.mult,
                       [('tmpf', c2), ('zg', c2, tbi)], [('oS', ch, tbi)])
                release(*by)
                bq = nextps()
                for c2 in range(2):
                    ch = g * 2 + c2
                    act(sqb[c2][:, :tn], oS[:, ch, t0:t0 + tn], AF.Square, [('oS', ch, tbi)], [('sq', c2)])
                    mm(ps[bq][:, :tn], ones_b, sqb[c2][:, :tn], c2 == 0, c2 == 1, [('sq', c2), 'cb'], ['ps%d' % bq])
                tt(ssacc[:, t0:t0 + tn], ssacc[:, t0:t0 + tn], ps[bq][:, :tn], ALU.add, ['ps%d' % bq, 'ssacc'], ['ssacc'])
        act(ssacc, ssacc, AF.Sqrt, ['ssacc'], ['ssacc'], scale=1.0 / 1024.0, bias=1e-6)
        S.op('dve', lambda e: e.reciprocal(out=ssacc, in_=ssacc), reads=['ssacc'], writes=['ssacc'])
        for ch in range(8):
            for tbi, (t0, tn) in enumerate(TB):
                stt(oS[:, ch, t0:t0 + tn], oS[:, ch, t0:t0 + tn], CF('ssmnorm', l * 8 + ch, 1), ssacc[:, t0:t0 + tn],
                    ALU.mult, ALU.mult, [('oS', ch, tbi), 'ssacc', 'cf'], [('oS', ch, tbi)])
        S.barrier()
        if KMIX == 2:
            return

        cosT = f32v(W0, T)
        sinT = f32v(W0 + 5120, T)
        qr = r3(bfv(W0 + 10240, 4 * T), 4)
        krT = bfv(W0 + 20480, T)
        q32 = f32v(W0 + 23040, T)
        qb16 = bfv(W0 + 28160, T)
        vtk = r3(bfv(W0 + 30720, NT * 256), NT)
        v32 = [f32v(W0 + 35840 + i * 1024, 256) for i in range(2)]
        cK = bfv(W0 + 37888, 512)
        cV = r3(bfv(W0 + 38912, 512), 4)
        PT = [bfv(W0 + 39936 + i * 1024, 512) for i in range(3)]
        k32 = f32v(W0 + 43008, T)
        rden = f32v(W0 + 48128, 512)
        sinkE = f32v(W0 + 50176, 8)
        dma('sp', f32v(W0, 2 * T), rope_d, [], ['rope'], 'ld_rope')
        act(sinkE, CF('sink', l * 8, 8), AF.Exp, ['cf'], ['sinkE'])
        if KATT == 0:
            S.barrier()
            return
        dma('pool', wasl[0], wv_d[l, 4], [], [('wa', 0)], 'ld_wa0')
        wv3 = r3(wasl[0], 16)
        for ti in range(NT):
            b = nextps()
            for kc in range(16):
                mm(ps[b][:, 0:256], hT[:, kc, ti * 128:(ti + 1) * 128], wv3[:, kc, :], kc == 0, kc == 15,
                   [('wa', 0), ('h', kc, ti // 4)], ['ps%d' % b])
            act(v32[ti % 2], ps[b][:, 0:256], AF.Copy, ['ps%d' % b], [('v32', ti % 2)])
            cp(vtk[:, ti, :], v32[ti % 2], [('v32', ti % 2)], [('vtk', ti)])
            if os.environ.get('KNOV') != '1':
                out_toks.append(dma('sp', ov_d[l, ti], v32[ti % 2], [('v32', ti % 2)], [], 'st_ov%d' % (ti % 2)))
        scale = 128 ** -0.5
        if KATT == 1:
            S.barrier()
            return

        def rope(dst, srckeys_w):
            for tbi, (t0, tn) in enumerate(TB):
                b = nextps()
                mm(ps[b][:, :tn], CB('pm'), qb16[:, t0:t0 + tn], True, True, ['cb', ('qb16', tbi)], ['ps%d' % b])
                tt(tmpf[0][:, :tn], q32[:, t0:t0 + tn], cosT[:, t0:t0 + tn], ALU.mult, [('q32', tbi), 'rope'], [('tmpf', 0)])
                tt(tmpf[1][:, :tn], ps[b][:, :tn], sinT[:, t0:t0 + tn], ALU.mult, ['ps%d' % b, 'rope'], [('tmpf', 1)])
                tt(dst[:, t0:t0 + tn], tmpf[0][:, :tn], tmpf[1][:, :tn], ALU.add, [('tmpf', 0), ('tmpf', 1)], [(srckeys_w, tbi)])

        for kv in range(2):
            def epi_qk(i, tbi, t0, tn, p, pk):
                act(q32[:, t0:t0 + tn], p[:, :tn], AF.Copy, [pk], [('q32', tbi)])
                cp(qb16[:, t0:t0 + tn], q32[:, t0:t0 + tn], [('q32', tbi)], [('qb16', tbi)])
            for j in range(4):
                proj_ws(wsl, 'wm', [wtile('aq%d' % (kv * 4 + j))], 16, h_rhs, h_keys, epi_qk)
                rope(qr[:, j, :], ('qr', j))

            def epi_kk(i, tbi, t0, tn, p, pk):
                act(q32[:, t0:t0 + tn], p[:, :tn], AF.Copy, [pk], [('q32', tbi)])
                act(k32[:, t0:t0 + tn], p[:, :tn], AF.Copy, [pk], [('k32', tbi)])
                cp(qb16[:, t0:t0 + tn], q32[:, t0:t0 + tn], [('q32', tbi)], [('qb16', tbi)])
            proj_ws(wsl, 'wm', [wtile('ak%d' % kv)], 16, h_rhs, h_keys, epi_kk)
            out_toks.append(dma('sp', ok_d[l, kv], k32, [('k32', 0), ('k32', 1), ('k32', 2)], [], 'st_ok'))
            rope(krT, 'krT')
            if KATT == 2:
                S.barrier()
                return
            dma('pool', cK, ctxk_d[l, kv], [], ['cK'], 'ld_cK')
            dma('pool', cV.rearrange("p a b -> p (a b)"), ctxv_d[l, kv], [], ['cV'], 'ld_cV')
            krkeys = [('krT', 0), ('krT', 1), ('krT', 2)]
            if KATT == 3:
                S.barrier()
                return
            for ti in range(NT):
                if KATT == 4 and ti == 1:
                    S.barrier()
                    return
                sl = slice(ti * 128, (ti + 1) * 128)
                qrhs = qr[:, :, sl]
                qkeys = [(('qr', j), ti // 4) for j in range(4)]
                bo = nextps(hold=True)
                bden = nextps(hold=True)
                chunks = [('c', c) for c in range(4)] if ti < 8 else []
                if ti > 0:
                    chunks.append(('l', ti - 1))
                chunks.append(('l', ti))
                if ti < NT - 1:
                    chunks.append(('l', ti + 1))
                for ci, (kind, c) in enumerate(chunks):
                    bs = nextps()
                    P = PT[ci % 3]
                    if kind == 'c':
                        mm(ps[bs][:, 0:512], cK[:, c * 128:(c + 1) * 128], qrhs, True, True, ['cK'] + qkeys, ['ps%d' % bs])
                        act(P, ps[bs][:, 0:512], AF.Exp, ['ps%d' % bs, 'cf'], [('PT', ci % 3)], scale=scale, bias=ctxbias)
                        vl = cV[:, c, :]
                        vkeys = ['cV']
                    else:
                        mm(ps[bs][:, 0:512], krT[:, c * 128:(c + 1) * 128], qrhs, True, True, krkeys + qkeys, ['ps%d' % bs])
                        act(P, ps[bs][:, 0:512], AF.Exp, ['ps%d' % bs], [('PT', ci % 3)], scale=scale)
                        if c != ti:
                            mi = (ti - 1) if c < ti else (9 + ti)
                            mk = CB('amask', mi * 128, 128)
                            tt(r3(P, 4), r3(P, 4), mk.unsqueeze(1).to_broadcast([128, 4, 128]), ALU.mult,
                               [('PT', ci % 3), 'cb'], [('PT', ci % 3)])
                        vl = vtk[:, c, kv * 128:(kv + 1) * 128]
                        vkeys = [('vtk', c)]
                    first = ci == 0
                    lastc = ci == len(chunks) - 1
                    mm(ps[bo][:, 0:512], vl, P, first, lastc, vkeys + [('PT', ci % 3)], ['ps%d' % bo])
                    mm(ps[bden][:, 0:512], ones_b, P, first, lastc, ['cb', ('PT', ci % 3)], ['ps%d' % bden])
                tt(r3(rden, 4), r3(ps[bden][:, 0:512], 4),
                   sinkE[:, kv * 4:kv * 4 + 4].unsqueeze(2).to_broadcast([128, 4, 128]), ALU.add, ['ps%d' % bden, 'sinkE'], ['rden'])
                S.op('dve', lambda e: e.reciprocal(out=rden, in_=rden), reads=['rden'], writes=['rden'])
                tt(oA[:, kv * 4:kv * 4 + 4, sl], r3(ps[bo][:, 0:512], 4), r3(rden, 4), ALU.mult, ['ps%d' % bo, 'rden'],
                   [('oAt', kv, ti)])
                release(bo, bden)
        S.barrier()
        if KMIX == 3:
            return

        mT = r3(bfv(O_W, 16 * T), 16)
        gsl = [bfv(O_W + 40960 + i * 4096, 2048) for i in range(2)]
        bsl = [bfv(O_X + 61440 + i * 2048, 1024) for i in range(4)]
        gat = [f32v(O_X + 61440 + 8192 + i * 2048, 512) for i in range(3)]
        osrc = [oG, oS, oA]
        mt2 = f32v(O_SCR, 512)
        cnt = proj_ws.cnt
        bg_start(l, 2, bfv(O_X + 75776, 2048))
        for c in range(16):
            for b_ in range(3):
                bg_step(1)
                s = cnt['wgl'] % 2
                cnt['wgl'] += 1
                dma('pool', gsl[s], wws_d[l, WSI['br%d_%d' % (b_, c)]], [], [('wg', s)], 'ld_wg%d' % s)
                g3 = r3(gsl[s], 16)
                s2 = cnt['wbl'] % 4
                cnt['wbl'] += 1
                dma('pool', bsl[s2], wbr_d[l, b_ * 16 + c], [], [('wb', s2)], 'ld_wb%d' % s2)
                b3 = r3(bsl[s2], 8)
                for tbi, (t0, tn) in enumerate(TB):
                    bg = nextps()
                    for kc in range(16):
                        mm(ps[bg][:, :tn], g3[:, kc, :], hT[:, kc, t0:t0 + tn], kc == 0, kc == 15,
                           [('wg', s), ('h', kc, tbi)], ['ps%d' % bg])
                    act(gat[tbi][:, :tn], ps[bg][:, :tn], AF.Sigmoid, ['ps%d' % bg], [('gat', tbi)])
                    bp = nextps()
                    for kc in range(8):
                        mm(ps[bp][:, :tn], b3[:, kc, :], osrc[b_][:, kc, t0:t0 + tn], kc == 0, kc == 7,
                           [('wb', s2)], ['ps%d' % bp])
                    if b_ == 0:
                        tt(macc[tbi][:, :tn], ps[bp][:, :tn], gat[tbi][:, :tn], ALU.mult,
                           ['ps%d' % bp, ('gat', tbi)], [('macc', tbi)])
                    else:
                        tt(mt2[:, :tn], ps[bp][:, :tn], gat[tbi][:, :tn], ALU.mult,
                           ['ps%d' % bp, ('gat', tbi)], ['mt2'])
                        if b_ == 1:
                            tt(macc[tbi][:, :tn], macc[tbi][:, :tn], mt2[:, :tn], ALU.add,
                               [('macc', tbi), 'mt2'], [('macc', tbi)])
                        else:
                            tt(mT[:, c, t0:t0 + tn], macc[tbi][:, :tn], mt2[:, :tn], ALU.add,
                               [('macc', tbi), 'mt2'], [('m', c, tbi)])
        bg_flush()
        S.barrier()
        for q in range(4):
            dma('sp', xflat[:, q * 4 * T:(q + 1) * 4 * T], xsp_d[:, q * 4 * T:(q + 1) * 4 * T], ['xsp%d' % q], xkeys(q), 'ld_x%d' % q)
        wosl = [bfv(O_W + 40960 + i * 4096, 2048) for i in range(2)]

        def epi_o(i, tbi, t0, tn, p, pk):
            grp = 0 if tbi < 2 else 1
            stt(xT[:, i, t0:t0 + tn], p[:, :tn], Gcol(1, i, grp), xT[:, i, t0:t0 + tn], ALU.mult, ALU.add,
                [pk, 'Gvec', ('x', i, tbi)], [('x', i, tbi)])
        proj_ws(wosl, 'wo', [wout_d[l, i] for i in range(16)], 16, lambda kc, t0, tn: mT[:, kc, t0:t0 + tn],
                lambda kc, tbi: [('m', kc, tbi)], epi_o)
        S.barrier()

    ones_f = f32v(O_MODV + 2048, 128)
    macc = [tmpf[0], tmpf[1], f32v(O_RSTD, 512)]
    memset(ones_f, 1.0, ['onesf'], eng='dve')

    import os
    STOP = int(os.environ.get('KSTOP', '9'))
    for l in range(L):
        if STOP >= 1 and l == 0:
            mod_part_fast(0, 0)
            S.barrier()
        if STOP >= 2:
            ffn_phase(l, 0)
            S.barrier()
        if STOP >= 3:
            mix_phase(l)
        if STOP >= 4:
            ffn_phase(l, 1)
            S.barrier()

    for tbi, (t0, tn) in enumerate(TB):
        rms_rstd(lambda kc: xT[:, kc, t0:t0 + tn], 16, t0, tn, lambda kc: [('x', kc, tbi)],
                 rstd[:, t0:t0 + tn], ('rstd', tbi), D)
        for kc in range(16):
            stt(xT[:, kc, t0:t0 + tn], xT[:, kc, t0:t0 + tn], CF('fnorm', kc, 1), rstd[:, t0:t0 + tn], ALU.mult, ALU.mult,
                [('x', kc, tbi), ('rstd', tbi), 'cf'], [('x', kc, tbi)])
    for q in range(4):
        out_toks.append(dma('sp', y_d[:, q * 4 * T:(q + 1) * 4 * T], xflat[:, q * 4 * T:(q + 1) * 4 * T], xkeys(q), [], 'st_y%d' % q))
    S.final_waits('sp', out_toks)
    S.emit()
    st.close()
    return nc


def _tiles_ws(W, KC):
    K, M = W.shape
    nm = M // 128
    return np.ascontiguousarray(W.reshape(KC, 128, nm, 128).transpose(2, 1, 0, 3)).reshape(nm, 128, KC * 128)


def _tile_cols(W, c0, n, pad_to):
    blk = W[:, c0:c0 + n]
    if n < pad_to:
        blk = np.concatenate([blk, np.zeros((W.shape[0], pad_to - n), W.dtype)], axis=1)
    KC = W.shape[0] // 128
    return np.ascontiguousarray(blk.reshape(KC, 128, pad_to).transpose(1, 0, 2)).reshape(128, KC * pad_to)


def _fm(v):
    v = np.asarray(v)
    C = v.shape[-1] // 128
    lead = v.shape[:-1]
    return np.ascontiguousarray(np.moveaxis(v.reshape(lead + (C, 128)), -1, 0))


def _core_slots(c):
    if c < 2:
        return [('s', c, i) for i in range(4)] + [('p', 30 + c, 0)]
    return [('p', 5 * (c - 2) + i, 0) for i in range(5)]


def _rope_tables(slots):
    half = 32
    freqs = (10000.0 ** (-np.arange(half, dtype=np.float32) / half)).astype(np.float32)
    cosT = np.ones((128, T), np.float32)
    sinT = np.zeros((128, T), np.float32)
    for si, (kind, idx, part) in enumerate(slots):
        if kind != 's':
            continue
        tpos = part * 256 + np.arange(256)
        row = (tpos // 64).astype(np.float32)
        col = (tpos % 64).astype(np.float32)
        for d in range(128):
            pos = row if d < 64 else col
            f = freqs[d % 32]
            ang = (pos * f).astype(np.float32)
            cosT[d, si * 256:(si + 1) * 256] = np.cos(ang)
            sgn = -1.0 if (d % 64) < 32 else 1.0
            sinT[d, si * 256:(si + 1) * 256] = sgn * np.sin(ang)
    return np.concatenate([cosT, sinT], axis=1)


_PROG = {}


NLAYERS = 2
CORES = list(range(NCORES))


def kernel(**inp):
    L = NLAYERS
    f32 = np.float32
    g = {k: np.asarray(v) for k, v in inp.items()}
    cfo, NCF = cf_layout(L)
    cbo, NCB = cb_layout(L)
    ar = np.arange(128)
    ident = np.eye(128, dtype=f32)
    triU = (ar[:, None] <= ar[None, :]).astype(f32)
    triL = (ar[:, None] >= ar[None, :]).astype(f32)

    shared = {}
    shared['wmod'] = np.stack([_tiles_ws(g['w_mod'][l], 16) for l in range(L)])
    wgu = np.empty((L, 2, 88, 128, 2048), f32)
    wd = np.empty((L, 2, 22, 128, 4096), f32)
    for l in range(L):
        for w, pre in enumerate(('ffn1', 'ffn2')):
            tg = _tiles_ws(g[pre + '_w_gate'][l], 16)
            tu = _tiles_ws(g[pre + '_w_up'][l], 16)
            wgu[l, w] = np.stack([tg, tu], axis=1).reshape(88, 128, 2048)
            Wd = g[pre + '_w_down'][l]
            wd[l, w] = np.ascontiguousarray(
                Wd.reshape(11, 4, 128, 2, 8, 128).transpose(0, 3, 2, 4, 1, 5)).reshape(22, 128, 4096)
    shared['wgu'] = wgu
    shared['wd'] = wd
    wws = np.empty((L, len(WS), 128, 2048), f32)
    wv = np.empty((L, 5, 128, 4096), f32)
    wdt = np.empty((L, 128, 512), f32)
    wbr = np.empty((L, 48, 128, 1024), f32)
    wout = np.empty((L, 16, 128, 2048), f32)
    for l in range(L):
        Win = g['w_in'][l]
        for i, (nm, c0, n) in enumerate(WS):
            wws[l, i] = _tile_cols(Win, c0, n, 128)
        for h in range(4):
            wv[l, h] = _tile_cols(Win, 1024 + h * 256, 256, 256)
        wv[l, 4] = _tile_cols(Win, 7488, 256, 256)
        wdt[l] = _tile_cols(Win, 6176, 32, 32)
        for b_, nm in enumerate(('w_br_gla', 'w_br_ssm', 'w_br_attn')):
            wbr[l, b_ * 16:(b_ + 1) * 16] = _tiles_ws(g[nm][l], 8)
        wout[l] = _tiles_ws(g['w_out'][l], 16)
    shared.update(wws=wws, wv=wv, wdt=wdt, wbr=wbr, wout=wout)

    in_maps = []
    for c in CORES:
        slots = _core_slots(c)
        is_s = c < 2
        toks = []
        for (kind, idx, part) in slots:
            if kind == 's':
                toks.append(g['x_sample'][idx, part * 256:(part + 1) * 256])
            else:
                toks.append(g['x_prompt'][idx])
        x = np.concatenate(toks, axis=0)
        xin = np.ascontiguousarray(x.T.reshape(16, 128, T).transpose(1, 0, 2)).reshape(128, 16 * T)
        cf = np.zeros((128, NCF), f32)

        def put(name, arr):
            o, n = cfo[name]
            cf[:, o:o + n] = np.asarray(arr, f32).reshape(128, n)
        put('ident', ident)
        put('triU', triU)
        put('triL', triL)
        put('ntriU', -triU)
        put('ntriL', -triL)
        put('mnegF', np.where(ar[:, None] <= ar[None, :], 0.0, NEG))
        put('mnegB', np.where(ar[:, None] >= ar[None, :], 0.0, NEG))
        put('cummask', np.broadcast_to((np.arange(T) % 128 != 0).astype(f32)[None, :], (128, T)))
        put('fs', np.full((128, 1), 1.0 if is_s else 0.0))
        put('ctxbias', np.full((128, 1), 0.0 if is_s else NEG))
        cA = g['c'][c] if is_s else g['c_ctx']
        put('cvec', np.stack([_fm(cA), _fm(g['c_ctx'])], axis=2))
        nw = np.stack([np.stack([_fm(g[k][l]) for k in ('ffn1_norm', 'mix_norm', 'ffn2_norm')], axis=1) for l in range(L)], axis=1)
        put('normw', nw)
        put('fnorm', _fm(g['final_norm']))
        put('bmod', np.stack([_fm(g['b_mod'][l]) for l in range(L)], axis=1))
        put('nbup', np.stack([_fm(g['gla_b_up'][l]) for l in range(L)], axis=1))
        put('gnorm', np.stack([_fm(g['gla_norm'][l]) for l in range(L)], axis=1))
        put('convw', np.stack([_fm(g['ssm_conv_w'][l]).transpose(0, 2, 1) for l in range(L)], axis=1))
        put('convb', np.stack([_fm(g['ssm_conv_b'][l]) for l in range(L)], axis=1))
        put('ssmD', np.stack([_fm(np.repeat(g['ssm_d'][l], 64)) for l in range(L)], axis=1))
        put('ssmnorm', np.stack([_fm(g['ssm_norm'][l]) for l in range(L)], axis=1))
        put('dtbias', np.broadcast_to(g['ssm_dt_bias'][:L].reshape(1, L * 32), (128, L * 32)))
        put('alog', np.broadcast_to(g['ssm_a_log'][:L].reshape(1, L * 32), (128, L * 32)))
        put('sink', np.broadcast_to(g['attn_sink'][:L].reshape(1, L * 8), (128, L * 8)))

        cbm = np.zeros((128, NCB), f32)

        def putb(name, arr):
            o, n = cbo[name]
            cbm[:, o:o + n] = np.asarray(arr, f32).reshape(128, n)
        putb('ones', np.ones((128, 128)))
        putb('identb', ident)
        perm = np.array([d + 32 if (d % 64) < 32 else d - 32 for d in range(128)])
        pm = np.zeros((128, 128), f32)
        pm[perm, np.arange(128)] = 1.0
        putb('pm', pm)
        putb('maskFB', np.concatenate([triU, triL], axis=1))
        am = np.zeros((128, 18, 128), f32)
        ones = np.ones((128, 128), f32)
        for j in range(NT):
            kind = slots[j // 2][0]
            if j >= 1:
                if kind == 's' and slots[(j - 1) // 2][0] == 's':
                    am[:, j - 1, :] = triL
                elif kind == 'p' and (j % 2 == 1):
                    am[:, j - 1, :] = ones
            if j <= NT - 2:
                if kind == 's' and slots[(j + 1) // 2][0] == 's':
                    am[:, 9 + j, :] = triU
                elif kind == 'p' and (j % 2 == 0):
                    am[:, 9 + j, :] = ones
        putb('amask', am)
        wup = np.zeros((128, L, 2, 512), f32)
        for l in range(L):
            for d_ in range(2):
                wup[d_ * 16:(d_ + 1) * 16, l, d_, :] = g['gla_w_up'][l, d_]
        putb('wup', wup)

        h0g = np.zeros((L, 2, 4, 128, 256), f32)
        h0s = np.zeros((L, 2, 128, 1024), f32)
        ctxk = np.zeros((L, 2, 128, 512), f32)
        ctxv = np.zeros((L, 2, 128, 512), f32)
        if is_s:
            for l in range(L):
                h0g[l] = g['state_gla'][c, l]
                h0s[l] = g['state_ssm'][c, l].transpose(0, 3, 1, 2).reshape(2, 128, 1024)
                ctxk[l] = g['cache_k'][c, l].transpose(1, 2, 0)
                ctxv[l] = g['cache_v'][c, l].reshape(4, 128, 2, 128).transpose(2, 1, 0, 3).reshape(2, 128, 512)
        m = dict(xin=xin, cf=cf, cb=cbm, rope=_rope_tables(slots), h0g=h0g, h0s=h0s, ctxk=ctxk, ctxv=ctxv)
        m.update(shared)
        in_maps.append(m)

    if L not in _PROG:
        _PROG[L] = build_program(L)
    import os
    if os.environ.get('KTRACE') == '1':
        res = run_bass_kernel_spmd(_PROG[L], in_maps, core_ids=list(range(len(CORES))), trace=True)
        print('EXEC_TIME_NS', res.exec_time_ns)
    else:
        res = run_bass_kernel_spmd(_PROG[L], in_maps, core_ids=list(range(len(CORES))))
    R = res.results

    y_prompt = np.zeros((32, 256, D), f32)
    y_sample = np.zeros((2, 1024, D), f32)
    nk = np.zeros((32, L, 256, 2, 128), f32)
    nv = np.zeros((32, L, 256, 2, 128), f32)
    ng = np.zeros((32, L, 2, 4, 128, 256), f32)
    ns = np.zeros((32, L, 2, 16, 64, 128), f32)
    for ci, c in enumerate(CORES):
        r = R[ci]
        y = r['yT'].reshape(128, 16, T).transpose(2, 1, 0).reshape(T, D)
        ok = r['ok'].reshape(L, 2, 128, T)
        ov = r['ov'].reshape(L, T, 2, 128)
        og = r['og'].reshape(L, 5, 2, 4, 128, 256)
        os_ = r['os'].reshape(L, 5, 2, 128, 16, 64)
        for si, (kind, idx, part) in enumerate(_core_slots(c)):
            sl = slice(si * 256, (si + 1) * 256)
            if kind == 's':
                y_sample[idx, part * 256:(part + 1) * 256] = y[sl]
            else:
                y_prompt[idx] = y[sl]
                nk[idx] = ok[:, :, :, sl].transpose(0, 3, 1, 2)
                nv[idx] = ov[:, sl]
                ng[idx] = og[:, si]
                ns[idx] = os_[:, si].transpose(0, 1, 3, 4, 2)
    return (y_prompt, y_sample, nk, nv, ng, ns)
```

```python
import math
import contextlib
import numpy as np
import concourse.bass as bass
import concourse.mybir as mybir
from concourse.bass_utils import run_bass_kernel_spmd

F32 = mybir.dt.float32
BF16 = mybir.dt.bfloat16
AF = mybir.ActivationFunctionType
ALU = mybir.AluOpType

EPOCH = 8192
NCORES = 8
T = 1280
NT = 10
D = 2048
DFF = 5632
TB = [(0, 512), (512, 512), (1024, 256)]
NEG = -30000.0


class Sched:
    ENGS = ('pe', 'act', 'dve', 'pool', 'sp')

    def __init__(self, nc):
        self.nc = nc
        self.ops = {e: [] for e in self.ENGS}
        self.count = {e: 0 for e in self.ENGS}
        self.last_w = {}
        self.readers = {}
        self.waited = {e: {} for e in self.ENGS}
        self.dma_count = {}
        self.semnames = set()
        self.last_tok = {}

    def _need(self, eng, is_dma_consumer, tok, raw):
        semkey, val, peng, pdma = tok
        if not pdma and not is_dma_consumer and peng == eng:
            if eng == 'pe':
                return False
        return True

    def _add_wait(self, eng, waits, tok):
        semkey, val = tok[0], tok[1]
        w = self.waited[eng]
        if w.get(semkey, 0) >= val:
            return
        w[semkey] = val
        if isinstance(semkey, tuple):
            pe_, ep = semkey
            for e2 in range(ep):
                w[(pe_, e2)] = EPOCH
        for i, (k, v) in enumerate(waits):
            if k == semkey:
                waits[i] = (k, max(v, val))
                return
        waits.append((semkey, val))

    def op(self, eng, fn, reads=(), writes=(), dma=None):
        is_dma = dma is not None
        waits = []
        for k in reads:
            t = self.last_w.get(k)
            if t is not None and self._need(eng, is_dma, t, True):
                self._add_wait(eng, waits, t)
        for k in writes:
            t = self.last_w.get(k)
            if t is not None and self._need(eng, is_dma, t, False):
                self._add_wait(eng, waits, t)
            for t in self.readers.get(k, {}).values():
                if self._need(eng, is_dma, t, False):
                    self._add_wait(eng, waits, t)
        if is_dma:
            n = self.dma_count.get(dma, 0) + 1
            self.dma_count[dma] = n
            tok = (dma, 16 * n, eng, True)
            inc = (dma, 16)
            rkey = dma
        else:
            idx = self.count[eng]
            self.count[eng] = idx + 1
            semkey = (eng, idx // EPOCH)
            tok = (semkey, idx % EPOCH + 1, eng, False)
            inc = (semkey, 1)
            rkey = eng
        self.semnames.add(inc[0])
        self.last_tok[inc[0] if is_dma else eng] = tok
        for k in writes:
            self.last_w[k] = tok
            self.readers[k] = {}
        for k in reads:
            self.readers.setdefault(k, {})[rkey] = tok
        self.ops[eng].append((fn, waits, inc))
        return tok

    def barrier(self):
        toks = list(self.last_tok.values())
        for eng in self.ENGS:
            waits = []
            for t in toks:
                if (not t[3]) and t[2] == eng:
                    continue
                self._add_wait(eng, waits, t)
            if waits:
                self.ops[eng].append((None, waits, None))
        self.last_w = {}
        self.readers = {}

    def final_waits(self, eng, toks):
        waits = []
        for t in toks:
            self._add_wait(eng, waits, t)
        self.ops[eng].append((None, waits, None))

    def emit(self):
        nc = self.nc
        with contextlib.ExitStack() as st:
            sems = {}
            for i, k in enumerate(sorted(self.semnames, key=str)):
                sems[k] = st.enter_context(nc.semaphore("sm%d" % i))
            block = st.enter_context(nc.Block())

            def run(engname):
                def body(e):
                    for fn, waits, inc in self.ops[engname]:
                        for (k, v) in waits:
                            e.wait_ge(sems[k], v)
                        if fn is not None:
                            fn(e).then_inc(sems[inc[0]], inc[1])
                return body
            block.tensor(run('pe'))
            block.scalar(run('act'))
            block.vector(run('dve'))
            block.gpsimd(run('pool'))
            block.sync(run('sp'))


def _layout(items):
    off = {}
    o = 0
    for name, n in items:
        off[name] = (o, n)
        o += n
    return off, o


def cf_layout(L):
    return _layout([
        ('ident', 128), ('triU', 128), ('triL', 128), ('ntriU', 128), ('ntriL', 128), ('mnegF', 128), ('mnegB', 128),
        ('cummask', 1280), ('fs', 1), ('ctxbias', 1), ('cvec', 32),
        ('normw', L * 48), ('fnorm', 16), ('bmod', L * 144), ('nbup', L * 8), ('gnorm', L * 2),
        ('convw', L * 80), ('convb', L * 16), ('ssmD', L * 8), ('ssmnorm', L * 8),
        ('dtbias', L * 32), ('alog', L * 32), ('sink', L * 8),
    ])


def cb_layout(L):
    return _layout([
        ('ones', 128), ('identb', 128), ('pm', 128), ('maskFB', 256), ('amask', 18 * 128),
        ('wup', L * 1024),
    ])


def ws_chunks():
    ch = [('gdown', 3072, 32)]
    for h in range(4):
        ch += [('gq%d' % h, h * 128, 128), ('gk%d' % h, 512 + h * 128, 128),
               ('gr%d_0' % h, 2048 + h * 256, 128), ('gr%d_1' % h, 2048 + h * 256 + 128, 128)]
    for g in range(4):
        ch += [('sx%d_0' % g, 4128 + g * 256, 128), ('sx%d_1' % g, 4128 + g * 256 + 128, 128),
               ('sB%d' % g, 5152 + g * 128, 128), ('sC%d' % g, 5664 + g * 128, 128),
               ('sz%d_0' % g, 3104 + g * 256, 128), ('sz%d_1' % g, 3104 + g * 256 + 128, 128)]
    for kv in range(2):
        for j in range(4):
            ch.append(('aq%d' % (kv * 4 + j), 6208 + (kv * 4 + j) * 128, 128))
        ch.append(('ak%d' % kv, 7232 + kv * 128, 128))
    for b in range(3):
        for c in range(16):
            ch.append(('br%d_%d' % (b, c), 7744 + b * 2048 + c * 128, 128))
    return ch


WS = ws_chunks()
WSI = {n: i for i, (n, _, _) in enumerate(WS)}

ARENA_B = 210944
O_CF = 0
O_CB = 12288
O_RSTD = 22528
O_MODV = 27648
O_SCR = 30208
O_X = 36352
O_H = 118272
O_W = 159232


def build_program(L):
    nc = bass.Bass("TRN2", target_bir_lowering=False)
    cfo, NCF = cf_layout(L)
    cbo, NCB = cb_layout(L)
    assert NCF * 4 <= O_CB and NCB * 2 <= O_RSTD - O_CB, (NCF, NCB)

    def din(name, shape):
        return nc.dram_tensor(name, shape, F32, kind="ExternalInput").ap()

    def dout(name, shape):
        return nc.dram_tensor(name, shape, F32, kind="ExternalOutput").ap()

    xin = din("xin", [128, 16 * T])
    cf_d = din("cf", [128, NCF])
    cb_d = din("cb", [128, NCB])
    rope_d = din("rope", [128, 2 * T])
    h0g_d = din("h0g", [L, 2, 4, 128, 256])
    h0s_d = din("h0s", [L, 2, 128, 1024])
    ctxk_d = din("ctxk", [L, 2, 128, 512])
    ctxv_d = din("ctxv", [L, 2, 128, 512])
    wmod_d = din("wmod", [L, 144, 128, 2048])
    wgu_d = din("wgu", [L, 2, 88, 128, 2048])
    wd_d = din("wd", [L, 2, 22, 128, 4096])
    wws_d = din("wws", [L, len(WS), 128, 2048])
    wv_d = din("wv", [L, 5, 128, 4096])
    wdt_d = din("wdt", [L, 128, 512])
    wbr_d = din("wbr", [L, 48, 128, 1024])
    wout_d = din("wout", [L, 16, 128, 2048])
    y_d = dout("yT", [128, 16 * T])
    ok_d = dout("ok", [L, 2, 128, T])
    ov_d = dout("ov", [L, NT, 128, 256])
    og_d = dout("og", [L, 5, 2, 4, 128, 256])
    os_d = dout("os", [L, 5, 2, 128, 1024])
    xsp_d = dout("xspill", [128, 16 * T])

    st = contextlib.ExitStack()
    arena = st.enter_context(nc.sbuf_tensor("arena", [128, ARENA_B // 4], F32))
    ps = [st.enter_context(nc.psum_tensor("ps%d" % i, [128, 512], F32)) for i in range(8)]
    S = Sched(nc)
    out_toks = []

    def f32v(off, n):
        assert off % 4 == 0
        return arena[:, off // 4: off // 4 + n]

    def bfv(off, n):
        assert off % 4 == 0 and n % 2 == 0
        return arena[:, off // 4: off // 4 + n // 2].bitcast(BF16)

    def r3(ap, a):
        return ap.rearrange("p (a b) -> p a b", a=a)

    def act(out, in_, func, r, w, **kw):
        S.op('act', lambda e: e.activation(out=out, in_=in_, func=func, **kw), reads=r, writes=w)

    def tt(out, a, b, op, r, w, eng='dve'):
        S.op(eng, lambda e: e.tensor_tensor(out=out, in0=a, in1=b, op=op), reads=r, writes=w)

    def ts(out, a, s1, s2, op0, op1, r, w):
        if s2 is None:
            S.op('dve', lambda e: e.tensor_scalar(out=out, in0=a, scalar1=s1, scalar2=None, op0=op0), reads=r, writes=w)
        else:
            S.op('dve', lambda e: e.tensor_scalar(out=out, in0=a, scalar1=s1, scalar2=s2, op0=op0, op1=op1), reads=r, writes=w)

    def stt(out, a, sc, b, op0, op1, r, w):
        S.op('dve', lambda e: e.scalar_tensor_tensor(out=out, in0=a, scalar=sc, in1=b, op0=op0, op1=op1), reads=r, writes=w)

    def cp(out, in_, r, w, eng='dve'):
        if eng == 'act':
            S.op(eng, lambda e: e.activation(out=out, in_=in_, func=AF.Copy), reads=r, writes=w)
        else:
            S.op(eng, lambda e: e.tensor_copy(out=out, in_=in_), reads=r, writes=w)

    def mm(out, lhsT, rhs, start, stop, r, w):
        S.op('pe', lambda e: e.matmul(out, lhsT=lhsT, rhs=rhs, start=start, stop=stop), reads=r, writes=w)

    def dma(q, out, in_, r, w, sem):
        return S.op(q, lambda e: e.dma_start(out=out, in_=in_), reads=r, writes=w, dma=sem)

    def memset(ap, val, w, eng='pool'):
        S.op(eng, lambda e: e.memset(ap, val), writes=w)

    cf = f32v(O_CF, NCF)
    cb = bfv(O_CB, NCB)

    def CF(name, i=0, n=None):
        o, m = cfo[name]
        n = m if n is None else n
        return cf[:, o + i: o + i + n]

    def CB(name, i=0, n=None):
        o, m = cbo[name]
        n = m if n is None else n
        return cb[:, o + i: o + i + n]

    rstd = f32v(O_RSTD, T)
    modv = r3(f32v(O_MODV, 288), 144)
    Avec = f32v(O_MODV + 1152, 96)
    Gvec = f32v(O_MODV + 1152 + 384, 96)
    siluc = f32v(O_MODV + 1152 + 768, 32)
    sqb = [bfv(O_SCR + i * 1024, 512) for i in range(2)]
    tmpf = [f32v(O_SCR + 2048 + i * 2048, 512) for i in range(2)]
    xT = r3(f32v(O_X, 16 * T), 16)
    hT = r3(bfv(O_H, 16 * T), 16)
    ones_b = CB('ones')
    ident_b = CB('identb')
    ident_f = CF('ident')
    fs = CF('fs')
    ctxbias = CF('ctxbias')

    dma('sp', cf, cf_d, [], ['cf'], 'ld_cf')
    dma('pool', cb, cb_d, [], ['cb'], 'ld_cb')
    def xkeys(q):
        return [('x', kc, tb_) for kc in range(q * 4, q * 4 + 4) for tb_ in range(3)]
    xflat = f32v(O_X, 16 * T)
    for q in range(4):
        dma('sp', xflat[:, q * 4 * T:(q + 1) * 4 * T], xin[:, q * 4 * T:(q + 1) * 4 * T], [], xkeys(q), 'ld_x%d' % q)
    act(siluc, CF('cvec'), AF.Silu, ['cf'], ['siluc'])
    ts(CF('nbup'), CF('nbup'), -1.0, None, ALU.mult, None, ['cf'], ['cf'])

    psrr = [0]
    held = set()

    def nextps(hold=False):
        while True:
            b = psrr[0] % 8
            psrr[0] += 1
            if b not in held:
                break
        if hold:
            held.add(b)
        return b

    def release(*bs):
        for b in bs:
            held.discard(b)

    scb = bfv(O_CF + 12032, 32)
    scb3 = r3(scb, 16)
    cp(scb, siluc, ['siluc'], ['scb'])

    def mod_finish(l, p, b):
        bm = CF('bmod', l * 144 + p * 48, 48)
        tt(modv[:, p * 48:(p + 1) * 48, :], r3(ps[b][:, 0:96], 48), bm.unsqueeze(2).to_broadcast([128, 48, 2]), ALU.add,
           ['ps%d' % b, 'cf'], ['modv'])
        release(b)
        i = p
        nw = CF('normw', l * 48 + i * 16, 16)
        stt(r3(Avec[:, i * 32:(i + 1) * 32], 16), modv[:, (3 * i + 1) * 16:(3 * i + 2) * 16, :], 1.0,
            nw.unsqueeze(2).to_broadcast([128, 16, 2]), ALU.add, ALU.mult, ['modv', 'cf'], ['Avec'])
        ts(r3(Gvec[:, i * 32:(i + 1) * 32], 16), modv[:, (3 * i + 2) * 16:(3 * i + 3) * 16, :],
           1.0 if i == 1 else 0.5, None, ALU.mult, None, ['modv'], ['Gvec'])

    def mod_part_fast(l, p):
        NS = 6
        slots = [bfv(O_W + i * 4096, 2048) for i in range(NS)]
        b = nextps(hold=True)
        for j in range(48):
            m = p * 48 + j
            s = j % NS
            dma('pool', slots[s], wmod_d[l, m], [], [('wmm', s)], 'ld_mod%d' % s)
            w3 = r3(slots[s], 16)
            for kc in range(16):
                mm(ps[b][:, 2 * j:2 * j + 2], w3[:, kc, :], scb3[:, kc, :], kc == 0, kc == 15,
                   [('wmm', s), 'scb'], ['ps%d' % b])
        mod_finish(l, p, b)

    bg = {'pending': [], 'bank': None}

    def bg_start(l, p, slot_ap):
        bg.update(l=l, p=p, slot=slot_ap, bank=nextps(hold=True), pending=list(range(48)))

    def bg_step(n=1):
        for _ in range(n):
            if bg['bank'] is None or not bg['pending']:
                return
            j = bg['pending'].pop(0)
            m = bg['p'] * 48 + j
            b = bg['bank']
            dma('pool', bg['slot'], wmod_d[bg['l'], m], [], ['bgw'], 'ld_bgw')
            w3 = r3(bg['slot'], 16)
            for kc in range(16):
                mm(ps[b][:, 2 * j:2 * j + 2], w3[:, kc, :], scb3[:, kc, :], kc == 0, kc == 15,
                   ['bgw', 'scb'], ['ps%d' % b])

    def bg_flush():
        if bg['bank'] is None:
            return
        bg_step(48)
        b = bg['bank']
        bg['bank'] = None
        mod_finish(bg['l'], bg['p'], b)

    def Acol(i, kc, grp):
        return Avec[:, i * 32 + kc * 2 + grp: i * 32 + kc * 2 + grp + 1]

    def Bcol(i, kc, grp):
        return modv[:, 3 * i * 16 + kc, grp:grp + 1]

    def Gcol(i, kc, grp):
        return Gvec[:, i * 32 + kc * 2 + grp: i * 32 + kc * 2 + grp + 1]

    def rms_rstd(src_fn, nchunks, t0, tn, rkeys_fn, out_ap, wkey, dim):
        b = nextps()
        for kc in range(nchunks):
            sq = sqb[kc % 2]
            act(sq[:, :tn], src_fn(kc), AF.Square, rkeys_fn(kc), [('sq', kc % 2)])
            mm(ps[b][:, :tn], ones_b, sq[:, :tn], kc == 0, kc == nchunks - 1, [('sq', kc % 2), 'cb'], ['ps%d' % b])
        act(out_ap, ps[b][:, :tn], AF.Sqrt, ['ps%d' % b], [wkey], scale=1.0 / dim, bias=1e-6)
        S.op('dve', lambda e: e.reciprocal(out=out_ap, in_=out_ap), reads=[wkey], writes=[wkey])

    def norm_to_h(i):
        for tbi, (t0, tn) in enumerate(TB):
            grp = 0 if tbi < 2 else 1
            rms_rstd(lambda kc: xT[:, kc, t0:t0 + tn], 16, t0, tn, lambda kc: [('x', kc, tbi)],
                     rstd[:, t0:t0 + tn], ('rstd', tbi), D)
            for kc in range(16):
                tf = tmpf[kc % 2]
                stt(tf[:, :tn], xT[:, kc, t0:t0 + tn], Acol(i, kc, grp), rstd[:, t0:t0 + tn], ALU.mult, ALU.mult,
                    [('x', kc, tbi), ('rstd', tbi), 'Avec'], [('tmpf', kc % 2)])
                act(hT[:, kc, t0:t0 + tn], tf[:, :tn], AF.Identity, [('tmpf', kc % 2), 'modv'], [('h', kc, tbi)],
                    bias=Bcol(i, kc, grp), scale=1.0)

    def proj_ws(wslots, wkey, dram_tiles, KCn, rhs_fn, rhs_keys_fn, epi, M=128, tbs=None):
        cnt = proj_ws.cnt
        for i, dt_ in enumerate(dram_tiles):
            s = cnt[wkey] % len(wslots)
            cnt[wkey] += 1
            wt = wslots[s]
            dma('pool', wt[:, :KCn * 128], dt_, [], [(wkey, s)], 'ld_%s%d' % (wkey, s))
            w3 = r3(wt[:, :KCn * 128], KCn)
            for tbi, (t0, tn) in enumerate(TB if tbs is None else tbs):
                b = nextps()
                for kc in range(KCn):
                    mm(ps[b][:M, :tn], w3[:, kc, 0:M], rhs_fn(kc, t0, tn), kc == 0, kc == KCn - 1,
                       [(wkey, s)] + rhs_keys_fn(kc, tbi), ['ps%d' % b])
                epi(i, tbi, t0, tn, ps[b], 'ps%d' % b)
    import collections
    proj_ws.cnt = collections.defaultdict(int)

    def h_rhs(kc, t0, tn):
        return hT[:, kc, t0:t0 + tn]

    def h_keys(kc, tbi):
        return [('h', kc, tbi)]

    def ffn_phase(l, which):
        i_sub = 0 if which == 0 else 2
        norm_to_h(i_sub)
        gT = r3(bfv(O_W, 4 * T), 4)
        sg = [bfv(O_W + 10240 + i * 2560, T) for i in range(2)]
        wsl = [bfv(O_W + 15360 + i * 4096, 2048) for i in range(3)]
        wdsl = [bfv(O_W + 31744 + i * 8192, 4096) for i in range(2)]
        if which == 0:
            bg_start(l, 1, bfv(O_W + 27648, 2048))
        elif l + 1 < L:
            bg_start(l + 1, 0, bfv(O_W + 27648, 2048))
        for g in range(11):
            for j in range(4):
                hc = g * 4 + j

                def epi(i, tbi, t0, tn, p, pk, j=j):
                    if i == 0:
                        act(sg[j % 2][:, t0:t0 + tn], p[:, :tn], AF.Silu, [pk], [('sg', j % 2, tbi)])
                    else:
                        tt(gT[:, j, t0:t0 + tn], p[:, :tn], sg[j % 2][:, t0:t0 + tn], ALU.mult,
                           [pk, ('sg', j % 2, tbi)], [('g', j, tbi)])
                proj_ws(wsl, 'wf', [wgu_d[l, which, 2 * hc], wgu_d[l, which, 2 * hc + 1]], 16, h_rhs, h_keys, epi)
                bg_step(1)
            for half in range(2):
                s = proj_ws.cnt['wd'] % 2
                proj_ws.cnt['wd'] += 1
                dma('pool', wdsl[s], wd_d[l, which, g * 2 + half], [], [('wd', s)], 'ld_wd%d' % s)
                w4 = wdsl[s].rearrange("p (m k j) -> p m k j", m=8, k=4)
                for mi in range(8):
                    mc = half * 8 + mi
                    for tbi, (t0, tn) in enumerate(TB):
                        grp = 0 if tbi < 2 else 1
                        b = nextps()
                        for k in range(4):
                            mm(ps[b][:, :tn], w4[:, mi, k, :], gT[:, k, t0:t0 + tn], k == 0, k == 3,
                               [('wd', s), ('g', k, tbi)], ['ps%d' % b])
                        stt(xT[:, mc, t0:t0 + tn], ps[b][:, :tn], Gcol(i_sub, mc, grp), xT[:, mc, t0:t0 + tn],
                            ALU.mult, ALU.add, ['ps%d' % b, 'Gvec', ('x', mc, tbi)], [('x', mc, tbi)])
                bg_step(1)
        bg_flush()

    import os
    KMIX = int(os.environ.get('KMIX', '9'))
    KATT = int(os.environ.get('KATT', '99'))

    def mix_phase(l):
        norm_to_h(1)
        for q in range(4):
            dma('sp', xsp_d[:, q * 4 * T:(q + 1) * 4 * T], xflat[:, q * 4 * T:(q + 1) * 4 * T], xkeys(q), ['xsp%d' % q], 'st_x%d' % q)
        S.barrier()
        if KMIX == 0:
            for q in range(4):
                dma('sp', xflat[:, q * 4 * T:(q + 1) * 4 * T], xsp_d[:, q * 4 * T:(q + 1) * 4 * T], ['xsp%d' % q], xkeys(q), 'ld_x%d' % q)
            S.barrier()
            return
        oG = r3(bfv(O_X, 8 * T), 8)
        oS = r3(bfv(O_X + 20480, 8 * T), 8)
        oA = r3(bfv(O_X + 40960, 8 * T), 8)
        wsl = [bfv(O_X + 61440 + i * 4096, 2048) for i in range(3)]
        wasl = [bfv(O_X + 61440 + 12288, 4096)]

        def wtile(name):
            return wws_d[l, WSI[name]]

        W0 = O_W
        gdT = f32v(W0, T)
        gdb = bfv(W0 + 5120, T)
        qT = f32v(W0 + 7680, T)
        kT = f32v(W0 + 12800, T)
        rg = r3(bfv(W0 + 17920, 2 * T), 2)
        vtok = r3(bfv(W0 + 23040, NT * 256), NT)
        la = f32v(W0 + 28160, T)
        ex = f32v(W0 + 33280, T)
        qk = [bfv(W0 + 38400 + i * 2560, T) for i in range(4)]
        sc1 = O_X + 20480
        SIn = [r3(bfv(sc1 + d_ * 5120, NT * 256), NT) for d_ in range(2)]
        S32 = [f32v(sc1 + 10240 + d_ * 1024, 256) for d_ in range(2)]
        Stmps = [f32v(sc1 + 12288, 256), f32v(sc1 + 31744, 256)]
        ktok = [bfv(sc1 + 13312 + d_ * 256, 128) for d_ in range(2)]
        ABt = [bfv(sc1 + 13824 + i * 512, 256) for i in range(2)]
        o32 = r3(f32v(sc1 + 14848, 2 * T), 2)
        rs2 = f32v(sc1 + 25088, T)
        Ecol = f32v(sc1 + 30208, 32)
        cendb = f32v(sc1 + 30336, 16)

        def epi_gd(i, tbi, t0, tn, p, pk):
            cp(gdb[0:32, t0:t0 + tn], p[0:32, :tn], [pk], [('gdb', tbi)])
        proj_ws(wsl, 'wm', [wtile('gdown')], 16, h_rhs, h_keys, epi_gd, M=32)
        lnscale = math.log(128 ** -0.5)
        for h in range(4):
            def epi_q(i, tbi, t0, tn, p, pk):
                act(qT[:, t0:t0 + tn], p[:, :tn], AF.Copy, [pk], [('qT', tbi)])

            def epi_k(i, tbi, t0, tn, p, pk):
                act(kT[:, t0:t0 + tn], p[:, :tn], AF.Copy, [pk], [('kT', tbi)])

            def epi_r(i, tbi, t0, tn, p, pk):
                act(rg[:, i, t0:t0 + tn], p[:, :tn], AF.Silu, [pk], [('rg', i, tbi)])
            proj_ws(wsl, 'wm', [wtile('gq%d' % h)], 16, h_rhs, h_keys, epi_q)
            proj_ws(wsl, 'wm', [wtile('gk%d' % h)], 16, h_rhs, h_keys, epi_k)
            for d_ in range(2):
                wup = CB('wup', l * 1024 + d_ * 512 + h * 128, 128)
                nb = CF('nbup', l * 8 + d_ * 4 + h, 1)
                for tbi, (t0, tn) in enumerate(TB):
                    b = nextps()
                    mm(ps[b][:, :tn], wup[0:32, :], gdb[0:32, t0:t0 + tn], True, True, ['cb', ('gdb', tbi)], ['ps%d' % b])
                    act(ex[:, t0:t0 + tn], ps[b][:, :tn], AF.Exp, ['ps%d' % b, 'cf'], ['ex'], scale=-1.0, bias=nb)
                    act(ex[:, t0:t0 + tn], ex[:, t0:t0 + tn], AF.Ln, ['ex'], ['ex'], bias=1.0, scale=1.0)
                ts(la, ex, -1.0 / 16.0, None, ALU.mult, None, ['ex'], ['la'])
                S.op('dve', lambda e: e.tensor_tensor_scan(out=ex, data0=CF('cummask'), data1=la, initial=0.0,
                                                           op0=ALU.mult, op1=ALU.add),
                     reads=['la', 'cf'], writes=['ex'])
                la3 = r3(la, NT)
                ex3 = r3(ex, NT)
                if d_ == 1:
                    tt(la3, la3, ex3, ALU.subtract, ['la', 'ex'], ['la'])
                    cp(cendb[:, 0:NT], ex3[:, :, 127], ['ex'], ['cendb'])
                    tt(ex3, la3, cendb[:, 0:NT].unsqueeze(2).to_broadcast([128, NT, 128]), ALU.add, ['la', 'ex', 'cendb'], ['ex'])
                    ckey = 'ex'
                    ecol = ex3[:, :, 0]
                else:
                    ckey = 'ex'
                    ecol = ex3[:, :, 127]
                act(Ecol[:, d_ * 16:d_ * 16 + NT], ecol, AF.Exp, [ckey], [('Ecol', d_)])
                act(la, ex, AF.Exp, [ckey], ['la'], bias=lnscale, scale=1.0)
                tt(qk[2 * d_], qT, la, ALU.mult, ['la', ('qT', 0), ('qT', 1), ('qT', 2)], [('qk', 2 * d_)])
                act(la, ex, AF.Exp, [ckey, ('qk', 2 * d_)], ['la'], scale=-1.0)
                tt(qk[2 * d_ + 1], kT, la, ALU.mult, ['la', ('kT', 0), ('kT', 1), ('kT', 2)], [('qk', 2 * d_ + 1)])
            proj_ws(wsl, 'wm', [wtile('gr%d_0' % h), wtile('gr%d_1' % h)], 16, h_rhs, h_keys, epi_r)
            dma('pool', wasl[0], wv_d[l, h], [], [('wa', 0)], 'ld_wa0')
            wv3 = r3(wasl[0], 16)
            for tt_ in range(NT):
                b = nextps()
                for kc in range(16):
                    mm(ps[b][:, 0:256], hT[:, kc, tt_ * 128:(tt_ + 1) * 128], wv3[:, kc, :], kc == 0, kc == 15,
                       [('wa', 0), ('h', kc, tt_ // 4)], ['ps%d' % b])
                cp(vtok[:, tt_, :], ps[b][:, 0:256], ['ps%d' % b], [('vtok', tt_)])
            for step in range(NT):
              for d_ in range(2):
                    ti = step if d_ == 0 else NT - 1 - step
                    kt_ = qk[2 * d_ + 1]
                    Stmp = Stmps[d_]
                    slot = ti // 2
                    first = (ti % 2 == 0) if d_ == 0 else (ti % 2 == 1)
                    if first:
                        chain = (slot in (1, 2, 3)) if d_ == 0 else (slot in (0, 1, 2))
                        has_h0 = (slot == 0) if d_ == 0 else (slot == 3)
                        if has_h0:
                            dma('sp', S32[d_], h0g_d[l, d_, h], [], [('S32', d_)], 'ld_h0%d' % d_)
                        elif chain:
                            ts(S32[d_], S32[d_], fs, None, ALU.mult, None, [('S32', d_), 'cf'], [('S32', d_)])
                        else:
                            memset(S32[d_], 0.0, [('S32', d_)], eng='dve')
                    cp(SIn[d_][:, ti, :], S32[d_], [('S32', d_)], [('SIn', d_, ti)], eng='act')
                    b = nextps()
                    mm(ps[b][:, 0:128], kt_[:, ti * 128:(ti + 1) * 128], ident_b, True, True, [('qk', 2 * d_ + 1), 'cb'], ['ps%d' % b])
                    cp(ktok[d_], ps[b][:, 0:128], ['ps%d' % b], [('ktok', d_)])
                    b2 = nextps()
                    mm(ps[b2][:, 0:256], ktok[d_], vtok[:, ti, :], True, True, [('ktok', d_), ('vtok', ti)], ['ps%d' % b2])
                    tt(Stmp, ps[b2][:, 0:256], S32[d_], ALU.add, ['ps%d' % b2, ('S32', d_)], [('Stmp', d_)])
                    ts(S32[d_], Stmp, Ecol[:, d_ * 16 + ti:d_ * 16 + ti + 1], None, ALU.mult, None,
                       [('Stmp', d_), ('Ecol', d_)], [('S32', d_)])
                    last = (ti % 2 == 1) if d_ == 0 else (ti % 2 == 0)
                    if last:
                        out_toks.append(dma('sp', og_d[l, slot, d_, h], S32[d_], [('S32', d_)], [], 'st_og%d' % d_))
            for tbi, (t0, tn) in enumerate(TB):
                ntl = tn // 128
                bo = [nextps(hold=True), nextps(hold=True)]
                for tq in range(ntl):
                    ti = t0 // 128 + tq
                    sl = slice(ti * 128, (ti + 1) * 128)
                    b = nextps()
                    mm(ps[b][:, 0:128], qk[1][:, sl], qk[0][:, sl], True, True, [('qk', 0), ('qk', 1)], ['ps%d' % b])
                    mm(ps[b][:, 128:256], qk[3][:, sl], qk[2][:, sl], True, True, [('qk', 2), ('qk', 3)], ['ps%d' % b])
                    AB = ABt[ti % 2]
                    tt(AB, ps[b][:, 0:256], CB('maskFB'), ALU.mult, ['ps%d' % b, 'cb'], [('AB', ti % 2)])
                    for vc in range(2):
                        o = ps[bo[vc]][:, tq * 128:(tq + 1) * 128]
                        vs = vtok[:, ti, vc * 128:(vc + 1) * 128]
                        mm(o, vs, AB[:, 0:128], True, False, [('vtok', ti), ('AB', ti % 2)], ['ps%d' % bo[vc]])
                        mm(o, vs, AB[:, 128:256], False, False, [('vtok', ti), ('AB', ti % 2)], ['ps%d' % bo[vc]])
                        mm(o, SIn[0][:, ti, vc * 128:(vc + 1) * 128], qk[0][:, sl], False, False,
                           [('SIn', 0, ti), ('qk', 0)], ['ps%d' % bo[vc]])
                        mm(o, SIn[1][:, ti, vc * 128:(vc + 1) * 128], qk[2][:, sl], False, True,
                           [('SIn', 1, ti), ('qk', 2)], ['ps%d' % bo[vc]])
                for vc in range(2):
                    act(o32[:, vc, t0:t0 + tn], ps[bo[vc]][:, :tn], AF.Copy, ['ps%d' % bo[vc]], [('o32', vc, tbi)])
                release(*bo)
                rms_rstd(lambda vc: o32[:, vc, t0:t0 + tn], 2, t0, tn, lambda vc: [('o32', vc, tbi)],
                         rs2[:, t0:t0 + tn], ('rs2', tbi), 256)
                for vc in range(2):
                    stt(o32[:, vc, t0:t0 + tn], o32[:, vc, t0:t0 + tn], CF('gnorm', l * 2 + vc, 1), rs2[:, t0:t0 + tn],
                        ALU.mult, ALU.mult, [('o32', vc, tbi), ('rs2', tbi), 'cf'], [('o32', vc, tbi)])
                    tt(oG[:, h * 2 + vc, t0:t0 + tn], o32[:, vc, t0:t0 + tn], rg[:, vc, t0:t0 + tn], ALU.mult,
                       [('o32', vc, tbi), ('rg', vc, tbi)], [('oG', h * 2 + vc, tbi)])
        S.barrier()
        if KMIX == 1:
            return

        dtt = r3(f32v(W0, NT * 32), NT)
        att = r3(f32v(W0 + 1280, NT * 32), NT)
        Abc = f32v(W0 + 2560, 32)
        abc = r3(f32v(W0 + 2688, 1024), 8)
        BtA = r3(bfv(W0 + 47024, NT * 128), NT)
        ahi = r3(bfv(W0 + 2688, 512), 4)
        alo = r3(bfv(W0 + 2688 + 1024, 512), 4)
        ntb = [bfv(W0 + 49584 + i * 256, 128) for i in range(4)]
        xpad2 = r3(f32v(W0 + 4736, 5 * 260), 5)
        XtA = r3(bfv(O_RSTD, NT * 256), NT)
        xpad = r3(f32v(W0 + 10880, 5 * 260), 5)
        cacc = r3(f32v(W0 + 16080, T), 5)
        xc = r3(f32v(W0 + 21200, 2 * T), 2)
        BT = bfv(W0 + 31440, T)
        CT = bfv(W0 + 34000, T)
        zg = r3(bfv(W0 + 36560, 2 * T), 2)
        ssacc = f32v(W0 + 41680, T)
        cumt = f32v(W0 + 46800, 32)
        wdec = f32v(W0 + 46928, 8)
        cend = f32v(W0 + 46960, 8)
        Eend = f32v(W0 + 46992, 8)
        sc2 = O_X + 40960
        xtok = f32v(sc2, 256)
        Btok = bfv(sc2 + 1024, 128)
        xdt = [bfv(sc2 + 1280 + i * 512, 256) for i in range(4)]
        segb = [bfv(sc2 + 3328 + i * 1024, 512) for i in range(2)]
        CBm = [bfv(sc2 + 5376 + i * 256, 128) for i in range(2)]
        Cdec = [bfv(sc2 + 5888 + i * 1024, 512) for i in range(2)]
        SsIn = [r3(bfv(sc2 + 7936 + d_ * 5120, NT * 256), NT) for d_ in range(2)]
        Ss32 = [f32v(sc2 + 18176 + d_ * 1024, 256) for d_ in range(2)]
        wdtb = wasl[0]
        dma('pool', wdtb[:, 0:512], wdt_d[l], [], [('wa', 0)], 'ld_wa0')
        wdt3 = r3(wdtb[:, 0:512], 16)
        act(Abc, CF('alog', l * 32, 32), AF.Exp, ['cf'], ['Abc'])
        memset(ssacc, 0.0, ['ssacc'], eng='dve')
        for i_, nm_ in enumerate(('ntriU', 'ntriL', 'mnegF', 'mnegB')):
            cp(ntb[i_], CF(nm_), ['cf'], ['ntb'])
        for ti in range(NT):
            b = nextps()
            for kc in range(16):
                mm(ps[b][:, 0:32], hT[:, kc, ti * 128:(ti + 1) * 128], wdt3[:, kc, :], kc == 0, kc == 15,
                   [('wa', 0), ('h', kc, ti // 4)], ['ps%d' % b])
            tt(dtt[:, ti, :], ps[b][:, 0:32], CF('dtbias', l * 32, 32), ALU.add, ['ps%d' % b, 'cf'], [('dtt', ti)])
            act(dtt[:, ti, :], dtt[:, ti, :], AF.Exp, [('dtt', ti)], [('dtt', ti)])
            act(dtt[:, ti, :], dtt[:, ti, :], AF.Ln, [('dtt', ti)], [('dtt', ti)], bias=1.0, scale=1.0)
            stt(att[:, ti, :], dtt[:, ti, :], -1.0, Abc, ALU.mult, ALU.mult, [('dtt', ti), 'Abc'], [('att', ti)])
        memset(r3(f32v(W0 + 10880, 5 * 260), 5), 0.0, ['xpad'], eng='dve')
        memset(xpad2, 0.0, ['xpad'], eng='dve')
        xpads = [xpad, xpad2]
        ccnt = [0]
        for g in range(4):
            names = ['sx%d_0' % g, 'sx%d_1' % g, 'sB%d' % g, 'sC%d' % g]
            for ci, nm in enumerate(names):
                chan = [g * 2, g * 2 + 1, 8 + g, 12 + g][ci]

                kx = ccnt[0] % 2
                ccnt[0] += 1
                xp = xpads[kx]

                def epi_c(i, tbi, t0, tn, p, pk, xp=xp, kx=kx):
                    for s_ in range(tn // 256):
                        slot = t0 // 256 + s_
                        act(xp[:, slot, 2:258], p[:, s_ * 256:(s_ + 1) * 256], AF.Copy, [pk, 'xpad'], [('xpad', kx, slot)])
                proj_ws(wsl, 'wm', [wtile(nm)], 16, h_rhs, h_keys, epi_c)
                allx = [('xpad', kx, s_) for s_ in range(5)]
                hk = [('xhalo', kx), ('xhalo2', kx)]
                ts(xp[:, 1:4, 0:2], xp[:, 0:3, 256:258], fs, None, ALU.mult, None, allx + ['cf'], [hk[0]])
                ts(xp[:, 0:3, 258:260], xp[:, 1:4, 2:4], fs, None, ALU.mult, None, allx + ['cf', hk[0]], [hk[1]])
                cw = lambda j: CF('convw', l * 80 + chan * 5 + j, 1)
                ts(cacc, xp[:, :, 0:256], cw(0), CF('convb', l * 16 + chan, 1), ALU.mult, ALU.add,
                   allx + hk + ['cf'], ['cacc'])
                for j in range(1, 5):
                    stt(cacc, xp[:, :, j:j + 256], cw(j), cacc, ALU.mult, ALU.add, allx + hk + ['cacc'],
                        ['cacc'] + (allx if j == 4 else []))
                dst = [xc[:, 0, :], xc[:, 1, :], BT, CT][ci]
                act(dst, cacc.rearrange("p a b -> p (a b)"), AF.Silu, ['cacc'], [('cv', ci)])
            def epi_z(i, tbi, t0, tn, p, pk):
                act(zg[:, i, t0:t0 + tn], p[:, :tn], AF.Silu, [pk], [('zg', i, tbi)])
            proj_ws(wsl, 'wm', [wtile('sz%d_0' % g), wtile('sz%d_1' % g)], 16, h_rhs, h_keys, epi_z)
            for ti in range(NT):
                sl = slice(ti * 128, (ti + 1) * 128)
                b2 = nextps()
                for c2 in range(2):
                    mm(ps[b2][:, c2 * 128:(c2 + 1) * 128], xc[:, c2, sl], ident_f, True, True, [('cv', c2), 'cf'], ['ps%d' % b2])
                mm(ps[b2][:, 256:384], BT[:, sl], ident_b, True, True, [('cv', 2), 'cb'], ['ps%d' % b2])
                cp(XtA[:, ti, :], ps[b2][:, 0:256], ['ps%d' % b2], [('XtA', ti)], eng='act')
                cp(BtA[:, ti, :], ps[b2][:, 256:384], ['ps%d' % b2], [('BtA', ti)], eng='act')
            for step in range(NT):
                for d_ in range(2):
                    ti = step if d_ == 0 else NT - 1 - step
                    tri = CF('triU') if d_ == 0 else CF('triL')
                    slot = ti // 2
                    first = (ti % 2 == 0) if d_ == 0 else (ti % 2 == 1)
                    cu = cumt[:, d_ * 8:d_ * 8 + 8]
                    wd_ = wdec[:, d_ * 4:d_ * 4 + 4]
                    Ee = Eend[:, d_ * 4:d_ * 4 + 4]
                    if first:
                        chain = (slot in (1, 2, 3)) if d_ == 0 else (slot in (0, 1, 2))
                        has_h0 = (slot == 0) if d_ == 0 else (slot == 3)
                        if has_h0:
                            dma('sp', Ss32[d_], h0s_d[l, d_, :, g * 256:(g + 1) * 256], [], [('Ss32', d_)], 'ld_h0%d' % d_)
                        elif chain:
                            ts(Ss32[d_], Ss32[d_], fs, None, ALU.mult, None, [('Ss32', d_), 'cf'], [('Ss32', d_)])
                        else:
                            memset(Ss32[d_], 0.0, [('Ss32', d_)], eng='dve')
                    cp(SsIn[d_][:, ti, :], Ss32[d_], [('Ss32', d_)], [('SsIn', d_, ti)], eng='act')
                    a4 = att[:, ti, d_ * 16 + g * 4: d_ * 16 + g * 4 + 4]
                    b = nextps()
                    mm(ps[b][:, 0:4], tri, a4, True, True, ['cf', ('att', ti)], ['ps%d' % b])
                    mm(ps[b][:, 4:8], ones_f, a4, True, True, ['onesf', ('att', ti)], ['ps%d' % b])
                    cp(cu, ps[b][:, 0:8], ['ps%d' % b], [('cumt', d_)])
                    tt(wd_, cu[:, 4:8], cu[:, 0:4], ALU.subtract, [('cumt', d_)], [('wdec', d_)])
                    act(wd_, wd_, AF.Exp, [('wdec', d_)], [('wdec', d_)])
                    tt(wd_, wd_, dtt[:, ti, d_ * 16 + g * 4: d_ * 16 + g * 4 + 4], ALU.mult,
                       [('wdec', d_), ('dtt', ti)], [('wdec', d_)])
                    act(Ee, cu[:, 4:8], AF.Exp, [('cumt', d_)], [('Eend', d_)])
                    xd = xdt[2 + d_]
                    tt(r3(xd, 4), r3(XtA[:, ti, :], 4), wd_.unsqueeze(2).to_broadcast([128, 4, 64]), ALU.mult,
                       [('XtA', ti), ('wdec', d_)], [('xd', d_)])
                    b3 = nextps()
                    mm(ps[b3][:, 0:256], BtA[:, ti, :], xd, True, True, [('BtA', ti), ('xd', d_)], ['ps%d' % b3])
                    tt(r3(Ss32[d_], 4), r3(Ss32[d_], 4), Ee.unsqueeze(2).to_broadcast([128, 4, 64]), ALU.mult,
                       [('Ss32', d_), ('Eend', d_)], [('Ss32', d_)])
                    tt(Ss32[d_], Ss32[d_], ps[b3][:, 0:256], ALU.add, [('Ss32', d_), 'ps%d' % b3], [('Ss32', d_)])
                    last = (ti % 2 == 1) if d_ == 0 else (ti % 2 == 0)
                    if last:
                        out_toks.append(dma('sp', os_d[l, slot, d_, :, g * 256:(g + 1) * 256], Ss32[d_], [('Ss32', d_)], [], 'st_os%d' % d_))
            for tbi, (t0, tn) in enumerate(TB):
                ntl = tn // 128
                by = [nextps(hold=True), nextps(hold=True)]
                for tq in range(ntl):
                    ti = t0 // 128 + tq
                    sl = slice(ti * 128, (ti + 1) * 128)
                    for d_ in range(2):
                        tt(r3(xdt[d_], 4), r3(XtA[:, ti, :], 4),
                           dtt[:, ti, d_ * 16 + g * 4: d_ * 16 + g * 4 + 4].unsqueeze(2).to_broadcast([128, 4, 64]), ALU.mult,
                           [('XtA', ti), ('dtt', ti)], [('xdt', d_)])
                    bc = nextps()
                    mm(ps[bc][:, 0:128], BT[:, sl], CT[:, sl], True, True, [('cv', 2), ('cv', 3)], ['ps%d' % bc])
                    cp(CBm[ti % 2], ps[bc][:, 0:128], ['ps%d' % bc], [('CBm', ti % 2)], eng='act')
                    for d_ in range(2):
                        a4 = att[:, ti, d_ * 16 + g * 4: d_ * 16 + g * 4 + 4]
                        tri = CF('triU') if d_ == 0 else CF('triL')
                        ntri = CF('ntriU') if d_ == 0 else CF('ntriL')
                        mneg = CF('mnegF') if d_ == 0 else CF('mnegB')
                        a4b = a4.unsqueeze(2).to_broadcast([128, 4, 128])
                        cp(ahi, a4b, [('att', ti)], ['ahi'])
                        tt(alo, a4b, ahi, ALU.subtract, [('att', ti), 'ahi'], ['alo'])
                        triB = CB('maskFB', d_ * 128, 128)
                        ntriB = ntb[d_]
                        mnegB_ = ntb[2 + d_]
                        bd = nextps()
                        be = nextps()
                        for hh in range(4):
                            o = ps[bd][:, hh * 128:(hh + 1) * 128]
                            e_ = ps[be][:, hh * 128:(hh + 1) * 128]
                            h_ = ahi[:, hh, :]
                            l_ = alo[:, hh, :]
                            mm(o, h_, triB, True, False, ['ahi', 'cb'], ['ps%d' % bd])
                            mm(o, l_, triB, False, False, ['alo', 'cb'], ['ps%d' % bd])
                            mm(o, ntriB, h_, False, False, ['ahi', 'ntb'], ['ps%d' % bd])
                            mm(o, ntriB, l_, False, False, ['alo', 'ntb'], ['ps%d' % bd])
                            mm(o, ident_b, mnegB_, False, True, ['cb', 'ntb'], ['ps%d' % bd])
                            mm(e_, h_, triB, True, False, ['ahi', 'cb'], ['ps%d' % be])
                            mm(e_, l_, triB, False, True, ['alo', 'cb'], ['ps%d' % be])
                        sgb = segb[d_]
                        act(sgb, ps[bd][:, 0:512], AF.Exp, ['ps%d' % bd], [('seg', d_)])
                        tt(r3(sgb, 4), r3(sgb, 4), CBm[ti % 2].unsqueeze(1).to_broadcast([128, 4, 128]), ALU.mult,
                           [('seg', d_), ('CBm', ti % 2)], [('seg', d_)])
                        cd = Cdec[d_]
                        act(cd, ps[be][:, 0:512], AF.Exp, ['ps%d' % be], [('Cdec', d_)])
                        tt(r3(cd, 4), r3(cd, 4), CT[:, sl].unsqueeze(1).to_broadcast([128, 4, 128]), ALU.mult,
                           [('Cdec', d_), ('cv', 3)], [('Cdec', d_)])
                    for hh in range(4):
                        o = ps[by[hh // 2]][(hh % 2) * 64:(hh % 2) * 64 + 64, tq * 128:(tq + 1) * 128]
                        for d_ in range(2):
                            mm(o, xdt[d_][:, hh * 64:(hh + 1) * 64], segb[d_][:, hh * 128:(hh + 1) * 128], d_ == 0, False,
                               [('xdt', d_), ('seg', d_)], ['ps%d' % by[hh // 2]])
                        for d_ in range(2):
                            mm(o, SsIn[d_][:, ti, hh * 64:(hh + 1) * 64], Cdec[d_][:, hh * 128:(hh + 1) * 128], False, d_ == 1,
                               [('SsIn', d_, ti), ('Cdec', d_)], ['ps%d' % by[hh // 2]])
                for c2 in range(2):
                    ch = g * 2 + c2
                    stt(tmpf[c2][:, :tn], xc[:, c2, t0:t0 + tn], CF('ssmD', l * 8 + ch, 1), ps[by[c2]][:, :tn], ALU.mult, ALU.add,
                        [('cv', c2), 'cf', 'ps%d' % by[c2]], [('tmpf', c2)])
                    tt(oS[:, ch, t0:t0 + tn], tmpf[c2][:, :tn], zg[:, c2, t0:t0 + tn], ALU.mult,
                       [('tmpf', c2), ('zg', c2, tbi)], [('oS', ch, tbi)])
                release(*by)
                bq = nextps()
                for c2 in range(2):
                    ch = g * 2 + c2
                    act(sqb[c2][:, :tn], oS[:, ch, t0:t0 + tn], AF.Square, [('oS', ch, tbi)], [('sq', c2)])
                    mm(ps[bq][:, :tn], ones_b, sqb[c2][:, :tn], c2 == 0, c2 == 1, [('sq', c2), 'cb'], ['ps%d' % bq])
                tt(ssacc[:, t0:t0 + tn], ssacc[:, t0:t0 + tn], ps[bq][:, :tn], ALU.add, ['ps%d' % bq, 'ssacc'], ['ssacc'])
        act(ssacc, ssacc, AF.Sqrt, ['ssacc'], ['ssacc'], scale=1.0 / 1024.0, bias=1e-6)
        S.op('dve', lambda e: e.reciprocal(out=ssacc, in_=ssacc), reads=['ssacc'], writes=['ssacc'])
        for ch in range(8):
            for tbi, (t0, tn) in enumerate(TB):
                stt(oS[:, ch, t0:t0 + tn], oS[:, ch, t0:t0 + tn], CF('ssmnorm', l * 8 + ch, 1), ssacc[:, t0:t0 + tn],
                    ALU.mult, ALU.mult, [('oS', ch, tbi), 'ssacc', 'cf'], [('oS', ch, tbi)])
        S.barrier()
        if KMIX == 2:
            return

        cosT = f32v(W0, T)
        sinT = f32v(W0 + 5120, T)
        qr = r3(bfv(W0 + 10240, 4 * T), 4)
        krT = bfv(W0 + 20480, T)
        q32 = f32v(W0 + 23040, T)
        qb16 = bfv(W0 + 28160, T)
        vtk = r3(bfv(W0 + 30720, NT * 256), NT)
        v32 = [f32v(W0 + 35840 + i * 1024, 256) for i in range(2)]
        cK = bfv(W0 + 37888, 512)
        cV = r3(bfv(W0 + 38912, 512), 4)
        PT = [bfv(W0 + 39936 + i * 1024, 512) for i in range(3)]
        k32 = f32v(W0 + 43008, T)
        rden = f32v(W0 + 48128, 512)
        sinkE = f32v(W0 + 50176, 8)
        dma('sp', f32v(W0, 2 * T), rope_d, [], ['rope'], 'ld_rope')
        act(sinkE, CF('sink', l * 8, 8), AF.Exp, ['cf'], ['sinkE'])
        if KATT == 0:
            S.barrier()
            return
        dma('pool', wasl[0], wv_d[l, 4], [], [('wa', 0)], 'ld_wa0')
        wv3 = r3(wasl[0], 16)
        for ti in range(NT):
            b = nextps()
            for kc in range(16):
                mm(ps[b][:, 0:256], hT[:, kc, ti * 128:(ti + 1) * 128], wv3[:, kc, :], kc == 0, kc == 15,
                   [('wa', 0), ('h', kc, ti // 4)], ['ps%d' % b])
            act(v32[ti % 2], ps[b][:, 0:256], AF.Copy, ['ps%d' % b], [('v32', ti % 2)])
            cp(vtk[:, ti, :], v32[ti % 2], [('v32', ti % 2)], [('vtk', ti)])
            if os.environ.get('KNOV') != '1':
                out_toks.append(dma('sp', ov_d[l, ti], v32[ti % 2], [('v32', ti % 2)], [], 'st_ov%d' % (ti % 2)))
        scale = 128 ** -0.5
        if KATT == 1:
            S.barrier()
            return

        def rope(dst, srckeys_w):
            for tbi, (t0, tn) in enumerate(TB):
                b = nextps()
                mm(ps[b][:, :tn], CB('pm'), qb16[:, t0:t0 + tn], True, True, ['cb', ('qb16', tbi)], ['ps%d' % b])
                tt(tmpf[0][:, :tn], q32[:, t0:t0 + tn], cosT[:, t0:t0 + tn], ALU.mult, [('q32', tbi), 'rope'], [('tmpf', 0)])
                tt(tmpf[1][:, :tn], ps[b][:, :tn], sinT[:, t0:t0 + tn], ALU.mult, ['ps%d' % b, 'rope'], [('tmpf', 1)])
                tt(dst[:, t0:t0 + tn], tmpf[0][:, :tn], tmpf[1][:, :tn], ALU.add, [('tmpf', 0), ('tmpf', 1)], [(srckeys_w, tbi)])

        for kv in range(2):
            def epi_qk(i, tbi, t0, tn, p, pk):
                act(q32[:, t0:t0 + tn], p[:, :tn], AF.Copy, [pk], [('q32', tbi)])
                cp(qb16[:, t0:t0 + tn], q32[:, t0:t0 + tn], [('q32', tbi)], [('qb16', tbi)])
            for j in range(4):
                proj_ws(wsl, 'wm', [wtile('aq%d' % (kv * 4 + j))], 16, h_rhs, h_keys, epi_qk)
                rope(qr[:, j, :], ('qr', j))

            def epi_kk(i, tbi, t0, tn, p, pk):
                act(q32[:, t0:t0 + tn], p[:, :tn], AF.Copy, [pk], [('q32', tbi)])
                act(k32[:, t0:t0 + tn], p[:, :tn], AF.Copy, [pk], [('k32', tbi)])
                cp(qb16[:, t0:t0 + tn], q32[:, t0:t0 + tn], [('q32', tbi)], [('qb16', tbi)])
            proj_ws(wsl, 'wm', [wtile('ak%d' % kv)], 16, h_rhs, h_keys, epi_kk)
            out_toks.append(dma('sp', ok_d[l, kv], k32, [('k32', 0), ('k32', 1), ('k32', 2)], [], 'st_ok'))
            rope(krT, 'krT')
            if KATT == 2:
                S.barrier()
                return
            dma('pool', cK, ctxk_d[l, kv], [], ['cK'], 'ld_cK')
            dma('pool', cV.rearrange("p a b -> p (a b)"), ctxv_d[l, kv], [], ['cV'], 'ld_cV')
            krkeys = [('krT', 0), ('krT', 1), ('krT', 2)]
            if KATT == 3:
                S.barrier()
                return
            for ti in range(NT):
                if KATT == 4 and ti == 1:
                    S.barrier()
                    return
                sl = slice(ti * 128, (ti + 1) * 128)
                qrhs = qr[:, :, sl]
                qkeys = [(('qr', j), ti // 4) for j in range(4)]
                bo = nextps(hold=True)
                bden = nextps(hold=True)
                chunks = [('c', c) for c in range(4)] if ti < 8 else []
                if ti > 0:
                    chunks.append(('l', ti - 1))
                chunks.append(('l', ti))
                if ti < NT - 1:
                    chunks.append(('l', ti + 1))
                for ci, (kind, c) in enumerate(chunks):
                    bs = nextps()
                    P = PT[ci % 3]
                    if kind == 'c':
                        mm(ps[bs][:, 0:512], cK[:, c * 128:(c + 1) * 128], qrhs, True, True, ['cK'] + qkeys, ['ps%d' % bs])
                        act(P, ps[bs][:, 0:512], AF.Exp, ['ps%d' % bs, 'cf'], [('PT', ci % 3)], scale=scale, bias=ctxbias)
                        vl = cV[:, c, :]
                        vkeys = ['cV']
                    else:
                        mm(ps[bs][:, 0:512], krT[:, c * 128:(c + 1) * 128], qrhs, True, True, krkeys + qkeys, ['ps%d' % bs])
                        act(P, ps[bs][:, 0:512], AF.Exp, ['ps%d' % bs], [('PT', ci % 3)], scale=scale)
                        if c != ti:
                            mi = (ti - 1) if c < ti else (9 + ti)
                            mk = CB('amask', mi * 128, 128)
                            tt(r3(P, 4), r3(P, 4), mk.unsqueeze(1).to_broadcast([128, 4, 128]), ALU.mult,
                               [('PT', ci % 3), 'cb'], [('PT', ci % 3)])
                        vl = vtk[:, c, kv * 128:(kv + 1) * 128]
                        vkeys = [('vtk', c)]
                    first = ci == 0
                    lastc = ci == len(chunks) - 1
                    mm(ps[bo][:, 0:512], vl, P, first, lastc, vkeys + [('PT', ci % 3)], ['ps%d' % bo])
                    mm(ps[bden][:, 0:512], ones_b, P, first, lastc, ['cb', ('PT', ci % 3)], ['ps%d' % bden])
                tt(r3(rden, 4), r3(ps[bden][:, 0:512], 4),
                   sinkE[:, kv * 4:kv * 4 + 4].unsqueeze(2).to_broadcast([128, 4, 128]), ALU.add, ['ps%d' % bden, 'sinkE'], ['rden'])
                S.op('dve', lambda e: e.reciprocal(out=rden, in_=rden), reads=['rden'], writes=['rden'])
                tt(oA[:, kv * 4:kv * 4 + 4, sl], r3(ps[bo][:, 0:512], 4), r3(rden, 4), ALU.mult, ['ps%d' % bo, 'rden'],
                   [('oAt', kv, ti)])
                release(bo, bden)
        S.barrier()
        if KMIX == 3:
            return

        mT = r3(bfv(O_W, 16 * T), 16)
        gsl = [bfv(O_W + 40960 + i * 4096, 2048) for i in range(2)]
        bsl = [bfv(O_X + 61440 + i * 2048, 1024) for i in range(4)]
        gat = [f32v(O_X + 61440 + 8192 + i * 2048, 512) for i in range(3)]
        osrc = [oG, oS, oA]
        mt2 = f32v(O_SCR, 512)
        cnt = proj_ws.cnt
        bg_start(l, 2, bfv(O_X + 75776, 2048))
        for c in range(16):
            for b_ in range(3):
                bg_step(1)
                s = cnt['wgl'] % 2
                cnt['wgl'] += 1
                dma('pool', gsl[s], wws_d[l, WSI['br%d_%d' % (b_, c)]], [], [('wg', s)], 'ld_wg%d' % s)
                g3 = r3(gsl[s], 16)
                s2 = cnt['wbl'] % 4
                cnt['wbl'] += 1
                dma('pool', bsl[s2], wbr_d[l, b_ * 16 + c], [], [('wb', s2)], 'ld_wb%d' % s2)
                b3 = r3(bsl[s2], 8)
                for tbi, (t0, tn) in enumerate(TB):
                    bg = nextps()
                    for kc in range(16):
                        mm(ps[bg][:, :tn], g3[:, kc, :], hT[:, kc, t0:t0 + tn], kc == 0, kc == 15,
                           [('wg', s), ('h', kc, tbi)], ['ps%d' % bg])
                    act(gat[tbi][:, :tn], ps[bg][:, :tn], AF.Sigmoid, ['ps%d' % bg], [('gat', tbi)])
                    bp = nextps()
                    for kc in range(8):
                        mm(ps[bp][:, :tn], b3[:, kc, :], osrc[b_][:, kc, t0:t0 + tn], kc == 0, kc == 7,
                           [('wb', s2)], ['ps%d' % bp])
                    if b_ == 0:
                        tt(macc[tbi][:, :tn], ps[bp][:, :tn], gat[tbi][:, :tn], ALU.mult,
                           ['ps%d' % bp, ('gat', tbi)], [('macc', tbi)])
                    else:
                        tt(mt2[:, :tn], ps[bp][:, :tn], gat[tbi][:, :tn], ALU.mult,
                           ['ps%d' % bp, ('gat', tbi)], ['mt2'])
                        if b_ == 1:
                            tt(macc[tbi][:, :tn], macc[tbi][:, :tn], mt2[:, :tn], ALU.add,
                               [('macc', tbi), 'mt2'], [('macc', tbi)])
                        else:
                            tt(mT[:, c, t0:t0 + tn], macc[tbi][:, :tn], mt2[:, :tn], ALU.add,
                               [('macc', tbi), 'mt2'], [('m', c, tbi)])
        bg_flush()
        S.barrier()
        for q in range(4):
            dma('sp', xflat[:, q * 4 * T:(q + 1) * 4 * T], xsp_d[:, q * 4 * T:(q + 1) * 4 * T], ['xsp%d' % q], xkeys(q), 'ld_x%d' % q)
        wosl = [bfv(O_W + 40960 + i * 4096, 2048) for i in range(2)]

        def epi_o(i, tbi, t0, tn, p, pk):
            grp = 0 if tbi < 2 else 1
            stt(xT[:, i, t0:t0 + tn], p[:, :tn], Gcol(1, i, grp), xT[:, i, t0:t0 + tn], ALU.mult, ALU.add,
                [pk, 'Gvec', ('x', i, tbi)], [('x', i, tbi)])
        proj_ws(wosl, 'wo', [wout_d[l, i] for i in range(16)], 16, lambda kc, t0, tn: mT[:, kc, t0:t0 + tn],
                lambda kc, tbi: [('m', kc, tbi)], epi_o)
        S.barrier()

    ones_f = f32v(O_MODV + 2048, 128)
    macc = [tmpf[0], tmpf[1], f32v(O_RSTD, 512)]
    memset(ones_f, 1.0, ['onesf'], eng='dve')

    import os
    STOP = int(os.environ.get('KSTOP', '9'))
    for l in range(L):
        if STOP >= 1 and l == 0:
            mod_part_fast(0, 0)
            S.barrier()
        if STOP >= 2:
            ffn_phase(l, 0)
            S.barrier()
        if STOP >= 3:
            mix_phase(l)
        if STOP >= 4:
            ffn_phase(l, 1)
            S.barrier()

    for tbi, (t0, tn) in enumerate(TB):
        rms_rstd(lambda kc: xT[:, kc, t0:t0 + tn], 16, t0, tn, lambda kc: [('x', kc, tbi)],
                 rstd[:, t0:t0 + tn], ('rstd', tbi), D)
        for kc in range(16):
            stt(xT[:, kc, t0:t0 + tn], xT[:, kc, t0:t0 + tn], CF('fnorm', kc, 1), rstd[:, t0:t0 + tn], ALU.mult, ALU.mult,
                [('x', kc, tbi), ('rstd', tbi), 'cf'], [('x', kc, tbi)])
    for q in range(4):
        out_toks.append(dma('sp', y_d[:, q * 4 * T:(q + 1) * 4 * T], xflat[:, q * 4 * T:(q + 1) * 4 * T], xkeys(q), [], 'st_y%d' % q))
    S.final_waits('sp', out_toks)
    S.emit()
    st.close()
    return nc


def _tiles_ws(W, KC):
    K, M = W.shape
    nm = M // 128
    return np.ascontiguousarray(W.reshape(KC, 128, nm, 128).transpose(2, 1, 0, 3)).reshape(nm, 128, KC * 128)


def _tile_cols(W, c0, n, pad_to):
    blk = W[:, c0:c0 + n]
    if n < pad_to:
        blk = np.concatenate([blk, np.zeros((W.shape[0], pad_to - n), W.dtype)], axis=1)
    KC = W.shape[0] // 128
    return np.ascontiguousarray(blk.reshape(KC, 128, pad_to).transpose(1, 0, 2)).reshape(128, KC * pad_to)


def _fm(v):
    v = np.asarray(v)
    C = v.shape[-1] // 128
    lead = v.shape[:-1]
    return np.ascontiguousarray(np.moveaxis(v.reshape(lead + (C, 128)), -1, 0))


def _core_slots(c):
    if c < 2:
        return [('s', c, i) for i in range(4)] + [('p', 30 + c, 0)]
    return [('p', 5 * (c - 2) + i, 0) for i in range(5)]


def _rope_tables(slots):
    half = 32
    freqs = (10000.0 ** (-np.arange(half, dtype=np.float32) / half)).astype(np.float32)
    cosT = np.ones((128, T), np.float32)
    sinT = np.zeros((128, T), np.float32)
    for si, (kind, idx, part) in enumerate(slots):
        if kind != 's':
            continue
        tpos = part * 256 + np.arange(256)
        row = (tpos // 64).astype(np.float32)
        col = (tpos % 64).astype(np.float32)
        for d in range(128):
            pos = row if d < 64 else col
            f = freqs[d % 32]
            ang = (pos * f).astype(np.float32)
            cosT[d, si * 256:(si + 1) * 256] = np.cos(ang)
            sgn = -1.0 if (d % 64) < 32 else 1.0
            sinT[d, si * 256:(si + 1) * 256] = sgn * np.sin(ang)
    return np.concatenate([cosT, sinT], axis=1)


_PROG = {}


NLAYERS = 2
CORES = list(range(NCORES))


def kernel(**inp):
    L = NLAYERS
    f32 = np.float32
    g = {k: np.asarray(v) for k, v in inp.items()}
    cfo, NCF = cf_layout(L)
    cbo, NCB = cb_layout(L)
    ar = np.arange(128)
    ident = np.eye(128, dtype=f32)
    triU = (ar[:, None] <= ar[None, :]).astype(f32)
    triL = (ar[:, None] >= ar[None, :]).astype(f32)

    shared = {}
    shared['wmod'] = np.stack([_tiles_ws(g['w_mod'][l], 16) for l in range(L)])
    wgu = np.empty((L, 2, 88, 128, 2048), f32)
    wd = np.empty((L, 2, 22, 128, 4096), f32)
    for l in range(L):
        for w, pre in enumerate(('ffn1', 'ffn2')):
            tg = _tiles_ws(g[pre + '_w_gate'][l], 16)
            tu = _tiles_ws(g[pre + '_w_up'][l], 16)
            wgu[l, w] = np.stack([tg, tu], axis=1).reshape(88, 128, 2048)
            Wd = g[pre + '_w_down'][l]
            wd[l, w] = np.ascontiguousarray(
                Wd.reshape(11, 4, 128, 2, 8, 128).transpose(0, 3, 2, 4, 1, 5)).reshape(22, 128, 4096)
    shared['wgu'] = wgu
    shared['wd'] = wd
    wws = np.empty((L, len(WS), 128, 2048), f32)
    wv = np.empty((L, 5, 128, 4096), f32)
    wdt = np.empty((L, 128, 512), f32)
    wbr = np.empty((L, 48, 128, 1024), f32)
    wout = np.empty((L, 16, 128, 2048), f32)
    for l in range(L):
        Win = g['w_in'][l]
        for i, (nm, c0, n) in enumerate(WS):
            wws[l, i] = _tile_cols(Win, c0, n, 128)
        for h in range(4):
            wv[l, h] = _tile_cols(Win, 1024 + h * 256, 256, 256)
        wv[l, 4] = _tile_cols(Win, 7488, 256, 256)
        wdt[l] = _tile_cols(Win, 6176, 32, 32)
        for b_, nm in enumerate(('w_br_gla', 'w_br_ssm', 'w_br_attn')):
            wbr[l, b_ * 16:(b_ + 1) * 16] = _tiles_ws(g[nm][l], 8)
        wout[l] = _tiles_ws(g['w_out'][l], 16)
    shared.update(wws=wws, wv=wv, wdt=wdt, wbr=wbr, wout=wout)

    in_maps = []
    for c in CORES:
        slots = _core_slots(c)
        is_s = c < 2
        toks = []
        for (kind, idx, part) in slots:
            if kind == 's':
                toks.append(g['x_sample'][idx, part * 256:(part + 1) * 256])
            else:
                toks.append(g['x_prompt'][idx])
        x = np.concatenate(toks, axis=0)
        xin = np.ascontiguousarray(x.T.reshape(16, 128, T).transpose(1, 0, 2)).reshape(128, 16 * T)
        cf = np.zeros((128, NCF), f32)

        def put(name, arr):
            o, n = cfo[name]
            cf[:, o:o + n] = np.asarray(arr, f32).reshape(128, n)
        put('ident', ident)
        put('triU', triU)
        put('triL', triL)
        put('ntriU', -triU)
        put('ntriL', -triL)
        put('mnegF', np.where(ar[:, None] <= ar[None, :], 0.0, NEG))
        put('mnegB', np.where(ar[:, None] >= ar[None, :], 0.0, NEG))
        put('cummask', np.broadcast_to((np.arange(T) % 128 != 0).astype(f32)[None, :], (128, T)))
        put('fs', np.full((128, 1), 1.0 if is_s else 0.0))
        put('ctxbias', np.full((128, 1), 0.0 if is_s else NEG))
        cA = g['c'][c] if is_s else g['c_ctx']
        put('cvec', np.stack([_fm(cA), _fm(g['c_ctx'])], axis=2))
        nw = np.stack([np.stack([_fm(g[k][l]) for k in ('ffn1_norm', 'mix_norm', 'ffn2_norm')], axis=1) for l in range(L)], axis=1)
        put('normw', nw)
        put('fnorm', _fm(g['final_norm']))
        put('bmod', np.stack([_fm(g['b_mod'][l]) for l in range(L)], axis=1))
        put('nbup', np.stack([_fm(g['gla_b_up'][l]) for l in range(L)], axis=1))
        put('gnorm', np.stack([_fm(g['gla_norm'][l]) for l in range(L)], axis=1))
        put('convw', np.stack([_fm(g['ssm_conv_w'][l]).transpose(0, 2, 1) for l in range(L)], axis=1))
        put('convb', np.stack([_fm(g['ssm_conv_b'][l]) for l in range(L)], axis=1))
        put('ssmD', np.stack([_fm(np.repeat(g['ssm_d'][l], 64)) for l in range(L)], axis=1))
        put('ssmnorm', np.stack([_fm(g['ssm_norm'][l]) for l in range(L)], axis=1))
        put('dtbias', np.broadcast_to(g['ssm_dt_bias'][:L].reshape(1, L * 32), (128, L * 32)))
        put('alog', np.broadcast_to(g['ssm_a_log'][:L].reshape(1, L * 32), (128, L * 32)))
        put('sink', np.broadcast_to(g['attn_sink'][:L].reshape(1, L * 8), (128, L * 8)))

        cbm = np.zeros((128, NCB), f32)

        def putb(name, arr):
            o, n = cbo[name]
            cbm[:, o:o + n] = np.asarray(arr, f32).reshape(128, n)
        putb('ones', np.ones((128, 128)))
        putb('identb', ident)
        perm = np.array([d + 32 if (d % 64) < 32 else d - 32 for d in range(128)])
        pm = np.zeros((128, 128), f32)
        pm[perm, np.arange(128)] = 1.0
        putb('pm', pm)
        putb('maskFB', np.concatenate([triU, triL], axis=1))
        am = np.zeros((128, 18, 128), f32)
        ones = np.ones((128, 128), f32)
        for j in range(NT):
            kind = slots[j // 2][0]
            if j >= 1:
                if kind == 's' and slots[(j - 1) // 2][0] == 's':
                    am[:, j - 1, :] = triL
                elif kind == 'p' and (j % 2 == 1):
                    am[:, j - 1, :] = ones
            if j <= NT - 2:
                if kind == 's' and slots[(j + 1) // 2][0] == 's':
                    am[:, 9 + j, :] = triU
                elif kind == 'p' and (j % 2 == 0):
                    am[:, 9 + j, :] = ones
        putb('amask', am)
        wup = np.zeros((128, L, 2, 512), f32)
        for l in range(L):
            for d_ in range(2):
                wup[d_ * 16:(d_ + 1) * 16, l, d_, :] = g['gla_w_up'][l, d_]
        putb('wup', wup)

        h0g = np.zeros((L, 2, 4, 128, 256), f32)
        h0s = np.zeros((L, 2, 128, 1024), f32)
        ctxk = np.zeros((L, 2, 128, 512), f32)
        ctxv = np.zeros((L, 2, 128, 512), f32)
        if is_s:
            for l in range(L):
                h0g[l] = g['state_gla'][c, l]
                h0s[l] = g['state_ssm'][c, l].transpose(0, 3, 1, 2).reshape(2, 128, 1024)
                ctxk[l] = g['cache_k'][c, l].transpose(1, 2, 0)
                ctxv[l] = g['cache_v'][c, l].reshape(4, 128, 2, 128).transpose(2, 1, 0, 3).reshape(2, 128, 512)
        m = dict(xin=xin, cf=cf, cb=cbm, rope=_rope_tables(slots), h0g=h0g, h0s=h0s, ctxk=ctxk, ctxv=ctxv)
        m.update(shared)
        in_maps.append(m)

    if L not in _PROG:
        _PROG[L] = build_program(L)
    import os
    if os.environ.get('KTRACE') == '1':
        res = run_bass_kernel_spmd(_PROG[L], in_maps, core_ids=list(range(len(CORES))), trace=True)
        print('EXEC_TIME_NS', res.exec_time_ns)
    else:
        res = run_bass_kernel_spmd(_PROG[L], in_maps, core_ids=list(range(len(CORES))))
    R = res.results

    y_prompt = np.zeros((32, 256, D), f32)
    y_sample = np.zeros((2, 1024, D), f32)
    nk = np.zeros((32, L, 256, 2, 128), f32)
    nv = np.zeros((32, L, 256, 2, 128), f32)
    ng = np.zeros((32, L, 2, 4, 128, 256), f32)
    ns = np.zeros((32, L, 2, 16, 64, 128), f32)
    for ci, c in enumerate(CORES):
        r = R[ci]
        y = r['yT'].reshape(128, 16, T).transpose(2, 1, 0).reshape(T, D)
        ok = r['ok'].reshape(L, 2, 128, T)
        ov = r['ov'].reshape(L, T, 2, 128)
        og = r['og'].reshape(L, 5, 2, 4, 128, 256)
        os_ = r['os'].reshape(L, 5, 2, 128, 16, 64)
        for si, (kind, idx, part) in enumerate(_core_slots(c)):
            sl = slice(si * 256, (si + 1) * 256)
            if kind == 's':
                y_sample[idx, part * 256:(part + 1) * 256] = y[sl]
            else:
                y_prompt[idx] = y[sl]
                nk[idx] = ok[:, :, :, sl].transpose(0, 3, 1, 2)
                nv[idx] = ov[:, sl]
                ng[idx] = og[:, si]
                ns[idx] = os_[:, si].transpose(0, 1, 3, 4, 2)
    return (y_prompt, y_sample, nk, nv, ng, ns)
```
